# Optimizing a Trainium2 kernel written in Bass

```python
import math
import jax, jax.numpy as jnp
from jax import lax
import numpy as np

D_MODEL = 1024
BATCH = 8
SEQ = 8192
DEPTH = 2

N_BRANCH = 4
D_BRANCH = D_MODEL // 2
LRU_BLOCKS = 8
LRU_BLOCK_DIM = D_BRANCH // LRU_BLOCKS
LRU_CONV = 4
LRU_C = 8.0
CONF_CONV = 31
DIFF_HEADS = 4
DIFF_DQK = 64
DIFF_DV = 2 * DIFF_DQK
SPA_HEADS = 4
SPA_DH = D_BRANCH // SPA_HEADS
IDX_HEADS = 8
IDX_DH = 32
TOPK_MAX = 256
ROPE_THETA = 500000.0
ROT_FRACTION_DEN = 4
D_FF = 2816
FFN_CONV = 3
PLE_DIM = 256
Q_BLOCK = 128
RMS_EPS = 1e-6

IN_SIZES = (
    D_BRANCH,
    D_BRANCH,
    2 * D_BRANCH,
    DIFF_HEADS * 2 * DIFF_DQK,
    DIFF_HEADS * 2 * DIFF_DQK,
    DIFF_HEADS * DIFF_DV,
    SPA_HEADS * SPA_DH,
    SPA_DH,
    SPA_DH,
    IDX_HEADS * IDX_DH,
    IDX_DH,
    IDX_HEADS,
)
D_IN = sum(IN_SIZES)
IN_SPLITS = tuple(sum(IN_SIZES[:n]) for n in range(1, len(IN_SIZES)))

kernel_name = "hybrid_gated_branch_decoder"


def rms_norm(x, gain):
    xf = x.astype(jnp.float32)
    y = xf * lax.rsqrt(jnp.mean(xf * xf, axis=-1, keepdims=True) + RMS_EPS)
    return (y * gain.astype(jnp.float32)).astype(x.dtype)


def causal_dwconv(x, w, b):
    k, c = w.shape
    y = lax.conv_general_dilated(
        x, w[:, None, :].astype(x.dtype), window_strides=(1,), padding=((k - 1, 0),),
        dimension_numbers=("NWC", "WIO", "NWC"), feature_group_count=c)
    return y + b


def rope_tables(seq, head_dim, dtype):
    rot = head_dim // ROT_FRACTION_DEN
    inv = ROPE_THETA ** (-jnp.arange(0, rot, 2, dtype=jnp.float32) / rot)
    ang = jnp.arange(seq, dtype=jnp.float32)[:, None] * inv[None, :]
    return jnp.cos(ang).astype(dtype), jnp.sin(ang).astype(dtype)


def apply_rope(x, cos, sin):
    half = cos.shape[-1]
    x1, x2, rest = x[..., :half], x[..., half:2 * half], x[..., 2 * half:]
    c, s = cos[:, None, :], sin[:, None, :]
    return jnp.concatenate([x1 * c - x2 * s, x1 * s + x2 * c, rest], axis=-1)


def to_blocks(t):
    b, s = t.shape[:2]
    t = t.reshape((b, s // Q_BLOCK, Q_BLOCK) + t.shape[2:])
    return jnp.moveaxis(t, 1, 0)


def from_blocks(t):
    t = jnp.moveaxis(t, 0, 1)
    b, nb, q = t.shape[:3]
    return t.reshape(b, nb * q, -1)


def _lin_combine(c1, c2):
    a1, b1 = c1
    a2, b2 = c2
    return a1 * a2, a2 * b1 + b2


def rg_lru_branch(xa, ga, conv_w, conv_b, w_r, b_r, w_i, b_i, lam):
    bsz, s, _ = xa.shape
    xc = causal_dwconv(xa, conv_w, conv_b)
    xb = xc.reshape(bsz, s, LRU_BLOCKS, LRU_BLOCK_DIM)
    r = jax.nn.sigmoid(jnp.einsum("bsgi,gij->bsgj", xb, w_r).reshape(bsz, s, D_BRANCH) + b_r)
    ig = jax.nn.sigmoid(jnp.einsum("bsgi,gij->bsgj", xb, w_i).reshape(bsz, s, D_BRANCH) + b_i)
    log_a = (-LRU_C * r.astype(jnp.float32)) * jax.nn.softplus(-lam.astype(jnp.float32))
    a = jnp.exp(log_a)
    drive = jnp.sqrt(-jnp.expm1(2.0 * log_a)) * (ig * xc).astype(jnp.float32)
    _, hseq = lax.associative_scan(_lin_combine, (a, drive), axis=1)
    return hseq.astype(xa.dtype) * jax.nn.gelu(ga)


def conformer_branch(u, conv_w, conv_b, norm_g):
    val, gate = jnp.split(u, 2, axis=-1)
    y = val * jax.nn.sigmoid(gate)
    y = causal_dwconv(y, conv_w, conv_b)
    y = rms_norm(y, norm_g)
    return jax.nn.silu(y)


def diff_attention_branch(q, k, v, q_norm, k_norm, lq1, lk1, lq2, lk2, subln, lam_init, cos, sin):
    bsz, s, _ = q.shape
    q = apply_rope(rms_norm(q.reshape(bsz, s, 2 * DIFF_HEADS, DIFF_DQK), q_norm), cos, sin)
    k = apply_rope(rms_norm(k.reshape(bsz, s, 2 * DIFF_HEADS, DIFF_DQK), k_norm), cos, sin)
    q = q.reshape(bsz, s, DIFF_HEADS, 2, DIFF_DQK)
    k = k.reshape(bsz, s, DIFF_HEADS, 2, DIFF_DQK)
    v = v.reshape(bsz, s, DIFF_HEADS, DIFF_DV)
    f32 = jnp.float32
    lam = (jnp.exp(jnp.sum(lq1.astype(f32) * lk1.astype(f32)))
           - jnp.exp(jnp.sum(lq2.astype(f32) * lk2.astype(f32))) + lam_init)
    scale = DIFF_DQK ** -0.5
    kpos = jnp.arange(s, dtype=jnp.int32)

    def block(args):
        q_blk, start = args
        sc = jnp.einsum("bqhcd,bkhcd->bhcqk", q_blk, k).astype(f32) * scale
        qpos = start + jnp.arange(Q_BLOCK, dtype=jnp.int32)
        mask = kpos[None, :] <= qpos[:, None]
        pr = jax.nn.softmax(jnp.where(mask, sc, -jnp.inf), axis=-1)
        attn = (pr[:, :, 0] - lam * pr[:, :, 1]).astype(v.dtype)
        return jnp.einsum("bhqk,bkhd->bqhd", attn, v)

    starts = jnp.arange(s // Q_BLOCK, dtype=jnp.int32) * Q_BLOCK
    o = from_blocks(lax.map(block, (to_blocks(q), starts))).reshape(bsz, s, DIFF_HEADS, DIFF_DV)
    o = rms_norm(o, subln) * (1.0 - lam_init)
    return o.reshape(bsz, s, DIFF_HEADS * DIFF_DV)


def sparse_attention_branch(q, k, v, qi, ki, wi, q_norm, k_norm, ki_norm, cos_s, sin_s, cos_i, sin_i):
    bsz, s, _ = q.shape
    k_sel = min(TOPK_MAX, s // 4)
    q = apply_rope(rms_norm(q.reshape(bsz, s, SPA_HEADS, SPA_DH), q_norm), cos_s, sin_s)
    k = apply_rope(rms_norm(k.reshape(bsz, s, 1, SPA_DH), k_norm), cos_s, sin_s)[:, :, 0]
    qi = apply_rope(qi.reshape(bsz, s, IDX_HEADS, IDX_DH), cos_i, sin_i)
    ki = apply_rope(rms_norm(ki.reshape(bsz, s, 1, IDX_DH), ki_norm), cos_i, sin_i)[:, :, 0]
    wi = wi * (IDX_HEADS ** -0.5 * IDX_DH ** -0.5)
    scale = SPA_DH ** -0.5
    kpos = jnp.arange(s, dtype=jnp.int32)
    gather = jax.vmap(lambda tb, ib: tb[ib])

    def block(args):
        q_blk, qi_blk, w_blk, start = args
        qpos = start + jnp.arange(Q_BLOCK, dtype=jnp.int32)
        mask = kpos[None, :] <= qpos[:, None]
        logits = jax.nn.relu(jnp.einsum("bqhd,bkd->bqhk", qi_blk, ki))
        score = jnp.einsum("bqhk,bqh->bqk", logits, w_blk).astype(jnp.float32)
        score = jnp.where(mask[None], score, -jnp.inf)
        _, idx = lax.top_k(score, k_sel)
        valid = idx <= qpos[None, :, None]
        k_g = gather(k, idx)
        v_g = gather(v, idx)
        sc = jnp.einsum("bqhd,bqkd->bhqk", q_blk, k_g).astype(jnp.float32) * scale
        pr = jax.nn.softmax(jnp.where(valid[:, None], sc, -jnp.inf), axis=-1).astype(v.dtype)
        return jnp.einsum("bhqk,bqkd->bqhd", pr, v_g)

    starts = jnp.arange(s // Q_BLOCK, dtype=jnp.int32) * Q_BLOCK
    o = lax.map(block, (to_blocks(q), to_blocks(qi), to_blocks(wi), starts))
    return from_blocks(o)


def setup_inputs(seed: int = 0) -> dict:
    key = jax.random.key(seed)
    ks = iter(jax.random.split(key, 48))

    def nrm(shape, scale):
        return scale * jax.random.normal(next(ks), shape, jnp.float32)

    def gain(shape):
        return 1.0 + nrm(shape, 0.02)

    L, D, DB = DEPTH, D_MODEL, D_BRANCH
    u = jax.random.uniform(next(ks), (L, DB), jnp.float32, 0.9, 0.999)
    sig = u ** (1.0 / LRU_C)
    lru_lambda = jnp.log(sig) - jnp.log1p(-sig)
    return {
        "x": nrm((BATCH, SEQ, D), 1.0),
        "p": nrm((DEPTH, BATCH, SEQ, PLE_DIM), 1.0),
        "norm_mix": gain((L, D)),
        "w_in": nrm((L, D, D_IN), D ** -0.5),
        "w_gate": nrm((L, N_BRANCH, D, D), D ** -0.5),
        "conv_a_w": nrm((L, LRU_CONV, DB), LRU_CONV ** -0.5),
        "conv_a_b": nrm((L, DB), 0.01),
        "lru_w_r": nrm((L, LRU_BLOCKS, LRU_BLOCK_DIM, LRU_BLOCK_DIM), LRU_BLOCK_DIM ** -0.5),
        "lru_b_r": nrm((L, DB), 0.01),
        "lru_w_i": nrm((L, LRU_BLOCKS, LRU_BLOCK_DIM, LRU_BLOCK_DIM), LRU_BLOCK_DIM ** -0.5),
        "lru_b_i": nrm((L, DB), 0.01),
        "lru_lambda": lru_lambda,
        "conv_b_w": nrm((L, CONF_CONV, DB), CONF_CONV ** -0.5),
        "conv_b_b": nrm((L, DB), 0.01),
        "conv_b_norm": gain((L, DB)),
        "diff_q_norm": gain((L, DIFF_DQK)),
        "diff_k_norm": gain((L, DIFF_DQK)),
        "diff_lq1": nrm((L, DIFF_DQK), 0.1),
        "diff_lk1": nrm((L, DIFF_DQK), 0.1),
        "diff_lq2": nrm((L, DIFF_DQK), 0.1),
        "diff_lk2": nrm((L, DIFF_DQK), 0.1),
        "diff_subln": gain((L, DIFF_DV)),
        "spa_q_norm": gain((L, SPA_DH)),
        "spa_k_norm": gain((L, SPA_DH)),
        "idx_k_norm": gain((L, IDX_DH)),
        "w_branch": nrm((L, N_BRANCH, DB, D), DB ** -0.5),
        "w_out": nrm((L, D, D), 0.5 * D ** -0.5),
        "norm_ffn": gain((L, D)),
        "w_up": nrm((L, D, 2 * D_FF), D ** -0.5),
        "conv_f_w": nrm((L, FFN_CONV, 2 * D_FF), FFN_CONV ** -0.5),
        "conv_f_b": nrm((L, 2 * D_FF), 0.01),
        "w_down": nrm((L, D_FF, D), D_FF ** -0.5),
        "ple_gate_norm": gain((L, D)),
        "w_ple_gate": nrm((L, D, D), D ** -0.5),
        "w_ple_in": nrm((L, PLE_DIM, D), PLE_DIM ** -0.5),
        "ple_norm": gain((L, D)),
    }


def reference(x, p, norm_mix, w_in, w_gate, conv_a_w, conv_a_b, lru_w_r, lru_b_r, lru_w_i, lru_b_i,
              lru_lambda, conv_b_w, conv_b_b, conv_b_norm, diff_q_norm, diff_k_norm, diff_lq1, diff_lk1,
              diff_lq2, diff_lk2, diff_subln, spa_q_norm, spa_k_norm, idx_k_norm, w_branch, w_out,
              norm_ffn, w_up, conv_f_w, conv_f_b, w_down, ple_gate_norm, w_ple_gate, w_ple_in, ple_norm):
    _, s, _ = x.shape
    dt = x.dtype
    cos_c, sin_c = rope_tables(s, DIFF_DQK, dt)
    cos_s, sin_s = rope_tables(s, SPA_DH, dt)
    cos_i, sin_i = rope_tables(s, IDX_DH, dt)
    h = x
    for i in range(DEPTH):
        lam_init = 0.8 - 0.6 * math.exp(-0.3 * i)
        xn = rms_norm(h, norm_mix[i])
        u = xn @ w_in[i]
        (a_x, a_g, b_u, c_q, c_k, c_v, d_q, d_k, d_v, i_q, i_k, i_w) = jnp.split(u, IN_SPLITS, axis=-1)
        o_a = rg_lru_branch(a_x, a_g, conv_a_w[i], conv_a_b[i], lru_w_r[i], lru_b_r[i],
                            lru_w_i[i], lru_b_i[i], lru_lambda[i])
        o_b = conformer_branch(b_u, conv_b_w[i], conv_b_b[i], conv_b_norm[i])
        o_c = diff_attention_branch(c_q, c_k, c_v, diff_q_norm[i], diff_k_norm[i], diff_lq1[i],
                                    diff_lk1[i], diff_lq2[i], diff_lk2[i], diff_subln[i], lam_init,
                                    cos_c, sin_c)
        o_d = sparse_attention_branch(d_q, d_k, d_v, i_q, i_k, i_w, spa_q_norm[i], spa_k_norm[i],
                                      idx_k_norm[i], cos_s, sin_s, cos_i, sin_i)
        branches = (o_a, o_b, o_c, o_d)
        mixed = jax.nn.sigmoid(xn @ w_gate[i, 0]) * (branches[0] @ w_branch[i, 0])
        for j in range(1, N_BRANCH):
            mixed = mixed + jax.nn.sigmoid(xn @ w_gate[i, j]) * (branches[j] @ w_branch[i, j])
        h = h + mixed @ w_out[i]
        f = causal_dwconv(rms_norm(h, norm_ffn[i]) @ w_up[i], conv_f_w[i], conv_f_b[i])
        f_gate, f_val = jnp.split(f, 2, axis=-1)
        h = h + (jax.nn.silu(f_gate) * f_val) @ w_down[i]
        e = rms_norm(p[i] @ w_ple_in[i], ple_norm[i])
        g = jax.nn.sigmoid(rms_norm(h, ple_gate_norm[i]) @ w_ple_gate[i])
        h = h + g * e
    return h
```

```python
import math
from contextlib import ExitStack

import numpy as np
import concourse.bass as bass
import concourse.mybir as mybir
from concourse.bass_utils import run_bass_kernel_spmd

F32 = mybir.dt.float32
BF16 = mybir.dt.bfloat16
I32 = mybir.dt.int32
AF = mybir.ActivationFunctionType
ALU = mybir.AluOpType
AX = mybir.AxisListType

ENGS = ("pe", "act", "dve", "pool", "sp")

D = 1024
DB = 512
DIN = 4648
DFF = 2816
PLE = 256
EPS = 1e-6
TOPK = 256
NBIS = 12


class Tok:
    __slots__ = ("name", "W", "R", "prev")

    def __init__(self, name=""):
        self.name = name
        self.W = set()
        self.R = set()
        self.prev = set()


class Ins:
    __slots__ = ("eng", "fn", "deps", "dsem", "needs_inc", "val", "waits")

    def __init__(self, eng, fn, deps, dsem):
        self.eng = eng
        self.fn = fn
        self.deps = deps
        self.dsem = dsem
        self.needs_inc = False
        self.val = None
        self.waits = None


class Prog:
    DMA_POOL = 8

    def __init__(self, nc):
        self.nc = nc
        self.ins = []
        self.dpool = {}
        self.last = {}
        self.bar = {}

    def tok(self, name=""):
        return Tok(name)

    def toks(self, n, name=""):
        return [Tok(f"{name}{i}") for i in range(n)]

    def _deps(self, eng, reads, writes, swrites=()):
        deps = set()
        idx = len(self.ins)
        for t in reads:
            deps |= t.W
            t.R.add(idx)
        for t in writes:
            deps |= t.W
            deps |= t.R
            deps |= t.prev
            t.W = {idx}
            t.R = set()
            t.prev = {idx}
        for t in swrites:
            if t.R:
                t.prev = t.R | t.W
                t.W = set()
                t.R = set()
            deps |= t.prev
            t.W.add(idx)
        deps.discard(idx)
        if eng in self.bar:
            deps |= self.bar.pop(eng)
        self.last[eng] = idx
        return deps

    def barrier(self):
        b = set(self.last.values())
        for hist in self.dpool.values():
            b |= set(hist[-self.DMA_POOL:])
        for e in ENGS:
            self.bar[e] = set(b) | self.bar.get(e, set())

    def op(self, eng, fn, reads=(), writes=(), swrites=()):
        deps = self._deps(eng, reads, writes, swrites)
        self.ins.append(Ins(eng, fn, deps, None))

    def dma(self, eng, out, in_, reads=(), writes=(), swrites=(), pool="ld", slow=False):
        deps = self._deps(eng, reads, writes, swrites)
        hist = self.dpool.setdefault(pool, [])
        i = len(hist)
        if i >= self.DMA_POOL:
            deps.add(hist[i - self.DMA_POOL])
        hist.append(len(self.ins))
        if slow:
            fn = lambda e: e.dma_start(out=out, in_=in_, allow_slow_non_contiguous=True)
        else:
            fn = lambda e: e.dma_start(out=out, in_=in_)
        self.ins.append(Ins(eng, fn, deps, f"{pool}{i % self.DMA_POOL}"))

    def build(self, final_pools=("st",)):
        ins = self.ins
        n = len(ins)

        def skip(p, it):
            return p.eng == "pe" and it.eng == "pe" and p.dsem is None and it.dsem is None

        for it in ins:
            for d in it.deps:
                p = ins[d]
                if not skip(p, it):
                    p.needs_inc = True
        final_ids = []
        for pl in final_pools:
            final_ids += self.dpool.get(pl, [])[-self.DMA_POOL:]
        cnt = {}
        for it in ins:
            if it.dsem is not None:
                key = "D_" + it.dsem
                cnt[key] = cnt.get(key, 0) + 16
                it.val = (key, cnt[key])
            elif it.needs_inc:
                key = "E_" + it.eng
                cnt[key] = cnt.get(key, 0) + 1
                it.val = (key, cnt[key])
        known = {e: {} for e in ENGS}
        evclock = {}
        nwaits = 0
        for it in ins:
            kn = known[it.eng]
            need = {}
            for d in it.deps:
                p = ins[d]
                if p.val is None or skip(p, it):
                    continue
                s, v = p.val
                if kn.get(s, 0) >= v:
                    continue
                if need.get(s, 0) < v:
                    need[s] = v
            waits = []
            for s, v in sorted(need.items(), key=lambda kv: -kv[1]):
                if kn.get(s, 0) >= v:
                    continue
                waits.append((s, v))
                ck = evclock.get((s, v))
                if ck:
                    for ks, kv in ck.items():
                        if kn.get(ks, 0) < kv:
                            kn[ks] = kv
                if kn.get(s, 0) < v:
                    kn[s] = v
            it.waits = waits
            nwaits += len(waits)
            if it.val is not None:
                ck = dict(kn)
                ck[it.val[0]] = it.val[1]
                evclock[it.val] = ck
        self.stats = dict(n=n, nwaits=nwaits, sems=dict(cnt))
        nc = self.nc
        semnames = sorted({it.val[0] for it in ins if it.val is not None})
        with ExitStack() as es:
            sems = {s: es.enter_context(nc.semaphore(s)) for s in semnames}
            block = es.enter_context(nc.Block())
            per = {e: [it for it in ins if it.eng == e] for e in ENGS}
            finals = [ins[d].val for d in final_ids]

            def run(engobj, lst, is_last=False):
                for it in lst:
                    for s, v in it.waits[1:]:
                        engobj.wait_ge(sems[s], v)
                    r = it.fn(engobj)
                    if it.waits:
                        r._wait_ge(sems[it.waits[0][0]], it.waits[0][1])
                    if it.val is not None:
                        r.then_inc(sems[it.val[0]], 16 if it.dsem is not None else 1)
                if is_last:
                    fm = {}
                    for s, v in finals:
                        fm[s] = max(fm.get(s, 0), v)
                    for s, v in fm.items():
                        engobj.wait_ge(sems[s], v)

            @block.tensor
            def _(e):
                run(e, per["pe"])

            @block.scalar
            def _(e):
                run(e, per["act"])

            @block.vector
            def _(e):
                run(e, per["dve"])

            @block.gpsimd
            def _(e):
                run(e, per["pool"], is_last=True)

            @block.sync
            def _(e):
                run(e, per["sp"])


VEC_FIELDS = [("gmix", 8), ("gffn", 8), ("gpg", 8), ("gple", 8), ("caw", 16), ("cab", 4), ("lbr", 4),
              ("lbi", 4), ("lam", 4), ("cbw", 124), ("cbb", 4), ("cbn", 4), ("dqn", 1), ("dkn", 1),
              ("sqn", 1), ("skn", 1), ("ikn", 1), ("sub", 1), ("lq1", 1), ("lk1", 1), ("lq2", 1),
              ("lk2", 1), ("cfw", 132), ("cfb", 44)]
VC = {}
_o = 0
for _n, _k in VEC_FIELDS:
    VC[_n] = (_o, _k)
    _o += _k
NV = _o


def _cols(v, n):
    return np.ascontiguousarray(v.reshape(n, 128).T)


def pack_vec(inp, l):
    out = np.zeros((128, NV), np.float32)

    def put(name, arr):
        o, k = VC[name]
        out[:, o:o + k] = arr.reshape(128, k)

    put("gmix", _cols(inp["norm_mix"][l], 8))
    put("gffn", _cols(inp["norm_ffn"][l], 8))
    put("gpg", _cols(inp["ple_gate_norm"][l], 8))
    put("gple", _cols(inp["ple_norm"][l], 8))
    caw = inp["conv_a_w"][l]
    put("caw", np.stack([_cols(caw[k], 4) for k in range(4)], axis=1).reshape(128, 16))
    put("cab", _cols(inp["conv_a_b"][l], 4))
    put("lbr", _cols(inp["lru_b_r"][l], 4))
    put("lbi", _cols(inp["lru_b_i"][l], 4))
    put("lam", _cols(inp["lru_lambda"][l], 4))
    cbw = inp["conv_b_w"][l]
    put("cbw", np.stack([_cols(cbw[k], 4) for k in range(31)], axis=1).reshape(128, 124))
    put("cbb", _cols(inp["conv_b_b"][l], 4))
    put("cbn", _cols(inp["conv_b_norm"][l], 4))
    p = np.arange(128)
    put("dqn", inp["diff_q_norm"][l][p % 64])
    put("dkn", inp["diff_k_norm"][l][p % 64])
    put("sqn", inp["spa_q_norm"][l][p])
    put("skn", inp["spa_k_norm"][l][p])
    put("ikn", inp["idx_k_norm"][l][p % 32])
    put("sub", inp["diff_subln"][l][p])
    for nm, key in (("lq1", "diff_lq1"), ("lk1", "diff_lk1"), ("lq2", "diff_lq2"), ("lk2", "diff_lk2")):
        v = np.zeros(128, np.float32)
        v[:64] = inp[key][l]
        put(nm, v)
    cfw = inp["conv_f_w"][l]
    put("cfw", np.stack([_cols(cfw[k], 44) for k in range(3)], axis=1).reshape(128, 132))
    put("cfb", _cols(inp["conv_f_b"][l], 44))
    return out


def rope_inv(rot):
    return (np.float32(500000.0) ** (-np.arange(0, rot, 2, dtype=np.float32) / np.float32(rot))).astype(np.float32)


def make_consts(S):
    cm = np.zeros((9, 128, 128), np.float32)
    cm[0] = np.eye(128)
    cm[1] = 1.0 / 1024
    p = np.arange(128)
    cm[2] = (p[:, None] // 64 == p[None, :] // 64) / 64.0
    cm[3] = 1.0 / 128
    cm[4] = (p[:, None] // 32 == p[None, :] // 32) / 32.0
    cm[5] = 1.0 / 512
    rope = np.zeros((3, 2, 128, S), np.float32)
    t = np.arange(S, dtype=np.float32)
    for ci, hd in enumerate((64, 128, 32)):
        rot = hd // 4
        half = rot // 2
        inv = rope_inv(rot)
        ang = (t[:, None] * inv[None, :]).astype(np.float32)
        cos = np.cos(ang).astype(np.float32)
        sin = np.sin(ang).astype(np.float32)
        R = np.zeros((128, 128), np.float32)
        for q in range(128):
            d = q % hd
            if d < half:
                R[q, q + half] = -1.0
                rope[ci, 0, q] = cos[:, d]
                rope[ci, 1, q] = sin[:, d]
            elif d < 2 * half:
                R[q, q - half] = 1.0
                rope[ci, 0, q] = cos[:, d - half]
                rope[ci, 1, q] = sin[:, d - half]
            else:
                rope[ci, 0, q] = 1.0
        cm[6 + ci] = R.T
    return cm, rope


def lru_blockdiag(inp):
    L = inp["lru_w_r"].shape[0]
    out = np.zeros((L, 2, 4, 128, 128), np.float32)
    for l in range(L):
        for gi, key in enumerate(("lru_w_r", "lru_w_i")):
            w = inp[key][l]
            for ct in range(4):
                out[l, gi, ct, :64, :64] = w[2 * ct]
                out[l, gi, ct, 64:, 64:] = w[2 * ct + 1]
    return out


class B:
    def __init__(self, S, L, dbg=False, phases=None):
        self.S, self.L, self.dbg = S, L, dbg
        self.phases = phases
        nc = self.nc = bass.Bass("TRN2", target_bir_lowering=False)
        self.p = Prog(nc)
        self.NCH = S // 512
        dt = nc.dram_tensor

        def inp(name, shape, dtype=F32):
            return dt(name, list(shape), dtype, kind="ExternalInput").ap()

        self.xT = inp("xT", [D, S])
        self.pT = inp("pT", [L, PLE, S])
        self.vec = inp("vec", [L, 128, NV])
        self.cmat = inp("cmat", [9, 128, 128])
        self.rope = inp("rope", [3, 2, 128, S])
        self.lru = inp("lru", [L, 2, 4, 128, 128])
        self.w_in = inp("w_in", [L, D, DIN])
        self.w_gate = inp("w_gate", [L, 4, D, D])
        self.w_branch = inp("w_branch", [L, 4, DB, D])
        self.w_out = inp("w_out", [L, D, D])
        self.w_up = inp("w_up", [L, D, 2 * DFF])
        self.w_down = inp("w_down", [L, DFF, D])
        self.w_pg = inp("w_ple_gate", [L, D, D])
        self.w_pi = inp("w_ple_in", [L, PLE, D])
        self.outT = dt("outT", [D, S], F32, kind="ExternalOutput").ap()

        def scr(name, shape, dtype):
            kind = "ExternalOutput" if dbg else "Internal"
            return dt(name, list(shape), dtype, kind=kind).ap()

        self.H = scr("H", [D, S], F32)
        self.XN = scr("XN", [D, S], BF16)
        self.U16 = scr("U16", [2048, S], F32)
        self.CQ = scr("CQ", [512, S], BF16)
        self.CK = scr("CK", [512, S], BF16)
        self.CV = scr("CV", [S, 512], BF16)
        self.DQ = scr("DQ", [512, S], BF16)
        self.DK = scr("DK", [128, S], BF16)
        self.DV = scr("DV", [S, 128], BF16)
        self.IQ = scr("IQ", [256, S], BF16)
        self.IK = scr("IK", [32, S], BF16)
        self.IW = scr("IW", [S, 128], F32)
        self.OBR = scr("OBR", [4, 512, S], BF16)
        p = self.p
        n = self.NCH
        self.tH = p.toks(n, "H")
        self.tXN = p.toks(n, "XN")
        self.tU = p.toks(n, "U")
        self.tQK = p.toks(n, "QK")
        self.tOB = [p.toks(n, f"OB{j}_") for j in range(4)]
        self.es = ExitStack()
        self._uid = 0

    def sb(self, es, name, shape, dtype):
        self._uid += 1
        return es.enter_context(self.nc.sbuf_tensor(f"{name}_{self._uid}", list(shape), dtype))

    def ps(self, es, name, shape, dtype=F32):
        self._uid += 1
        return es.enter_context(self.nc.psum_tensor(f"{name}_{self._uid}", list(shape), dtype))

    def load_consts(self):
        p, nc = self.p, self.nc
        es = self.es
        self.cm = self.sb(es, "cm", [128, 9, 128], F32)
        self.t_cm = p.tok("cm")
        p.dma("sp", self.cm[:], self.cmat.rearrange("c p n -> p c n"), writes=[self.t_cm])
        self.ones_bf = self.sb(es, "ones_bf", [128, 128], BF16)
        self.t_ones = p.tok("ones")
        p.op("pool", lambda e: e.memset(self.ones_bf[:], 1.0), writes=[self.t_ones])
        self.ident_bf = self.sb(es, "ident_bf", [128, 128], BF16)
        self.t_identb = p.tok("identb")
        p.op("dve", lambda e: e.tensor_copy(out=self.ident_bf[:], in_=self.cm[:, 0, :]),
             reads=[self.t_cm], writes=[self.t_identb])
        self.vecs = self.sb(es, "vecs", [128, self.L, NV], F32)
        self.t_vec = p.tok("vec")
        p.dma("sp", self.vecs[:], self.vec.rearrange("l p n -> p l n"), writes=[self.t_vec])
        self.epsc = self.sb(es, "epsc", [128, 1], F32)
        self.t_eps = p.tok("eps")
        p.op("pool", lambda e: e.memset(self.epsc[:], EPS), writes=[self.t_eps])

    def freg(self, e, val):
        if not hasattr(self, "_fregs"):
            self._fregs = {}
        if val not in self._fregs:
            self._fregs[val] = e.to_reg(val)
        return self._fregs[val]

    def vcol(self, l, name, j=0, n=1):
        o, k = VC[name]
        return self.vecs[:, l, o + j:o + j + n]

    def load_weight(self, dst, dst_toks, src, K, N, stg, stg_toks, gain=None, col0=0, engs=("dve", "pool"),
                    rows=128):
        p = self.p
        for kc in range(K):
            i = self._wl % len(stg)
            e = engs[self._wl % len(engs)]
            self._wl += 1
            st, stt = stg[i], stg_toks[i]
            p.dma("sp", st[0:rows, 0:N], src[kc * rows:(kc + 1) * rows, :], writes=[stt], pool="w")
            o = dst[0:rows, kc, col0:col0 + N]
            if gain is not None:
                g = gain(kc)
                p.op(e, (lambda o=o, st=st, g=g: lambda en: en.tensor_scalar(
                    out=o, in0=st[0:rows, 0:N], scalar1=g, scalar2=1.0, op0=ALU.mult, op1=ALU.mult))(),
                    reads=[stt, self.t_vec], swrites=[dst_toks[kc]])
            else:
                p.op(e, (lambda o=o, st=st: lambda en: en.tensor_copy(out=o, in_=st[0:rows, 0:N]))(),
                     reads=[stt], swrites=[dst_toks[kc]])

    _wl = 0
    _cc = 0

    def phase1(self, l):
        p, nc, S = self.p, self.nc, self.S
        hin = self.xT if l == 0 else self.H
        cm = self.cm
        with ExitStack() as es:
            W = self.sb(es, "p1W", [128, 8, DIN], BF16)
            tW = p.toks(8, "p1W")
            Wdi = self.sb(es, "p1Wdi", [128, 8, 256], BF16)
            tWdi = p.tok("Wdi")
            with ExitStack() as es2:
                stg = [self.sb(es2, "p1stg", [128, DIN], F32) for _ in range(2)]
                tstg = p.toks(2, "stg")
                self.load_weight(W, tW, self.w_in[l], 8, DIN, stg, tstg,
                                 gain=lambda kc: self.vcol(l, "gmix", kc))
                p.op("pool", lambda e: e.memset(Wdi[:], 0.0), writes=[tWdi])
                for kc in range(8):
                    p.op("dve", lambda e, kc=kc: e.tensor_copy(out=Wdi[:, kc, 0:128], in_=W[:, kc, 4224:4352]),
                         reads=[tW[kc]], swrites=[tWdi])
                    p.op("dve", lambda e, kc=kc: e.tensor_copy(out=Wdi[:, kc, 128:136], in_=W[:, kc, 4640:4648]),
                         reads=[tW[kc]], swrites=[tWdi])
                p.barrier()
            hb = [self.sb(es, "hb", [128, 8, 512], F32) for _ in range(2)]
            thb = p.toks(2, "hb")
            rp = [self.sb(es, "rp", [128, 6, 512], F32) for _ in range(1)] * 2
            trp = [p.tok("rp")] * 2
            sq = [self.sb(es, "sq", [128, 512], F32) for _ in range(2)]
            tsq = p.toks(2, "sq")
            rstd = self.sb(es, "rstd", [128, 512], F32)
            trstd = p.tok("rstd")
            xn = [self.sb(es, "xn", [128, 8, 512], BF16) for _ in range(2)]
            txn = p.toks(2, "xn")
            raw = self.sb(es, "raw", [128, 8, 512], F32)
            traw = p.tok("raw")
            NE = 3
            ev = [self.sb(es, "ev", [128, 512], F32) for _ in range(NE)]
            tev = p.toks(NE, "ev")
            sq2 = [self.sb(es, "sq2", [128, 512], F32) for _ in range(NE)]
            tsq2 = p.toks(NE, "sq2")
            rs2 = [self.sb(es, "rs2", [128, 512], F32) for _ in range(NE)]
            trs2 = p.toks(NE, "rs2")
            xg = [self.sb(es, "xg", [128, 512], F32) for _ in range(NE)]
            txg = p.toks(NE, "xg")
            t1, tt1 = sq2, tsq2
            t2, tt2 = rs2, trs2
            ob = [self.sb(es, "ob", [128, 512], BF16) for _ in range(NE)]
            tob = p.toks(NE, "ob")
            tv = [self.sb(es, "tv", [128, 512], BF16)] * 2
            ttv = [p.tok("tv")] * 2
            tdv = [self.sb(es, "tdv", [128, 128], BF16) for _ in range(2)]
            ttdv = p.toks(2, "tdv")
            tiw = [self.sb(es, "tiw", [128, 128], F32) for _ in range(2)]
            ttiw = p.toks(2, "tiw")
            pst = self.ps(es, "pst", [128, 512])
            tpst = p.tok("pst")
            pm = [self.ps(es, "pm", [128, 512]) for _ in range(3)]
            tpm = p.toks(3, "pm")
            ps2 = self.ps(es, "ps2", [128, 512])
            tps2 = p.tok("ps2")
            ps3 = self.ps(es, "ps3", [128, 512])
            tps3 = p.tok("ps3")
            ptk = self.ps(es, "ptk", [128, 512])
            tptk = p.tok("ptk")
            ptd = self.ps(es, "ptd", [128, 256])
            tptd = p.tok("ptd")

            def loads(c):
                b = c % 2
                sl = slice(c * 512, (c + 1) * 512)
                rd = [self.tH[c]] if l > 0 else []
                p.dma("sp", hb[b][:], hin[:, sl].rearrange("(k p) t -> p k t", p=128), reads=rd, writes=[thb[b]])

            def load_rp(c):
                sl = slice(c * 512, (c + 1) * 512)
                p.dma("sp", rp[0][:], self.rope[:, :, :, sl].rearrange("c s p t -> p (c s) t"), writes=[trp[0]])

            qk_tiles = []
            for i in range(4):
                qk_tiles.append((16 + i, 128, 2, "dqn", 0, self.CQ, i * 128))
            for i in range(4):
                qk_tiles.append((20 + i, 128, 2, "dkn", 0, self.CK, i * 128))
            for i in range(4):
                qk_tiles.append((28 + i, 128, 3, "sqn", 1, self.DQ, i * 128))
            qk_tiles.append((32, 128, 3, "skn", 1, self.DK, 0))
            qk_tiles.append((34, 128, None, None, 2, self.IQ, 0))
            qk_tiles.append((35, 128, None, None, 2, self.IQ, 128))
            qk_tiles.append((36, 32, 4, "ikn", 2, self.IK, 0))

            loads(0)
            cnt = [0, 0]
            for c in range(self.NCH):
                b = c % 2
                sl = slice(c * 512, (c + 1) * 512)
                if c + 1 < self.NCH:
                    loads(c + 1)
                load_rp(c)
                hbb, xnb, rpb = hb[b], xn[b], rp[b]
                STOP = 9
                for kc in range(8):
                    si = kc % 2
                    p.op("act", lambda e, hbb=hbb, kc=kc, si=si: e.activation(out=sq[si][:], in_=hbb[:, kc, :],
                                                                              func=AF.Square),
                         reads=[thb[b]], writes=[tsq[si]])
                    p.op("pe", lambda e, kc=kc, si=si: e.matmul(pst[:], lhsT=cm[:, 1, :], rhs=sq[si][:],
                                                               start=(kc == 0), stop=(kc == 7)),
                         reads=[tsq[si], self.t_cm], writes=[tpst])
                p.op("act", lambda e: e.activation(out=rstd[:], in_=pst[:], func=AF.Sqrt, bias=self.epsc[:]),
                     reads=[tpst, self.t_eps], writes=[trstd])
                p.op("dve", lambda e: e.reciprocal(out=rstd[:], in_=rstd[:]), reads=[trstd], writes=[trstd])
                for kc in range(8):
                    eng = "dve" if kc % 2 == 0 else "pool"
                    p.op(eng, lambda e, kc=kc, hbb=hbb, xnb=xnb: e.tensor_tensor(
                        out=xnb[:, kc, :], in0=hbb[:, kc, :], in1=rstd[:], op=ALU.mult),
                        reads=[thb[b], trstd], swrites=[txn[b]])
                p.dma("sp", self.XN[:, sl].rearrange("(k p) t -> p k t", p=128), xnb[:],
                      reads=[txn[b]], writes=[self.tXN[c]], pool="st")

                def mm_fm(m, M, pidx):
                    c0 = m * 128
                    for kc in range(8):
                        p.op("pe", lambda e, kc=kc, c0=c0, M=M, pidx=pidx, xnb=xnb: e.matmul(
                            pm[pidx][0:M, :], lhsT=W[:, kc, c0:c0 + M], rhs=xnb[:, kc, :],
                            start=(kc == 0), stop=(kc == 7)),
                            reads=[tW[kc], txn[b]], writes=[tpm[pidx]])

                for m in range(16 if STOP > 2 else 0):
                    pidx = cnt[0] % 3
                    cnt[0] += 1
                    mm_fm(m, 128, pidx)
                    if m % 2 == 0:
                        p.op("act", lambda e, m=m, pidx=pidx: e.copy(out=raw[:, m % 8, :], in_=pm[pidx][:]),
                             reads=[tpm[pidx]], swrites=[traw])
                    else:
                        p.op("dve", lambda e, m=m, pidx=pidx: e.tensor_copy(out=raw[:, m % 8, :], in_=pm[pidx][:]),
                             reads=[tpm[pidx]], swrites=[traw])
                    if m % 8 == 7:
                        m0 = (m // 8) * 1024
                        p.dma("sp", self.U16[m0:m0 + 1024, sl].rearrange("(m p) t -> p m t", p=128), raw[:],
                              reads=[traw], swrites=[self.tU[c]], pool="st")
                nq = len(qk_tiles)
                pid = {}

                def stA(n):
                    (m, M, gi, gname, rc, dst, r0) = qk_tiles[n]
                    pidx = cnt[0] % 3
                    cnt[0] += 1
                    i = (cnt[1] + n) % NE
                    mm_fm(m, M, pidx)
                    p.op("act", lambda e, i=i, pidx=pidx, M=M: e.copy(out=ev[i][0:M, :], in_=pm[pidx][0:M, :]),
                         reads=[tpm[pidx]], writes=[tev[i]])
                    if gi is not None:
                        p.op("act", lambda e, i=i, pidx=pidx, M=M: e.activation(out=sq2[i][0:M, :],
                                                                               in_=pm[pidx][0:M, :], func=AF.Square),
                             reads=[tpm[pidx]], writes=[tsq2[i]])

                def stB(n):
                    (m, M, gi, gname, rc, dst, r0) = qk_tiles[n]
                    i = (cnt[1] + n) % NE
                    if gi is None:
                        return
                    p.op("pe", lambda e, i=i, M=M, gi=gi: e.matmul(ps2[0:M, :], lhsT=cm[0:M, gi, 0:M],
                                                                  rhs=sq2[i][0:M, :], start=True, stop=True),
                         reads=[tsq2[i], self.t_cm], writes=[tps2])
                    p.op("act", lambda e, i=i, M=M: e.activation(out=rs2[i][0:M, :], in_=ps2[0:M, :],
                                                                 func=AF.Sqrt, bias=self.epsc[0:M, :]),
                         reads=[tps2, self.t_eps], writes=[trs2[i]])
                    p.op("dve", lambda e, i=i, M=M: e.reciprocal(out=rs2[i][0:M, :], in_=rs2[i][0:M, :]),
                         reads=[trs2[i]], writes=[trs2[i]])
                    g = self.vcol(l, gname)[0:M, :]
                    p.op("dve", lambda e, i=i, M=M, g=g: e.scalar_tensor_tensor(
                        out=xg[i][0:M, :], in0=ev[i][0:M, :], scalar=g, in1=rs2[i][0:M, :],
                        op0=ALU.mult, op1=ALU.mult),
                        reads=[tev[i], trs2[i], self.t_vec], writes=[txg[i]])

                def stC(n):
                    (m, M, gi, gname, rc, dst, r0) = qk_tiles[n]
                    i = (cnt[1] + n) % NE
                    xs, txs = (xg[i], txg[i]) if gi is not None else (ev[i], tev[i])
                    p.op("pe", lambda e, xs=xs, M=M, rc=rc: e.matmul(ps3[0:M, :], lhsT=cm[0:M, 6 + rc, 0:M],
                                                                    rhs=xs[0:M, :], start=True, stop=True),
                         reads=[txs, self.t_cm], writes=[tps3])
                    p.op("pool", lambda e, i=i, xs=xs, M=M, rc=rc, rpb=rpb: e.tensor_tensor(
                        out=t1[i][0:M, :], in0=xs[0:M, :], in1=rpb[0:M, 2 * rc, :], op=ALU.mult),
                        reads=[txs, trp[b]], writes=[tt1[i]])
                    p.op("dve", lambda e, i=i, M=M, rc=rc, rpb=rpb: e.tensor_tensor(
                        out=t2[i][0:M, :], in0=ps3[0:M, :], in1=rpb[0:M, 2 * rc + 1, :], op=ALU.mult),
                        reads=[tps3, trp[b]], writes=[tt2[i]])
                    p.op("pool", lambda e, i=i, M=M: e.tensor_tensor(
                        out=ob[i][0:M, :], in0=t1[i][0:M, :], in1=t2[i][0:M, :], op=ALU.add),
                        reads=[tt1[i], tt2[i]], writes=[tob[i]])
                    p.dma("sp", dst[r0:r0 + M, sl], ob[i][0:M, :], reads=[tob[i]], swrites=[self.tQK[c]], pool="st")

                for step in range(nq + 2):
                    if step < nq:
                        stA(step)
                    if 0 <= step - 1 < nq:
                        stB(step - 1)
                    if 0 <= step - 2 < nq:
                        stC(step - 2)
                cnt[1] += nq
                for j in range(4 if STOP > 4 else 0):
                    jb = j % 2
                    ts = slice(j * 128, (j + 1) * 128)
                    r0 = c * 512 + j * 128
                    for kc in range(8):
                        p.op("pe", lambda e, kc=kc, ts=ts, xnb=xnb: e.matmul(
                            ptk[:], lhsT=xnb[:, kc, ts], rhs=W[:, kc, 3072:3584], start=(kc == 0), stop=(kc == 7)),
                            reads=[tW[kc], txn[b]], writes=[tptk])
                    for kc in range(8):
                        p.op("pe", lambda e, kc=kc, ts=ts, xnb=xnb: e.matmul(
                            ptd[:, 0:256], lhsT=xnb[:, kc, ts], rhs=Wdi[:, kc, :], start=(kc == 0),
                            stop=(kc == 7)), reads=[tWdi, txn[b]], writes=[tptd])
                    p.op("act", lambda e, jb=jb: e.copy(out=tv[jb][:], in_=ptk[:]), reads=[tptk], writes=[ttv[jb]])
                    p.op("dve", lambda e, jb=jb: e.tensor_copy(out=tdv[jb][:], in_=ptd[:, 0:128]),
                         reads=[tptd], writes=[ttdv[jb]])
                    p.op("dve", lambda e, jb=jb: e.tensor_scalar(out=tiw[jb][:], in0=ptd[:, 128:256], scalar1=1.0 / 16.0,
                                                                 scalar2=None, op0=ALU.mult),
                         reads=[tptd], writes=[ttiw[jb]])
                    p.dma("sp", self.CV[r0:r0 + 128, :], tv[jb][:], reads=[ttv[jb]], swrites=[self.tQK[c]], pool="st")
                    p.dma("sp", self.DV[r0:r0 + 128, :], tdv[jb][:], reads=[ttdv[jb]], swrites=[self.tQK[c]],
                          pool="st")
                    p.dma("sp", self.IW[r0:r0 + 128, :], tiw[jb][:], reads=[ttiw[jb]], swrites=[self.tQK[c]],
                          pool="st")
            p.barrier()

    def build(self):
        nc = self.nc
        with nc.allow_low_precision("bf16 matmul operands, fp32 accumulation"):
            with self.es:
                self.load_consts()
                ph = self.phases
                for l in range(self.L):
                    if ph is None or "p1" in ph:
                        self.phase1(l)
                    if ph is None or "a" in ph or "b" in ph:
                        self.phaseAB(l)
                    if ph is None or "c" in ph:
                        self.phaseC(l)
                    if ph is None or "d" in ph:
                        self.phaseD(l)
                    if ph is None or "p3" in ph:
                        self.phase3(l)
                    if ph is None or "p4" in ph:
                        self.phase4(l)
                self.p.build()
        return nc


def host_inputs(inp, S, L):
    cm, rope = make_consts(S)
    vec = np.stack([pack_vec(inp, l) for l in range(L)])
    lru = lru_blockdiag(inp)
    common = dict(vec=vec, cmat=cm, rope=rope, lru=lru[:L])
    for k_dev, k_in in (("w_in", "w_in"), ("w_gate", "w_gate"), ("w_branch", "w_branch"), ("w_out", "w_out"),
                        ("w_up", "w_up"), ("w_down", "w_down"), ("w_ple_gate", "w_ple_gate"),
                        ("w_ple_in", "w_ple_in")):
        common[k_dev] = np.ascontiguousarray(inp[k_in][:L], dtype=np.float32)
    maps = []
    nb = inp["x"].shape[0]
    for b in range(nb):
        m = dict(common)
        m["xT"] = np.ascontiguousarray(inp["x"][b].T)
        m["pT"] = np.ascontiguousarray(np.transpose(inp["p"][:L, b], (0, 2, 1)))
        maps.append(m)
    return maps


def kernel(**inputs):
    inp = {k: np.asarray(v) for k, v in inputs.items()}
    nb, S, _ = inp["x"].shape
    L = inp["w_in"].shape[0]
    bld = B(S, L)
    nc = bld.build()
    maps = host_inputs(inp, S, L)
    res = run_bass_kernel_spmd(nc, maps, core_ids=list(range(nb)))
    out = np.stack([np.ascontiguousarray(res.results[b]["outT"].T) for b in range(nb)])
    return out.astype(np.float32)


def genA(self, l, es):
    p, nc, S = self.p, self.nc, self.S
    TC = 1024
    NT = S // TC
    if True:
        lst = self.sb(es, "lrust", [128, 8, 128], F32)
        tlst = p.tok("lrust")
        lw = self.sb(es, "lruw", [128, 8, 128], BF16)
        tlw = p.tok("lruw")
        p.dma("sp", lst[:], self.lru[l].rearrange("g c p n -> p (g c) n"), writes=[tlst])
        p.op("dve", lambda e: e.tensor_copy(out=lw[:], in_=lst[:]), reads=[tlst], writes=[tlw])
        cc = self.sb(es, "lruc", [128, 12], F32)
        tcc = p.tok("lruc")
        onec = self.sb(es, "onec", [128, 1], F32)
        tone = p.tok("onec")
        p.op("pool", lambda e: e.memset(onec[:], 1.0 + 2.0 ** -23), writes=[tone])
        lam = self.vcol(l, "lam", 0, 4)
        p.op("act", lambda e: e.activation(out=cc[:, 0:4], in_=lam, func=AF.Exp, scale=-1.0),
             reads=[self.t_vec], writes=[tcc])
        p.op("act", lambda e: e.activation(out=cc[:, 0:4], in_=cc[:, 0:4], func=AF.Ln, bias=1.0),
             reads=[tcc], writes=[tcc])
        p.op("dve", lambda e: e.tensor_scalar(out=cc[:, 4:8], in0=cc[:, 0:4], scalar1=-8.0, scalar2=None,
                                              op0=ALU.mult), reads=[tcc], writes=[tcc])
        p.op("dve", lambda e: e.tensor_scalar(out=cc[:, 8:12], in0=cc[:, 0:4], scalar1=-16.0, scalar2=None,
                                              op0=ALU.mult), reads=[tcc], writes=[tcc])

        def f32buf(name, n=TC):
            return self.sb(es, name, [128, n], F32), p.tok(name)

        xa, txa = f32buf("xa", TC + 3)
        xc, txc = f32buf("xc")
        xcb = self.sb(es, "xcb", [128, TC], BF16)
        txcb = p.tok("xcb")
        r, tr = f32buf("r")
        ig, tig = f32buf("ig")
        a, ta = f32buf("a")
        dr, tdr = f32buf("dr")
        gx, tgx = f32buf("gx")
        h, th = f32buf("h")
        ag, tag = f32buf("ag")
        tg, ttg = f32buf("tg")
        sg, tsg = f32buf("sg")
        ob = self.sb(es, "oba", [128, TC], BF16)
        tob = p.tok("oba")
        hst = self.sb(es, "hst", [128, 1], F32)
        thst = p.tok("hst")
        pr = self.ps(es, "pr", [128, TC])
        tpr = p.tok("pr")
        pi = self.ps(es, "pi", [128, TC])
        tpi = p.tok("pi")
        nsub = TC // 512
        for ct in range(4):
            rows = slice(ct * 128, (ct + 1) * 128)
            p.op("pool", lambda e: e.memset(hst[:], 0.0), writes=[thst])
            for t in range(NT):
                t0 = t * TC
                chs = list(range(t0 // 512, (t0 + TC) // 512))
                rd = [self.tU[c] for c in chs]
                if t == 0:
                    p.op("pool", lambda e: e.memset(xa[:, 0:3], 0.0), writes=[txa])
                    p.dma("sp", xa[:, 3:TC + 3], self.U16[rows, 0:TC], reads=rd, swrites=[txa])
                else:
                    p.dma("sp", xa[:, :], self.U16[rows, t0 - 3:t0 + TC], reads=rd + [self.tU[chs[0] - 1]],
                          writes=[txa])
                p.dma("sp", ag[:], self.U16[512 + ct * 128:512 + (ct + 1) * 128, t0:t0 + TC], reads=rd, writes=[tag])
                w = lambda k: self.vcol(l, "caw", k * 4 + ct)
                p.op("dve", lambda e, w0=w(0), bb=self.vcol(l, "cab", ct): e.tensor_scalar(
                    out=xc[:], in0=xa[:, 0:TC], scalar1=w0, scalar2=bb, op0=ALU.mult, op1=ALU.add),
                    reads=[txa, self.t_vec], writes=[txc])
                for k in range(1, 4):
                    p.op("dve", lambda e, k=k, wk=w(k): e.scalar_tensor_tensor(
                        out=xc[:], in0=xa[:, k:k + TC], scalar=wk, in1=xc[:], op0=ALU.mult, op1=ALU.add),
                        reads=[txa, txc, self.t_vec], writes=[txc])
                p.op("pool", lambda e: e.tensor_copy(out=xcb[:], in_=xc[:]), reads=[txc], writes=[txcb])
                for s in range(nsub):
                    ss = slice(s * 512, (s + 1) * 512)
                    p.op("pe", lambda e, ss=ss, ct=ct: e.matmul(pr[:, ss], lhsT=lw[:, ct, :], rhs=xcb[:, ss],
                                                               start=True, stop=True),
                         reads=[tlw, txcb], swrites=[tpr])
                    p.op("pe", lambda e, ss=ss, ct=ct: e.matmul(pi[:, ss], lhsT=lw[:, 4 + ct, :], rhs=xcb[:, ss],
                                                               start=True, stop=True),
                         reads=[tlw, txcb], swrites=[tpi])
                p.op("act", lambda e, bb=self.vcol(l, "lbr", ct): e.activation(out=r[:], in_=pr[:], func=AF.Sigmoid,
                                                                              bias=bb),
                     reads=[tpr, self.t_vec], writes=[tr])
                p.op("act", lambda e, bb=self.vcol(l, "lbi", ct): e.activation(out=ig[:], in_=pi[:], func=AF.Sigmoid,
                                                                              bias=bb),
                     reads=[tpi, self.t_vec], writes=[tig])
                p.op("act", lambda e, ct=ct: e.activation(out=a[:], in_=r[:], func=AF.Exp, scale=cc[:, 4 + ct:5 + ct]),
                     reads=[tr, tcc], writes=[ta])
                p.op("act", lambda e, ct=ct: e.activation(out=dr[:], in_=r[:], func=AF.Exp,
                                                          scale=cc[:, 8 + ct:9 + ct]),
                     reads=[tr, tcc], writes=[tdr])
                p.op("act", lambda e: e.activation(out=dr[:], in_=dr[:], func=AF.Sqrt, scale=-1.0, bias=onec[:]),
                     reads=[tdr, tone], writes=[tdr])
                p.op("pool", lambda e: e.tensor_tensor(out=gx[:], in0=ig[:], in1=xc[:], op=ALU.mult),
                     reads=[tig, txc], writes=[tgx])
                p.op("pool", lambda e: e.tensor_tensor(out=gx[:], in0=gx[:], in1=dr[:], op=ALU.mult),
                     reads=[tgx, tdr], writes=[tgx])
                p.op("dve", lambda e: e.tensor_tensor_scan(out=h[:], data0=a[:], data1=gx[:], initial=hst[:],
                                                           op0=ALU.mult, op1=ALU.add),
                     reads=[ta, tgx, thst], writes=[th])
                p.op("act", lambda e: e.copy(out=hst[:], in_=h[:, TC - 1:TC]), reads=[th], writes=[thst])
                p.op("pool", lambda e: e.tensor_tensor(out=tg[:], in0=ag[:], in1=ag[:], op=ALU.mult),
                     reads=[tag], writes=[ttg])
                p.op("pool", lambda e: e.tensor_scalar(out=tg[:], in0=tg[:], scalar1=0.044715, scalar2=1.0,
                                                       op0=ALU.mult, op1=ALU.add), reads=[ttg], writes=[ttg])
                p.op("pool", lambda e: e.tensor_tensor(out=tg[:], in0=tg[:], in1=ag[:], op=ALU.mult),
                     reads=[ttg, tag], writes=[ttg])
                p.op("act", lambda e: e.activation(out=sg[:], in_=tg[:], func=AF.Sigmoid,
                                                   scale=2.0 * math.sqrt(2.0 / math.pi)),
                     reads=[ttg], writes=[tsg])
                p.op("pool", lambda e: e.tensor_tensor(out=sg[:], in0=sg[:], in1=ag[:], op=ALU.mult),
                     reads=[tsg, tag], writes=[tsg])
                p.op("dve", lambda e: e.tensor_tensor(out=ob[:], in0=h[:], in1=sg[:], op=ALU.mult),
                     reads=[th, tsg], writes=[tob])
                for c in chs:
                    o = c * 512 - t0
                    p.dma("sp", self.OBR[0, rows, c * 512:(c + 1) * 512], ob[:, o:o + 512], reads=[tob],
                          swrites=[self.tOB[0][c]], pool="st")
                yield


B.genA = genA


def genB(self, l, es):
    p, nc, S = self.p, self.nc, self.S
    TC = 1024
    NT = S // TC
    KC = 31
    if True:
        vb = [self.sb(es, "vb", [128, TC], F32) for _ in range(2)]
        tvb = p.toks(2, "vb")
        gb = [self.sb(es, "gb", [128, TC], F32) for _ in range(2)]
        tgb = p.toks(2, "gb")
        y = [self.sb(es, "y", [128, TC + KC - 1], F32) for _ in range(4)]
        ty = p.toks(4, "y")
        acc = [self.sb(es, "acc", [128, TC], F32) for _ in range(4)]
        tacc = p.toks(4, "acc")
        sqb = [self.sb(es, "sqb", [128, TC], F32) for _ in range(2)]
        tsqb = p.toks(2, "sqb")
        rs = self.sb(es, "rsb", [128, TC], F32)
        trs = p.tok("rsb")
        zb = [self.sb(es, "zb", [128, TC], F32) for _ in range(2)]
        tzb = p.toks(2, "zb")
        ob = [self.sb(es, "obb", [128, TC], BF16) for _ in range(2)]
        tob = p.toks(2, "obb")
        pn = self.ps(es, "pn", [128, TC])
        tpn = p.tok("pn")
        nsub = TC // 512
        for ct in range(4):
            p.op("pool", lambda e, ct=ct: e.memset(y[ct][:, 0:KC - 1], 0.0), writes=[ty[ct]])
        for t in range(NT):
            t0 = t * TC
            chs = list(range(t0 // 512, (t0 + TC) // 512))
            rd = [self.tU[c] for c in chs]
            for ct in range(4):
                b = ct % 2
                p.dma("sp", vb[b][:], self.U16[1024 + ct * 128:1024 + (ct + 1) * 128, t0:t0 + TC], reads=rd,
                      writes=[tvb[b]])
                p.dma("sp", gb[b][:], self.U16[1536 + ct * 128:1536 + (ct + 1) * 128, t0:t0 + TC], reads=rd,
                      writes=[tgb[b]])
                p.op("act", lambda e, b=b: e.activation(out=gb[b][:], in_=gb[b][:], func=AF.Sigmoid),
                     reads=[tgb[b]], writes=[tgb[b]])
                if t > 0:
                    p.op("act", lambda e, ct=ct: e.copy(out=y[ct][:, 0:KC - 1], in_=y[ct][:, TC:TC + KC - 1]),
                         reads=[ty[ct]], writes=[ty[ct]])
                p.op("pool", lambda e, ct=ct, b=b: e.tensor_tensor(out=y[ct][:, KC - 1:KC - 1 + TC], in0=vb[b][:],
                                                                  in1=gb[b][:], op=ALU.mult),
                     reads=[tvb[b], tgb[b], ty[ct]], writes=[ty[ct]])
                w = lambda k: self.vcol(l, "cbw", k * 4 + ct)
                p.op("dve", lambda e, ct=ct, w0=w(0), bb=self.vcol(l, "cbb", ct): e.tensor_scalar(
                    out=acc[ct][:], in0=y[ct][:, 0:TC], scalar1=w0, scalar2=bb, op0=ALU.mult, op1=ALU.add),
                    reads=[ty[ct], self.t_vec], writes=[tacc[ct]])
                for k in range(1, KC):
                    p.op("dve", lambda e, ct=ct, k=k, wk=w(k): e.scalar_tensor_tensor(
                        out=acc[ct][:], in0=y[ct][:, k:k + TC], scalar=wk, in1=acc[ct][:], op0=ALU.mult,
                        op1=ALU.add), reads=[ty[ct], tacc[ct], self.t_vec], writes=[tacc[ct]])
                p.op("act", lambda e, ct=ct, b=b: e.activation(out=sqb[b][:], in_=acc[ct][:], func=AF.Square),
                     reads=[tacc[ct]], writes=[tsqb[b]])
                for s in range(nsub):
                    ss = slice(s * 512, (s + 1) * 512)
                    p.op("pe", lambda e, ss=ss, b=b, ct=ct: e.matmul(pn[:, ss], lhsT=self.cm[:, 5, :],
                                                                    rhs=sqb[b][:, ss], start=(ct == 0),
                                                                    stop=(ct == 3)),
                         reads=[tsqb[b], self.t_cm], swrites=[tpn])
                yield
            p.op("act", lambda e: e.activation(out=rs[:], in_=pn[:], func=AF.Sqrt, bias=self.epsc[:]),
                 reads=[tpn, self.t_eps], writes=[trs])
            p.op("dve", lambda e: e.reciprocal(out=rs[:], in_=rs[:]), reads=[trs], writes=[trs])
            for ct in range(4):
                b = ct % 2
                p.op("dve", lambda e, ct=ct, b=b, g=self.vcol(l, "cbn", ct): e.scalar_tensor_tensor(
                    out=zb[b][:], in0=acc[ct][:], scalar=g, in1=rs[:], op0=ALU.mult, op1=ALU.mult),
                    reads=[tacc[ct], trs, self.t_vec], writes=[tzb[b]])
                p.op("act", lambda e, b=b: e.activation(out=ob[b][:], in_=zb[b][:], func=AF.Silu),
                     reads=[tzb[b]], writes=[tob[b]])
                for c in chs:
                    o = c * 512 - t0
                    p.dma("sp", self.OBR[1, ct * 128:(ct + 1) * 128, c * 512:(c + 1) * 512], ob[b][:, o:o + 512],
                          reads=[tob[b]], swrites=[self.tOB[1][c]], pool="st")
            yield


B.genB = genB


def phaseAB(self, l):
    p = self.p
    NT = self.S // 1024
    with ExitStack() as es:
        gens = [(self.genA(l, es), 4 * NT), (self.genB(l, es), 5 * NT)]
        prog = [0] * len(gens)
        alive = [True] * len(gens)
        while any(alive):
            i = min((k for k in range(len(gens)) if alive[k]), key=lambda k: prog[k] / gens[k][1])
            try:
                next(gens[i][0])
                prog[i] += 1
            except StopIteration:
                alive[i] = False
    p.barrier()


B.phaseAB = phaseAB


def phaseC(self, l):
    p, nc, S = self.p, self.nc, self.S
    NQ = S // 512
    NKT = S // 128
    lam_init = 0.8 - 0.6 * math.exp(-0.3 * l)
    with ExitStack() as es:
        pr16 = self.sb(es, "pr16", [128, 16], F32)
        tpr16 = p.tok("pr16")
        lamt = self.sb(es, "lamt", [128, 4], F32)
        tlam = p.tok("lamt")
        sub2 = self.sb(es, "sub2", [128, 1], F32)
        NP = 2
        pss = [self.ps(es, "pss", [128, 1024]) for _ in range(NP)]
        tpss = p.toks(NP, "pss")
        pO = [self.ps(es, "pO", [128, 512]) for _ in range(2)]
        tpO = p.toks(2, "pO")
        pL = [self.ps(es, "pL", [128, 512]) for _ in range(2)]
        tpL = p.toks(2, "pL")
        pstat = pL[1][:, :]
        tpstat = tpL[1]
        p.op("pool", lambda e: e.memset(pr16[:], 0.0), writes=[tpr16])
        p.op("dve", lambda e: e.tensor_tensor(out=pr16[:, 0:1], in0=self.vcol(l, "lq1"), in1=self.vcol(l, "lk1"),
                                              op=ALU.mult), reads=[self.t_vec], swrites=[tpr16])
        p.op("dve", lambda e: e.tensor_tensor(out=pr16[:, 1:2], in0=self.vcol(l, "lq2"), in1=self.vcol(l, "lk2"),
                                              op=ALU.mult), reads=[self.t_vec], swrites=[tpr16])
        p.op("pe", lambda e: e.matmul(pstat[:, 0:16], lhsT=self.cm[:, 3, :], rhs=pr16[:], start=True, stop=True),
             reads=[tpr16, self.t_cm], writes=[tpstat])
        p.op("act", lambda e: e.activation(out=lamt[:, 0:2], in_=pstat[:, 0:2], func=AF.Exp, scale=128.0),
             reads=[tpstat], writes=[tlam])
        p.op("dve", lambda e: e.tensor_tensor(out=lamt[:, 2:3], in0=lamt[:, 1:2], in1=lamt[:, 0:1], op=ALU.subtract),
             reads=[tlam], writes=[tlam])
        p.op("dve", lambda e: e.tensor_scalar(out=lamt[:, 2:3], in0=lamt[:, 2:3], scalar1=-lam_init, scalar2=None,
                                              op0=ALU.add), reads=[tlam], writes=[tlam])
        p.op("dve", lambda e: e.tensor_scalar(out=lamt[:, 3:4], in0=self.vcol(l, "sub"), scalar1=1.0 - lam_init,
                                              scalar2=None, op0=ALU.mult), reads=[tlam, self.t_vec], writes=[tlam])
        kT = [self.sb(es, "kT", [128, S], BF16) for _ in range(2)]
        qT = [self.sb(es, "qT", [128, S], BF16) for _ in range(2)]
        vT = [self.sb(es, "vT", [128, NKT, 128], BF16) for _ in range(2)]
        thd = p.toks(2, "hd")
        pt = [self.sb(es, "pt", [128, 1024], BF16) for _ in range(3)]
        tpt = p.toks(3, "pt")
        rl = [self.sb(es, "rl", [128, 512], F32) for _ in range(2)]
        trl = p.toks(2, "rl")
        on = [self.sb(es, "on", [128, 512], F32) for _ in range(2)]
        ton = p.toks(2, "on")
        o = self.sb(es, "oc", [128, 512], F32)
        to = p.tok("oc")
        sq = self.sb(es, "sqc", [128, 512], F32)
        tsq = p.tok("sqc")
        rs = self.sb(es, "rsc", [128, 512], F32)
        trs = p.tok("rsc")
        ob = [self.sb(es, "obc", [128, 512], BF16) for _ in range(2)]
        tob = p.toks(2, "obc")
        allqk = list(self.tQK)
        mb = self.sb(es, "maskb", [128, 4, 512], BF16)
        tmb = p.tok("maskb")
        p.op("pool", lambda e: e.memset(mb[:], 0.0), writes=[tmb])
        for j in range(4):
            p.op("pool", lambda e, j=j: e.affine_select(
                out=mb[:, j, :], in_=mb[:, j, :], pattern=[[1, 512]], compare_op=ALU.is_ge,
                fill=self.freg(e, -30000.0), base=-128 * j, channel_multiplier=-1), reads=[tmb], writes=[tmb])

        def load_head(hd):
            b = hd % 2
            rows = slice(hd * 128, (hd + 1) * 128)
            p.dma("sp", kT[b][:], self.CK[rows, :], reads=allqk, writes=[thd[b]])
            p.dma("sp", qT[b][:], self.CQ[rows, :], reads=allqk, swrites=[thd[b]])
            p.dma("sp", vT[b][:], self.CV[:, rows].rearrange("(t p) d -> p t d", p=128), reads=allqk,
                  swrites=[thd[b]])

        load_head(0)
        for hd in range(4):
            b = hd % 2
            if hd + 1 < 4:
                load_head(hd + 1)
            units = [(qc, c, kp) for qc in range(NQ) for c in range(2) for kp in range(2 * qc + 2)]
            base = self._cc
            self._cc += len(units)

            def emit_S(u, idx):
                qc, c, kp = u
                i = idx % NP
                ds = slice(c * 64, (c + 1) * 64)
                qs = slice(qc * 512, (qc + 1) * 512)
                for t in range(2):
                    kt = 2 * kp + t
                    j = kt - 4 * qc
                    p.op("pe", lambda e, i=i, b=b, ds=ds, kt=kt, qs=qs, t=t, j=j: e.matmul(
                        pss[i][:, t * 512:(t + 1) * 512], lhsT=kT[b][ds, kt * 128:(kt + 1) * 128], rhs=qT[b][ds, qs],
                        start=True, stop=(j < 0)), reads=[thd[b]], swrites=[tpss[i]])
                    if j >= 0:
                        p.op("pe", lambda e, i=i, t=t, j=j: e.matmul(
                            pss[i][:, t * 512:(t + 1) * 512], lhsT=self.ident_bf[:], rhs=mb[:, j, :],
                            start=False, stop=True), reads=[tmb, self.t_identb], swrites=[tpss[i]])

            emit_S(units[0], base)
            for n, u in enumerate(units):
                qc, c, kp = u
                idx = base + n
                i = idx % NP
                ip = idx % 3
                nk = 4 * qc + 4
                qs = slice(qc * 512, (qc + 1) * 512)
                p.op("act", lambda e, i=i, ip=ip: e.activation(out=pt[ip][:], in_=pss[i][:], func=AF.Exp, scale=0.125),
                     reads=[tpss[i]], writes=[tpt[ip]])
                if n + 1 < len(units):
                    emit_S(units[n + 1], idx + 1)
                for t in range(2):
                    kt = 2 * kp + t
                    p.op("pe", lambda e, ip=ip, b=b, kt=kt, c=c, nk=nk, t=t: e.matmul(
                        pO[c][:], lhsT=vT[b][:, kt, :], rhs=pt[ip][:, t * 512:(t + 1) * 512], start=(kt == 0),
                        stop=(kt == nk - 1)), reads=[thd[b], tpt[ip]], writes=[tpO[c]])
                    p.op("pe", lambda e, ip=ip, kt=kt, c=c, nk=nk, t=t: e.matmul(
                        pL[c][:], lhsT=self.ones_bf[:], rhs=pt[ip][:, t * 512:(t + 1) * 512], start=(kt == 0),
                        stop=(kt == nk - 1)), reads=[self.t_ones, tpt[ip]], writes=[tpL[c]])
                if 2 * kp + 1 == nk - 1:
                    p.op("dve", lambda e, c=c: e.reciprocal(out=rl[c][:], in_=pL[c][:]), reads=[tpL[c]],
                         writes=[trl[c]])
                    p.op("dve", lambda e, c=c: e.tensor_tensor(out=on[c][:], in0=pO[c][:], in1=rl[c][:], op=ALU.mult),
                         reads=[tpO[c], trl[c]], writes=[ton[c]])
                    if c == 1:
                        ib = (hd * NQ + qc) % 2
                        p.op("dve", lambda e: e.scalar_tensor_tensor(out=o[:], in0=on[1][:], scalar=lamt[:, 2:3],
                                                                     in1=on[0][:], op0=ALU.mult, op1=ALU.add),
                             reads=[ton[0], ton[1], tlam], writes=[to])
                        p.op("pool", lambda e: e.tensor_tensor(out=sq[:], in0=o[:], in1=o[:], op=ALU.mult), reads=[to],
                             writes=[tsq])
                        p.op("pe", lambda e: e.matmul(pstat, lhsT=self.cm[:, 3, :], rhs=sq[:], start=True,
                                                      stop=True), reads=[tsq, self.t_cm], writes=[tpstat])
                        p.op("act", lambda e: e.activation(out=rs[:], in_=pstat, func=AF.Sqrt, bias=self.epsc[:]),
                             reads=[tpstat, self.t_eps], writes=[trs])
                        p.op("dve", lambda e: e.reciprocal(out=rs[:], in_=rs[:]), reads=[trs], writes=[trs])
                        p.op("dve", lambda e, ib=ib: e.scalar_tensor_tensor(out=ob[ib][:], in0=o[:],
                                                                           scalar=lamt[:, 3:4], in1=rs[:],
                                                                           op0=ALU.mult, op1=ALU.mult),
                             reads=[to, trs, tlam], writes=[tob[ib]])
                        p.dma("sp", self.OBR[2, hd * 128:(hd + 1) * 128, qs], ob[ib][:], reads=[tob[ib]],
                              swrites=[self.tOB[2][qc]], pool="st")
    p.barrier()


B.phaseC = phaseC


def phaseD(self, l):
    p, nc, S = self.p, self.nc, self.S
    NQT = S // 128
    NKT = S // 128
    NEG = -1.0e30
    with ExitStack() as es:
        ik4 = self.sb(es, "ik4", [128, S], BF16)
        iqt = [self.sb(es, "iqt", [128, 3, 128], BF16) for _ in range(2)]
        tiqt = p.toks(2, "iqt")
        dkT = self.sb(es, "dkT", [128, S], BF16)
        dvT = self.sb(es, "dvT", [128, NKT, 128], BF16)
        tres = p.tok("dres")
        allqk = list(self.tQK)
        for g in range(3):
            p.dma("sp", ik4[g * 32:(g + 1) * 32, :], self.IK[:, :], reads=allqk, swrites=[tres])
        p.dma("sp", dkT[:], self.DK[:, :], reads=allqk, swrites=[tres])
        p.dma("sp", dvT[:], self.DV.rearrange("(t p) d -> p t d", p=128), reads=allqk, swrites=[tres])
        SC = [self.sb(es, "SC", [128, S], F32) for _ in range(2)]
        tSC = p.toks(2, "SC")
        MK = [self.sb(es, "MK", [128, S], BF16) for _ in range(2)]
        tMK = p.toks(2, "MK")
        wq = [self.sb(es, "wq", [128, 128], F32) for _ in range(2)]
        twq = p.toks(2, "wq")
        dg = [self.sb(es, "dg", [128, 8, 128], BF16) for _ in range(2)]
        tdg = p.toks(2, "dg")
        T = [self.sb(es, "T", [128, 2, 512], BF16) for _ in range(2)]
        tT = p.toks(2, "T")
        st = [self.sb(es, "bst", [128, 8], F32) for _ in range(2)]
        tst = p.toks(2, "bst")
        W2 = [self.sb(es, "W2", [128, NBIS], F32) for _ in range(2)]
        tW2 = p.toks(2, "W2")
        ctab = self.sb(es, "ctab", [128, NBIS], F32)
        tctab = p.tok("ctab")
        for it in range(NBIS):
            p.op("pool", lambda e, it=it: e.memset(ctab[:, it:it + 1], 0.5 ** (it + 1)), swrites=[tctab])
        mTs = [self.sb(es, "mTs", [128, 128], BF16) for _ in range(2)]
        tmTs = p.toks(2, "mTs")
        dq = [self.sb(es, "dq", [128, 4, 128], BF16) for _ in range(2)]
        tdq = p.toks(2, "dq")
        E = [self.sb(es, "E", [128, 4, 128], BF16) for _ in range(2)]
        tE = p.toks(2, "E")
        P = [self.sb(es, "P", [128, 4, 128], BF16) for _ in range(2)]
        tP = p.toks(2, "P")
        rl = self.sb(es, "rld", [128, 512], F32)
        trl = p.tok("rld")
        od = [self.sb(es, "od", [128, 4, 128], BF16) for _ in range(2)]
        tod = p.toks(2, "od")
        pl = [self.ps(es, "pl", [128, 512]) for _ in range(2)]
        tpl = p.toks(2, "pl")
        pl1 = [self.ps(es, "pl1", [128, 512]) for _ in range(2)]
        tpl1 = p.toks(2, "pl1")
        psc = self.ps(es, "psc", [128, 512])
        tpsc = p.tok("psc")
        pmT = self.ps(es, "pmT", [128, 256], BF16)
        tpmT = p.toks(2, "pmT")
        pO = self.ps(es, "pOd", [128, 512])
        tpO = p.tok("pOd")
        pL = self.ps(es, "pLd", [128, 512])
        tpL = p.tok("pLd")
        scale = 128.0 ** -0.5
        ctr = [0, 0]

        def part1(qt):
            qb = qt % 2
            qs = slice(qt * 128, (qt + 1) * 128)
            L = 128 * (qt + 1)
            nkc = (L + 511) // 512
            p.dma("sp", wq[qb][:], self.IW[qs, :], reads=[self.tQK[qt // 4]], writes=[twq[qb]])
            p.dma("sp", iqt[qb][0:96, 0, :], self.IQ[0:96, qs], reads=[self.tQK[qt // 4]], writes=[tiqt[qb]])
            p.dma("sp", iqt[qb][0:96, 1, :], self.IQ[96:192, qs], reads=[self.tQK[qt // 4]], swrites=[tiqt[qb]])
            p.dma("sp", iqt[qb][0:64, 2, :], self.IQ[192:256, qs], reads=[self.tQK[qt // 4]], swrites=[tiqt[qb]])
            for h in range(8):
                p.op("pool", lambda e, h=h, qb=qb: e.tensor_scalar(out=dg[qb][:, h, :], in0=self.ident_bf[:],
                                                                   scalar1=wq[qb][:, h:h + 1], scalar2=1.0,
                                                                   op0=ALU.mult, op1=ALU.mult),
                     reads=[twq[qb], self.t_identb], swrites=[tdg[qb]])
            subs = [(kc, h) for kc in range(nkc) for h in range(8)]
            x0 = ctr[0]
            ctr[0] += len(subs)

            def logits(n):
                kc, h = subs[n]
                x = (x0 + n) % 2
                N = min(512, L - kc * 512)
                ks = slice(kc * 512, kc * 512 + N)
                g, r = h // 3, (h % 3) * 32
                p.op("pe", lambda e, x=x, g=g, r=r, ks=ks, N=N, qb=qb: e.matmul(
                    pl1[x][:, 0:N], lhsT=iqt[qb][r:r + 32, g, :], rhs=ik4[r:r + 32, ks],
                    start=True, stop=True), reads=[tres, tiqt[qb]], writes=[tpl1[x]])

            logits(0)
            for n, (kc, h) in enumerate(subs):
                x = (x0 + n) % 2
                N = min(512, L - kc * 512)
                ks = slice(kc * 512, kc * 512 + N)
                p.op("act", lambda e, x=x, N=N: e.activation(out=T[x][:, 0, 0:N], in_=pl1[x][:, 0:N], func=AF.Relu),
                     reads=[tpl1[x]], writes=[tT[x]])
                if n + 1 < len(subs):
                    logits(n + 1)
                p.op("pe", lambda e, x=x, h=h, N=N, qb=qb: e.matmul(
                    psc[:, 0:N], lhsT=dg[qb][:, h, :], rhs=T[x][:, 0, 0:N], start=(h == 0), stop=(h == 7)),
                    reads=[tdg[qb], tT[x]], writes=[tpsc])
                if h == 7:
                    p.op("act", lambda e, ks=ks, N=N, qb=qb: e.copy(out=SC[qb][:, ks], in_=psc[:, 0:N]), reads=[tpsc],
                         swrites=[tSC[qb]])
                if h % 2 == 1:
                    yield

        def thresh(qt):
            qb = qt % 2
            L = 128 * (qt + 1)
            sc, mk, s_, ts_ = SC[qb], MK[qb], st[qb], tst[qb]
            w2 = W2[qb]
            if L > TOPK:
                p.op("dve", lambda e: e.tensor_reduce(out=s_[:, 1:2], in_=sc[:, 0:L], axis=AX.X, op=ALU.max),
                     reads=[tSC[qb]], writes=[ts_])
                p.op("dve", lambda e: e.tensor_reduce(out=s_[:, 0:1], in_=sc[:, 0:TOPK], axis=AX.X, op=ALU.min),
                     reads=[tSC[qb]], writes=[ts_])
                p.op("dve", lambda e: e.tensor_tensor(out=s_[:, 2:3], in0=s_[:, 1:2], in1=s_[:, 0:1],
                                                      op=ALU.subtract), reads=[ts_], writes=[ts_])
                p.op("dve", lambda e: e.tensor_scalar(out=w2[:], in0=ctab[:], scalar1=s_[:, 2:3], scalar2=None,
                                                      op0=ALU.mult), reads=[ts_, tctab], writes=[tW2[qb]])
                p.op("dve", lambda e: e.tensor_tensor(out=s_[:, 3:4], in0=s_[:, 0:1], in1=w2[:, 0:1], op=ALU.add),
                     reads=[ts_, tW2[qb]], writes=[ts_])
            else:
                p.op("dve", lambda e: e.memset(s_[:, 0:1], -1.0e29), writes=[ts_])
            p.op("pool", lambda e: e.affine_select(
                out=sc[:, qt * 128:(qt + 1) * 128], in_=sc[:, qt * 128:(qt + 1) * 128], pattern=[[-1, 128]],
                compare_op=ALU.is_ge, fill=self.freg(e, NEG), base=0, channel_multiplier=1),
                reads=[tSC[qb]], writes=[tSC[qb]])
            if L > TOPK:
                for it in range(NBIS):
                    p.op("dve", lambda e: e.tensor_scalar(out=mk[:, 0:L], in0=sc[:, 0:L], scalar1=s_[:, 3:4],
                                                          scalar2=None, op0=ALU.is_ge, op1=ALU.add,
                                                          accum_out=s_[:, 4:5]),
                         reads=[ts_, tSC[qb]], writes=[ts_], swrites=[tMK[qb]])
                    p.op("dve", lambda e: e.tensor_scalar(out=s_[:, 5:6], in0=s_[:, 4:5], scalar1=TOPK - 0.5,
                                                          scalar2=-0.5, op0=ALU.is_ge, op1=ALU.add),
                         reads=[ts_], writes=[ts_])
                    p.op("dve", lambda e, it=it: e.scalar_tensor_tensor(out=s_[:, 3:4], in0=s_[:, 5:6],
                                                                       scalar=w2[:, it:it + 1], in1=s_[:, 3:4],
                                                                       op0=ALU.mult, op1=ALU.add),
                         reads=[ts_, tW2[qb]], writes=[ts_])
                    yield
                p.op("dve", lambda e: e.scalar_tensor_tensor(out=s_[:, 0:1], in0=w2[:, NBIS - 1:NBIS], scalar=-0.5,
                                                             in1=s_[:, 3:4], op0=ALU.mult, op1=ALU.add),
                     reads=[ts_, tW2[qb]], writes=[ts_])
            p.op("dve", lambda e: e.tensor_scalar(out=mk[:, 0:L], in0=sc[:, 0:L], scalar1=s_[:, 0:1],
                                                  scalar2=None, op0=ALU.is_ge),
                 reads=[ts_, tSC[qb]], writes=[tMK[qb]])

        def part2(qt):
            qb = qt % 2
            qs = slice(qt * 128, (qt + 1) * 128)
            p.dma("sp", dq[qb][:], self.DQ[:, qs].rearrange("(h d) q -> d h q", d=128),
                  reads=[self.tQK[qt // 4]], writes=[tdq[qb]])
            z0 = ctr[1]
            ctr[1] += qt + 1

            def frontT(kt):
                z = (z0 + kt) % 2
                kts = slice(kt * 128, (kt + 1) * 128)
                p.op("pe", lambda e, kts=kts, qb=qb, z=z: e.transpose(out=pmT[:, z * 128:(z + 1) * 128],
                                                                     in_=MK[qb][:, kts], identity=self.ident_bf[:]),
                     reads=[tMK[qb], self.t_identb], writes=[tpmT[z]])

            def front(kt):
                z = (z0 + kt) % 2
                kts = slice(kt * 128, (kt + 1) * 128)
                p.op("pe", lambda e, z=z, kts=kts, qb=qb: e.matmul(
                    pl[z][:, 0:512], lhsT=dkT[:, kts], rhs=dq[qb][:].rearrange("p h q -> p (h q)"), start=True,
                    stop=True), reads=[tres, tdq[qb]], writes=[tpl[z]])

            front(0)
            for kt in range(qt + 1):
                z = (z0 + kt) % 2
                frontT(kt)
                p.op("act", lambda e, z=z: e.activation(out=E[z][:].rearrange("p h q -> p (h q)"), in_=pl[z][:, 0:512],
                                                        func=AF.Exp, scale=scale),
                     reads=[tpl[z]], writes=[tE[z]])
                p.op("act", lambda e, z=z: e.copy(out=mTs[z][:], in_=pmT[:, z * 128:(z + 1) * 128]),
                     reads=[tpmT[z]], writes=[tmTs[z]])
                p.op("pool", lambda e, z=z: e.tensor_tensor(
                    out=P[z][:], in0=E[z][:],
                    in1=mTs[z][:].rearrange("p (o q) -> p o q", o=1).to_broadcast([128, 4, 128]),
                    op=ALU.mult), reads=[tE[z], tmTs[z]], writes=[tP[z]])
                if kt + 1 <= qt:
                    front(kt + 1)
                p.op("pe", lambda e, z=z, kt=kt, qt=qt: e.matmul(
                    pO[:], lhsT=dvT[:, kt, :], rhs=P[z][:].rearrange("p h q -> p (h q)"), start=(kt == 0),
                    stop=(kt == qt)), reads=[tres, tP[z]], writes=[tpO])
                p.op("pe", lambda e, z=z, kt=kt, qt=qt: e.matmul(
                    pL[:], lhsT=self.ones_bf[:], rhs=P[z][:].rearrange("p h q -> p (h q)"), start=(kt == 0),
                    stop=(kt == qt)), reads=[self.t_ones, tP[z]], writes=[tpL])
                yield
            p.op("dve", lambda e: e.reciprocal(out=rl[:], in_=pL[:]), reads=[tpL], writes=[trl])
            p.op("dve", lambda e, qb=qb: e.tensor_tensor(out=od[qb][:].rearrange("p h q -> p (h q)"), in0=pO[:],
                                                         in1=rl[:], op=ALU.mult),
                 reads=[tpO, trl], writes=[tod[qb]])
            p.dma("sp", self.OBR[3, :, qs].rearrange("(h d) q -> d h q", d=128), od[qb][:], reads=[tod[qb]],
                  swrites=[self.tOB[3][qt // 4]], pool="st")

        def merge(gens):
            prog = [0] * len(gens)
            alive = [True] * len(gens)
            while any(alive):
                i = min((k for k in range(len(gens)) if alive[k]), key=lambda k: prog[k] / gens[k][1])
                try:
                    next(gens[i][0])
                    prog[i] += 1
                except StopIteration:
                    alive[i] = False

        def ensure_gen(g):
            return g

        for stage in range(NQT + 2):
            gens = []
            t1, t2, t3 = stage, stage - 1, stage - 2
            if 0 <= t1 < NQT:
                gens.append((part1(t1), ((128 * (t1 + 1) + 511) // 512) * 4 + 1))
            if 0 <= t2 < NQT:
                gens.append((thresh(t2), NBIS + 1))
            if 0 <= t3 < NQT:
                gens.append((part2(t3), t3 + 2))
            merge(gens)
    p.barrier()


B.phaseD = phaseD


def phase3(self, l):
    p, nc, S = self.p, self.nc, self.S
    hin = self.xT if l == 0 else self.H
    with ExitStack() as es:
        Wg = self.sb(es, "Wg", [128, 32, D], BF16)
        tWg = p.toks(32, "Wg")
        Wb = self.sb(es, "Wb", [128, 16, D], BF16)
        tWb = p.toks(16, "Wb")
        Wo = self.sb(es, "Wo", [128, 8, D], BF16)
        tWo = p.toks(8, "Wo")
        with ExitStack() as es2:
            stg = [self.sb(es2, "p3stg", [128, D], F32) for _ in range(2)]
            tstg = p.toks(2, "p3stg")
            for j in range(4):
                self.load_weight(Wg[:, j * 8:(j + 1) * 8, :], tWg[j * 8:(j + 1) * 8], self.w_gate[l, j], 8, D, stg,
                                 tstg, gain=lambda kc: self.vcol(l, "gmix", kc))
                self.load_weight(Wb[:, j * 4:(j + 1) * 4, :], tWb[j * 4:(j + 1) * 4], self.w_branch[l, j], 4, D, stg,
                                 tstg)
            self.load_weight(Wo, tWo, self.w_out[l], 8, D, stg, tstg)
            p.barrier()
        xn = [self.sb(es, "xn3", [128, 8, 512], BF16) for _ in range(2)]
        txn = p.toks(2, "xn3")
        obr = [self.sb(es, "obr3", [128, 16, 512], BF16) for _ in range(2)]
        tobr = p.toks(2, "obr3")
        hb = self.sb(es, "hb3", [128, 8, 512], F32)
        thb = p.tok("hb3")
        mixed = self.sb(es, "mixed", [128, 8, 512], BF16)
        tmixed = p.toks(8, "mixed")
        sg = [self.sb(es, "sg3", [128, 512], F32) for _ in range(2)]
        tsg = p.toks(2, "sg3")
        pr = [self.sb(es, "pr3", [128, 512], F32) for _ in range(2)]
        tpr = p.toks(2, "pr3")
        mix = self.sb(es, "mix3", [128, 512], F32)
        tmix = p.tok("mix3")
        pg = [self.ps(es, "pg", [128, 512]) for _ in range(2)]
        tpg = p.toks(2, "pg")
        pp = [self.ps(es, "pp", [128, 512]) for _ in range(2)]
        tpp = p.toks(2, "pp")
        po = [self.ps(es, "po", [128, 512]) for _ in range(2)]
        tpo = p.toks(2, "po")

        def loads(c):
            b = c % 2
            sl = slice(c * 512, (c + 1) * 512)
            p.dma("sp", xn[b][:], self.XN[:, sl].rearrange("(k p) t -> p k t", p=128), reads=[self.tXN[c]],
                  writes=[txn[b]])
            for j in range(4):
                p.dma("sp", obr[b][:, j * 4:(j + 1) * 4, :],
                      self.OBR[j, :, sl].rearrange("(k p) t -> p k t", p=128), reads=[self.tOB[j][c]],
                      swrites=[tobr[b]])

        loads(0)
        cnt = 0
        for c in range(self.NCH):
            b = c % 2
            sl = slice(c * 512, (c + 1) * 512)
            if c + 1 < self.NCH:
                loads(c + 1)
            rd = [self.tH[c]] if l > 0 else []
            p.dma("sp", hb[:], hin[:, sl].rearrange("(k p) t -> p k t", p=128), reads=rd, writes=[thb])
            for m in range(8):
                ms = slice(m * 128, (m + 1) * 128)
                for j in range(4):
                    i = cnt % 2
                    cnt += 1
                    for kc in range(8):
                        p.op("pe", lambda e, i=i, j=j, kc=kc, ms=ms, b=b: e.matmul(
                            pg[i][:], lhsT=Wg[:, j * 8 + kc, ms], rhs=xn[b][:, kc, :], start=(kc == 0), stop=(kc == 7)),
                            reads=[tWg[j * 8 + kc], txn[b]], writes=[tpg[i]])
                    for kc in range(4):
                        p.op("pe", lambda e, i=i, j=j, kc=kc, ms=ms, b=b: e.matmul(
                            pp[i][:], lhsT=Wb[:, j * 4 + kc, ms], rhs=obr[b][:, j * 4 + kc, :], start=(kc == 0),
                            stop=(kc == 3)), reads=[tWb[j * 4 + kc], tobr[b]], writes=[tpp[i]])
                    p.op("act", lambda e, i=i: e.activation(out=sg[i][:], in_=pg[i][:], func=AF.Sigmoid),
                         reads=[tpg[i]], writes=[tsg[i]])
                    if j == 0:
                        p.op("dve", lambda e, i=i: e.tensor_tensor(out=mix[:], in0=pp[i][:], in1=sg[i][:], op=ALU.mult),
                             reads=[tpp[i], tsg[i]], writes=[tmix])
                    else:
                        p.op("dve", lambda e, i=i: e.tensor_tensor(out=pr[i][:], in0=pp[i][:], in1=sg[i][:],
                                                                   op=ALU.mult),
                             reads=[tpp[i], tsg[i]], writes=[tpr[i]])
                        if j < 3:
                            p.op("pool", lambda e, i=i: e.tensor_tensor(out=mix[:], in0=mix[:], in1=pr[i][:],
                                                                        op=ALU.add),
                                 reads=[tmix, tpr[i]], writes=[tmix])
                        else:
                            p.op("pool", lambda e, i=i, m=m: e.tensor_tensor(out=mixed[:, m, :], in0=mix[:],
                                                                             in1=pr[i][:], op=ALU.add),
                                 reads=[tmix, tpr[i]], writes=[tmixed[m]])
            for m2 in range(8):
                i = m2 % 2
                ms = slice(m2 * 128, (m2 + 1) * 128)
                for m in range(8):
                    p.op("pe", lambda e, i=i, m=m, ms=ms: e.matmul(po[i][:], lhsT=Wo[:, m, ms], rhs=mixed[:, m, :],
                                                                  start=(m == 0), stop=(m == 7)),
                         reads=[tWo[m], tmixed[m]], writes=[tpo[i]])
                p.op("dve", lambda e, i=i, m2=m2: e.tensor_tensor(out=hb[:, m2, :], in0=hb[:, m2, :], in1=po[i][:],
                                                                  op=ALU.add),
                     reads=[tpo[i]], swrites=[thb])
            p.dma("sp", self.H[:, sl].rearrange("(k p) t -> p k t", p=128), hb[:], reads=[thb],
                  writes=[self.tH[c]], pool="st")
    p.barrier()


B.phase3 = phase3


def phase4(self, l):
    p, nc, S = self.p, self.nc, self.S
    NF = 44
    with ExitStack() as es:
        Wu = self.sb(es, "Wu", [128, 8, 2 * DFF], BF16)
        tWu = p.toks(8, "Wu")
        Wd = self.sb(es, "Wd", [128, 22, D], BF16)
        tWd = p.toks(22, "Wd")
        with ExitStack() as es2:
            stg = [self.sb(es2, "p4stg", [128, 2 * DFF], F32) for _ in range(2)]
            tstg = p.toks(2, "p4stg")
            self.load_weight(Wu, tWu, self.w_up[l], 8, 2 * DFF, stg, tstg, gain=lambda kc: self.vcol(l, "gffn", kc))
            self.load_weight(Wd, tWd, self.w_down[l], 22, D, stg, tstg)
            p.barrier()
        hb = self.sb(es, "hb4", [128, 8, 512], F32)
        thb = p.tok("hb4")
        sq = [self.sb(es, "sq4", [128, 512], F32) for _ in range(2)]
        tsq = p.toks(2, "sq4")
        rstd = self.sb(es, "rstd4", [128, 512], F32)
        trstd = p.tok("rstd4")
        xf = self.sb(es, "xf", [128, 8, 512], BF16)
        txf = p.tok("xf")
        F = [self.sb(es, "F4", [128, 514], F32) for _ in range(2)]
        tF = p.toks(2, "F4")
        y = [self.sb(es, "y4", [128, 512], F32) for _ in range(4)]
        ty = p.toks(4, "y4")
        A = self.sb(es, "A4", [128, 22, 512], BF16)
        tA = p.toks(22, "A4")
        carry = [self.sb(es, "carry", [128, NF, 2], F32) for _ in range(2)]
        tcar = [p.toks(NF, "carry") for _ in range(2)]
        pst = self.ps(es, "pst4", [128, 512])
        tpst = p.tok("pst4")
        pu = [self.ps(es, "pu", [128, 512]) for _ in range(3)]
        tpu = p.toks(3, "pu")
        pd = [self.ps(es, "pd", [128, 512]) for _ in range(2)]
        tpd = p.toks(2, "pd")
        p.op("pool", lambda e: e.memset(carry[0][:], 0.0), writes=tcar[0])
        cnt = [0, 0]
        for c in range(self.NCH):
            sl = slice(c * 512, (c + 1) * 512)
            p.dma("sp", hb[:], self.H[:, sl].rearrange("(k p) t -> p k t", p=128), reads=[self.tH[c]], writes=[thb])
            for kc in range(8):
                i = kc % 2
                p.op("act", lambda e, i=i, kc=kc: e.activation(out=sq[i][:], in_=hb[:, kc, :], func=AF.Square),
                     reads=[thb], writes=[tsq[i]])
                p.op("pe", lambda e, i=i, kc=kc: e.matmul(pst[:], lhsT=self.cm[:, 1, :], rhs=sq[i][:], start=(kc == 0),
                                                         stop=(kc == 7)),
                     reads=[tsq[i], self.t_cm], writes=[tpst])
            p.op("act", lambda e: e.activation(out=rstd[:], in_=pst[:], func=AF.Sqrt, bias=self.epsc[:]),
                 reads=[tpst, self.t_eps], writes=[trstd])
            p.op("dve", lambda e: e.reciprocal(out=rstd[:], in_=rstd[:]), reads=[trstd], writes=[trstd])
            for kc in range(8):
                eng = "dve" if kc % 2 == 0 else "pool"
                p.op(eng, lambda e, kc=kc: e.tensor_tensor(out=xf[:, kc, :], in0=hb[:, kc, :], in1=rstd[:], op=ALU.mult),
                     reads=[thb, trstd], swrites=[txf])

            def up_tile(ft, yb):
                i = cnt[0] % 3
                cnt[0] += 1
                fb = cnt[1] % 2
                cnt[1] += 1
                cs = slice(ft * 128, (ft + 1) * 128)
                for kc in range(8):
                    p.op("pe", lambda e, i=i, kc=kc, cs=cs: e.matmul(pu[i][:], lhsT=Wu[:, kc, cs], rhs=xf[:, kc, :],
                                                                    start=(kc == 0), stop=(kc == 7)),
                         reads=[tWu[kc], txf], writes=[tpu[i]])
                w = lambda k: self.vcol(l, "cfw", k * NF + ft)
                cin, cout = carry[c % 2], carry[(c + 1) % 2]
                tcin, tcout = tcar[c % 2], tcar[(c + 1) % 2]
                p.op("act", lambda e, fb=fb, ft=ft, cin=cin: e.copy(out=F[fb][:, 0:2], in_=cin[:, ft, :]),
                     reads=[tcin[ft]], writes=[tF[fb]])
                p.op("act", lambda e, fb=fb, i=i: e.copy(out=F[fb][:, 2:514], in_=pu[i][:]), reads=[tpu[i]],
                     swrites=[tF[fb]])
                p.op("act", lambda e, i=i, ft=ft, cout=cout: e.copy(out=cout[:, ft, :], in_=pu[i][:, 510:512]),
                     reads=[tpu[i]], writes=[tcout[ft]])
                p.op("act", lambda e, i=i, yb=yb, w2=w(2), bb=self.vcol(l, "cfb", ft): e.activation(
                    out=y[yb][:], in_=pu[i][:], func=AF.Identity, scale=w2, bias=bb),
                    reads=[tpu[i], self.t_vec], writes=[ty[yb]])
                for k in (1, 0):
                    p.op("dve", lambda e, fb=fb, yb=yb, k=k, wk=w(k): e.scalar_tensor_tensor(
                        out=y[yb][:], in0=F[fb][:, k:k + 512], scalar=wk, in1=y[yb][:], op0=ALU.mult, op1=ALU.add),
                        reads=[tF[fb], ty[yb], self.t_vec], writes=[ty[yb]])

            for j in range(22):
                yg = (2 * j) % 4
                yv = (2 * j + 1) % 4
                up_tile(j, yg)
                up_tile(j + 22, yv)
                p.op("act", lambda e, yg=yg: e.activation(out=y[yg][:], in_=y[yg][:], func=AF.Silu),
                     reads=[ty[yg]], writes=[ty[yg]])
                p.op("pool", lambda e, yg=yg, yv=yv, j=j: e.tensor_tensor(out=A[:, j, :], in0=y[yg][:], in1=y[yv][:],
                                                                         op=ALU.mult),
                     reads=[ty[yg], ty[yv]], writes=[tA[j]])
            for m2 in range(8):
                i = m2 % 2
                ms = slice(m2 * 128, (m2 + 1) * 128)
                for j in range(22):
                    p.op("pe", lambda e, i=i, j=j, ms=ms: e.matmul(pd[i][:], lhsT=Wd[:, j, ms], rhs=A[:, j, :],
                                                                  start=(j == 0), stop=(j == 21)),
                         reads=[tWd[j], tA[j]], writes=[tpd[i]])
                p.op("dve", lambda e, i=i, m2=m2: e.tensor_tensor(out=hb[:, m2, :], in0=hb[:, m2, :], in1=pd[i][:],
                                                                  op=ALU.add),
                     reads=[tpd[i]], swrites=[thb])
            p.dma("sp", self.H[:, sl].rearrange("(k p) t -> p k t", p=128), hb[:], reads=[thb],
                  writes=[self.tH[c]], pool="st")
    p.barrier()
    self.phase5(l)


B.phase4 = phase4


def phase5(self, l):
    p, nc, S = self.p, self.nc, self.S
    last = (l == self.L - 1)
    dst = self.outT if last else self.H
    with ExitStack() as es:
        Wpg = self.sb(es, "Wpg", [128, 8, D], BF16)
        tWpg = p.toks(8, "Wpg")
        Wpi = self.sb(es, "Wpi", [128, 2, D], BF16)
        tWpi = p.toks(2, "Wpi")
        stg = [self.sb(es, "p5stg", [128, D], F32) for _ in range(2)]
        tstg = p.toks(2, "p5stg")
        self.load_weight(Wpg, tWpg, self.w_pg[l], 8, D, stg, tstg, gain=lambda kc: self.vcol(l, "gpg", kc))
        self.load_weight(Wpi, tWpi, self.w_pi[l], 2, D, stg, tstg)
        hb = [self.sb(es, "hb5", [128, 8, 512], F32) for _ in range(2)]
        thb = p.toks(2, "hb5")
        pb = [self.sb(es, "pb5", [128, 2, 512], F32) for _ in range(2)]
        tpb = p.toks(2, "pb5")
        pbb = self.sb(es, "pbb5", [128, 2, 512], BF16)
        tpbb = p.tok("pbb5")
        sq = [self.sb(es, "sq5", [128, 512], F32) for _ in range(2)]
        tsq = p.toks(2, "sq5")
        rstd = self.sb(es, "rstd5", [128, 512], F32)
        trstd = p.tok("rstd5")
        rse = self.sb(es, "rse5", [128, 512], F32)
        trse = p.tok("rse5")
        xg = self.sb(es, "xg5", [128, 8, 512], BF16)
        txg = p.tok("xg5")
        ee = self.sb(es, "ee5", [128, 8, 512], F32)
        tee = p.toks(8, "ee5")
        sg = [self.sb(es, "sg5", [128, 512], F32) for _ in range(2)]
        tsg = p.toks(2, "sg5")
        t1 = [self.sb(es, "t15", [128, 512], F32) for _ in range(2)]
        tt1 = p.toks(2, "t15")
        pst = self.ps(es, "pst5", [128, 512])
        tpst = p.tok("pst5")
        pse = self.ps(es, "pse5", [128, 512])
        tpse = p.tok("pse5")
        pe_ = [self.ps(es, "pe5", [128, 512]) for _ in range(2)]
        tpe = p.toks(2, "pe5")
        pg = [self.ps(es, "pg5", [128, 512]) for _ in range(2)]
        tpg = p.toks(2, "pg5")

        def loads(c):
            b = c % 2
            sl = slice(c * 512, (c + 1) * 512)
            p.dma("sp", hb[b][:], self.H[:, sl].rearrange("(k p) t -> p k t", p=128), reads=[self.tH[c]],
                  writes=[thb[b]])
            p.dma("sp", pb[b][:], self.pT[l, :, sl].rearrange("(k p) t -> p k t", p=128), writes=[tpb[b]])

        loads(0)
        for c in range(self.NCH):
            b = c % 2
            sl = slice(c * 512, (c + 1) * 512)
            if c + 1 < self.NCH:
                loads(c + 1)
            hbb = hb[b]
            for kc in range(8):
                i = kc % 2
                p.op("act", lambda e, i=i, kc=kc, hbb=hbb: e.activation(out=sq[i][:], in_=hbb[:, kc, :], func=AF.Square),
                     reads=[thb[b]], writes=[tsq[i]])
                p.op("pe", lambda e, i=i, kc=kc: e.matmul(pst[:], lhsT=self.cm[:, 1, :], rhs=sq[i][:], start=(kc == 0),
                                                         stop=(kc == 7)),
                     reads=[tsq[i], self.t_cm], writes=[tpst])
            p.op("act", lambda e: e.activation(out=rstd[:], in_=pst[:], func=AF.Sqrt, bias=self.epsc[:]),
                 reads=[tpst, self.t_eps], writes=[trstd])
            p.op("dve", lambda e: e.reciprocal(out=rstd[:], in_=rstd[:]), reads=[trstd], writes=[trstd])
            for kc in range(8):
                eng = "dve" if kc % 2 == 0 else "pool"
                p.op(eng, lambda e, kc=kc, hbb=hbb: e.tensor_tensor(out=xg[:, kc, :], in0=hbb[:, kc, :], in1=rstd[:],
                                                                   op=ALU.mult),
                     reads=[thb[b], trstd], swrites=[txg])
            p.op("pool", lambda e, b=b: e.tensor_copy(out=pbb[:], in_=pb[b][:]), reads=[tpb[b]], writes=[tpbb])
            for m in range(8):
                i = m % 2
                ms = slice(m * 128, (m + 1) * 128)
                for kc in range(2):
                    p.op("pe", lambda e, i=i, kc=kc, ms=ms: e.matmul(pe_[i][:], lhsT=Wpi[:, kc, ms], rhs=pbb[:, kc, :],
                                                                    start=(kc == 0), stop=(kc == 1)),
                         reads=[tWpi[kc], tpbb], writes=[tpe[i]])
                p.op("act", lambda e, i=i, m=m: e.copy(out=ee[:, m, :], in_=pe_[i][:]), reads=[tpe[i]], writes=[tee[m]])
                p.op("pool", lambda e, i=i, m=m: e.tensor_tensor(out=sq[i][:], in0=ee[:, m, :], in1=ee[:, m, :],
                                                                 op=ALU.mult),
                     reads=[tee[m]], writes=[tsq[i]])
                p.op("pe", lambda e, i=i, m=m: e.matmul(pse[:], lhsT=self.cm[:, 1, :], rhs=sq[i][:], start=(m == 0),
                                                       stop=(m == 7)),
                     reads=[tsq[i], self.t_cm], writes=[tpse])
            p.op("act", lambda e: e.activation(out=rse[:], in_=pse[:], func=AF.Sqrt, bias=self.epsc[:]),
                 reads=[tpse, self.t_eps], writes=[trse])
            p.op("dve", lambda e: e.reciprocal(out=rse[:], in_=rse[:]), reads=[trse], writes=[trse])
            for m in range(8):
                i = m % 2
                ms = slice(m * 128, (m + 1) * 128)
                for kc in range(8):
                    p.op("pe", lambda e, i=i, kc=kc, ms=ms: e.matmul(pg[i][:], lhsT=Wpg[:, kc, ms], rhs=xg[:, kc, :],
                                                                    start=(kc == 0), stop=(kc == 7)),
                         reads=[tWpg[kc], txg], writes=[tpg[i]])
                p.op("act", lambda e, i=i: e.activation(out=sg[i][:], in_=pg[i][:], func=AF.Sigmoid),
                     reads=[tpg[i]], writes=[tsg[i]])
                p.op("dve", lambda e, i=i, m=m, g=self.vcol(l, "gple", m): e.scalar_tensor_tensor(
                    out=t1[i][:], in0=ee[:, m, :], scalar=g, in1=rse[:], op0=ALU.mult, op1=ALU.mult),
                    reads=[tee[m], trse, self.t_vec], writes=[tt1[i]])
                p.op("pool", lambda e, i=i: e.tensor_tensor(out=t1[i][:], in0=t1[i][:], in1=sg[i][:], op=ALU.mult),
                     reads=[tt1[i], tsg[i]], writes=[tt1[i]])
                p.op("dve", lambda e, i=i, m=m, hbb=hbb: e.tensor_tensor(out=hbb[:, m, :], in0=hbb[:, m, :],
                                                                        in1=t1[i][:], op=ALU.add),
                     reads=[tt1[i], txg], swrites=[thb[b]])
            p.dma("sp", dst[:, sl].rearrange("(k p) t -> p k t", p=128), hbb[:], reads=[thb[b]],
                  writes=[self.tH[c]], pool="st")
    p.barrier()


B.phase5 = phase5
```

```python
import math
from contextlib import ExitStack

import numpy as np
import concourse.bass as bass
import concourse.mybir as mybir
from concourse.bass_utils import run_bass_kernel_spmd

F32 = mybir.dt.float32
BF16 = mybir.dt.bfloat16
I32 = mybir.dt.int32
AF = mybir.ActivationFunctionType
ALU = mybir.AluOpType
AX = mybir.AxisListType

ENGS = ("pe", "act", "dve", "pool", "sp")

D = 1024
DB = 512
DIN = 4648
DFF = 2816
PLE = 256
EPS = 1e-6
TOPK = 256
NBIS = 12


class Tok:
    __slots__ = ("name", "W", "R", "prev")

    def __init__(self, name=""):
        self.name = name
        self.W = set()
        self.R = set()
        self.prev = set()


class Ins:
    __slots__ = ("eng", "fn", "deps", "dsem", "needs_inc", "val", "waits")

    def __init__(self, eng, fn, deps, dsem):
        self.eng = eng
        self.fn = fn
        self.deps = deps
        self.dsem = dsem
        self.needs_inc = False
        self.val = None
        self.waits = None


class Prog:
    DMA_POOL = 8

    def __init__(self, nc):
        self.nc = nc
        self.ins = []
        self.dpool = {}
        self.last = {}
        self.bar = {}

    def tok(self, name=""):
        return Tok(name)

    def toks(self, n, name=""):
        return [Tok(f"{name}{i}") for i in range(n)]

    def _deps(self, eng, reads, writes, swrites=()):
        deps = set()
        idx = len(self.ins)
        for t in reads:
            deps |= t.W
            t.R.add(idx)
        for t in writes:
            deps |= t.W
            deps |= t.R
            deps |= t.prev
            t.W = {idx}
            t.R = set()
            t.prev = {idx}
        for t in swrites:
            if t.R:
                t.prev = t.R | t.W
                t.W = set()
                t.R = set()
            deps |= t.prev
            t.W.add(idx)
        deps.discard(idx)
        if eng in self.bar:
            deps |= self.bar.pop(eng)
        self.last[eng] = idx
        return deps

    def barrier(self):
        b = set(self.last.values())
        for hist in self.dpool.values():
            b |= set(hist[-self.DMA_POOL:])
        for e in ENGS:
            self.bar[e] = set(b) | self.bar.get(e, set())

    def op(self, eng, fn, reads=(), writes=(), swrites=()):
        deps = self._deps(eng, reads, writes, swrites)
        self.ins.append(Ins(eng, fn, deps, None))

    def dma(self, eng, out, in_, reads=(), writes=(), swrites=(), pool="ld", slow=False):
        deps = self._deps(eng, reads, writes, swrites)
        hist = self.dpool.setdefault(pool, [])
        i = len(hist)
        if i >= self.DMA_POOL:
            deps.add(hist[i - self.DMA_POOL])
        hist.append(len(self.ins))
        if slow:
            fn = lambda e: e.dma_start(out=out, in_=in_, allow_slow_non_contiguous=True)
        else:
            fn = lambda e: e.dma_start(out=out, in_=in_)
        self.ins.append(Ins(eng, fn, deps, f"{pool}{i % self.DMA_POOL}"))

    def build(self, final_pools=("st",)):
        ins = self.ins
        n = len(ins)

        def skip(p, it):
            return p.eng == "pe" and it.eng == "pe" and p.dsem is None and it.dsem is None

        for it in ins:
            for d in it.deps:
                p = ins[d]
                if not skip(p, it):
                    p.needs_inc = True
        final_ids = []
        for pl in final_pools:
            final_ids += self.dpool.get(pl, [])[-self.DMA_POOL:]
        cnt = {}
        for it in ins:
            if it.dsem is not None:
                key = "D_" + it.dsem
                cnt[key] = cnt.get(key, 0) + 16
                it.val = (key, cnt[key])
            elif it.needs_inc:
                key = "E_" + it.eng
                cnt[key] = cnt.get(key, 0) + 1
                it.val = (key, cnt[key])
        known = {e: {} for e in ENGS}
        evclock = {}
        nwaits = 0
        for it in ins:
            kn = known[it.eng]
            need = {}
            for d in it.deps:
                p = ins[d]
                if p.val is None or skip(p, it):
                    continue
                s, v = p.val
                if kn.get(s, 0) >= v:
                    continue
                if need.get(s, 0) < v:
                    need[s] = v
            waits = []
            for s, v in sorted(need.items(), key=lambda kv: -kv[1]):
                if kn.get(s, 0) >= v:
                    continue
                waits.append((s, v))
                ck = evclock.get((s, v))
                if ck:
                    for ks, kv in ck.items():
                        if kn.get(ks, 0) < kv:
                            kn[ks] = kv
                if kn.get(s, 0) < v:
                    kn[s] = v
            it.waits = waits
            nwaits += len(waits)
            if it.val is not None:
                ck = dict(kn)
                ck[it.val[0]] = it.val[1]
                evclock[it.val] = ck
        self.stats = dict(n=n, nwaits=nwaits, sems=dict(cnt))
        nc = self.nc
        semnames = sorted({it.val[0] for it in ins if it.val is not None})
        with ExitStack() as es:
            sems = {s: es.enter_context(nc.semaphore(s)) for s in semnames}
            block = es.enter_context(nc.Block())
            per = {e: [it for it in ins if it.eng == e] for e in ENGS}
            finals = [ins[d].val for d in final_ids]

            def run(engobj, lst, is_last=False):
                for it in lst:
                    for s, v in it.waits[1:]:
                        engobj.wait_ge(sems[s], v)
                    r = it.fn(engobj)
                    if it.waits:
                        r._wait_ge(sems[it.waits[0][0]], it.waits[0][1])
                    if it.val is not None:
                        r.then_inc(sems[it.val[0]], 16 if it.dsem is not None else 1)
                if is_last:
                    fm = {}
                    for s, v in finals:
                        fm[s] = max(fm.get(s, 0), v)
                    for s, v in fm.items():
                        engobj.wait_ge(sems[s], v)

            @block.tensor
            def _(e):
                run(e, per["pe"])

            @block.scalar
            def _(e):
                run(e, per["act"])

            @block.vector
            def _(e):
                run(e, per["dve"])

            @block.gpsimd
            def _(e):
                run(e, per["pool"], is_last=True)

            @block.sync
            def _(e):
                run(e, per["sp"])


VEC_FIELDS = [("gmix", 8), ("gffn", 8), ("gpg", 8), ("gple", 8), ("caw", 16), ("cab", 4), ("lbr", 4),
              ("lbi", 4), ("lam", 4), ("cbw", 124), ("cbb", 4), ("cbn", 4), ("dqn", 1), ("dkn", 1),
              ("sqn", 1), ("skn", 1), ("ikn", 1), ("sub", 1), ("lq1", 1), ("lk1", 1), ("lq2", 1),
              ("lk2", 1), ("cfw", 132), ("cfb", 44)]
VC = {}
_o = 0
for _n, _k in VEC_FIELDS:
    VC[_n] = (_o, _k)
    _o += _k
NV = _o


def _cols(v, n):
    return np.ascontiguousarray(v.reshape(n, 128).T)


def pack_vec(inp, l):
    out = np.zeros((128, NV), np.float32)

    def put(name, arr):
        o, k = VC[name]
        out[:, o:o + k] = arr.reshape(128, k)

    put("gmix", _cols(inp["norm_mix"][l], 8))
    put("gffn", _cols(inp["norm_ffn"][l], 8))
    put("gpg", _cols(inp["ple_gate_norm"][l], 8))
    put("gple", _cols(inp["ple_norm"][l], 8))
    caw = inp["conv_a_w"][l]
    put("caw", np.stack([_cols(caw[k], 4) for k in range(4)], axis=1).reshape(128, 16))
    put("cab", _cols(inp["conv_a_b"][l], 4))
    put("lbr", _cols(inp["lru_b_r"][l], 4))
    put("lbi", _cols(inp["lru_b_i"][l], 4))
    put("lam", _cols(inp["lru_lambda"][l], 4))
    cbw = inp["conv_b_w"][l]
    put("cbw", np.stack([_cols(cbw[k], 4) for k in range(31)], axis=1).reshape(128, 124))
    put("cbb", _cols(inp["conv_b_b"][l], 4))
    put("cbn", _cols(inp["conv_b_norm"][l], 4))
    p = np.arange(128)
    put("dqn", inp["diff_q_norm"][l][p % 64])
    put("dkn", inp["diff_k_norm"][l][p % 64])
    put("sqn", inp["spa_q_norm"][l][p])
    put("skn", inp["spa_k_norm"][l][p])
    put("ikn", inp["idx_k_norm"][l][p % 32])
    put("sub", inp["diff_subln"][l][p])
    for nm, key in (("lq1", "diff_lq1"), ("lk1", "diff_lk1"), ("lq2", "diff_lq2"), ("lk2", "diff_lk2")):
        v = np.zeros(128, np.float32)
        v[:64] = inp[key][l]
        put(nm, v)
    cfw = inp["conv_f_w"][l]
    put("cfw", np.stack([_cols(cfw[k], 44) for k in range(3)], axis=1).reshape(128, 132))
    put("cfb", _cols(inp["conv_f_b"][l], 44))
    return out


def rope_inv(rot):
    return (np.float32(500000.0) ** (-np.arange(0, rot, 2, dtype=np.float32) / np.float32(rot))).astype(np.float32)


def make_consts(S):
    cm = np.zeros((9, 128, 128), np.float32)
    cm[0] = np.eye(128)
    cm[1] = 1.0 / 1024
    p = np.arange(128)
    cm[2] = (p[:, None] // 64 == p[None, :] // 64) / 64.0
    cm[3] = 1.0 / 128
    cm[4] = (p[:, None] // 32 == p[None, :] // 32) / 32.0
    cm[5] = 1.0 / 512
    rope = np.zeros((3, 2, 128, S), np.float32)
    t = np.arange(S, dtype=np.float32)
    for ci, hd in enumerate((64, 128, 32)):
        rot = hd // 4
        half = rot // 2
        inv = rope_inv(rot)
        ang = (t[:, None] * inv[None, :]).astype(np.float32)
        cos = np.cos(ang).astype(np.float32)
        sin = np.sin(ang).astype(np.float32)
        R = np.zeros((128, 128), np.float32)
        for q in range(128):
            d = q % hd
            if d < half:
                R[q, q + half] = -1.0
                rope[ci, 0, q] = cos[:, d]
                rope[ci, 1, q] = sin[:, d]
            elif d < 2 * half:
                R[q, q - half] = 1.0
                rope[ci, 0, q] = cos[:, d - half]
                rope[ci, 1, q] = sin[:, d - half]
            else:
                rope[ci, 0, q] = 1.0
        cm[6 + ci] = R.T
    return cm, rope


def lru_blockdiag(inp):
    L = inp["lru_w_r"].shape[0]
    out = np.zeros((L, 2, 4, 128, 128), np.float32)
    for l in range(L):
        for gi, key in enumerate(("lru_w_r", "lru_w_i")):
            w = inp[key][l]
            for ct in range(4):
                out[l, gi, ct, :64, :64] = w[2 * ct]
                out[l, gi, ct, 64:, 64:] = w[2 * ct + 1]
    return out


class B:
    def __init__(self, S, L, dbg=False, phases=None):
        self.S, self.L, self.dbg = S, L, dbg
        self.phases = phases
        nc = self.nc = bass.Bass("TRN2", target_bir_lowering=False)
        self.p = Prog(nc)
        self.NCH = S // 512
        dt = nc.dram_tensor

        def inp(name, shape, dtype=F32):
            return dt(name, list(shape), dtype, kind="ExternalInput").ap()

        self.xT = inp("xT", [D, S])
        self.pT = inp("pT", [L, PLE, S])
        self.vec = inp("vec", [L, 128, NV])
        self.cmat = inp("cmat", [9, 128, 128])
        self.rope = inp("rope", [3, 2, 128, S])
        self.lru = inp("lru", [L, 2, 4, 128, 128])
        self.w_in = inp("w_in", [L, D, DIN])
        self.w_gate = inp("w_gate", [L, 4, D, D])
        self.w_branch = inp("w_branch", [L, 4, DB, D])
        self.w_out = inp("w_out", [L, D, D])
        self.w_up = inp("w_up", [L, D, 2 * DFF])
        self.w_down = inp("w_down", [L, DFF, D])
        self.w_pg = inp("w_ple_gate", [L, D, D])
        self.w_pi = inp("w_ple_in", [L, PLE, D])
        self.outT = dt("outT", [D, S], F32, kind="ExternalOutput").ap()

        def scr(name, shape, dtype):
            kind = "ExternalOutput" if dbg else "Internal"
            return dt(name, list(shape), dtype, kind=kind).ap()

        self.H = scr("H", [D, S], F32)
        self.XN = scr("XN", [D, S], BF16)
        self.U16 = scr("U16", [2048, S], F32)
        self.CQ = scr("CQ", [512, S], BF16)
        self.CK = scr("CK", [512, S], BF16)
        self.CV = scr("CV", [S, 512], BF16)
        self.DQ = scr("DQ", [512, S], BF16)
        self.DK = scr("DK", [128, S], BF16)
        self.DV = scr("DV", [S, 128], BF16)
        self.IQ = scr("IQ", [256, S], BF16)
        self.IK = scr("IK", [32, S], BF16)
        self.IW = scr("IW", [S, 128], F32)
        self.OBR = scr("OBR", [4, 512, S], BF16)
        p = self.p
        n = self.NCH
        self.tH = p.toks(n, "H")
        self.tXN = p.toks(n, "XN")
        self.tU = p.toks(n, "U")
        self.tQK = p.toks(n, "QK")
        self.tOB = [p.toks(n, f"OB{j}_") for j in range(4)]
        self.es = ExitStack()
        self._uid = 0

    def sb(self, es, name, shape, dtype):
        self._uid += 1
        return es.enter_context(self.nc.sbuf_tensor(f"{name}_{self._uid}", list(shape), dtype))

    def ps(self, es, name, shape, dtype=F32):
        self._uid += 1
        return es.enter_context(self.nc.psum_tensor(f"{name}_{self._uid}", list(shape), dtype))

    def load_consts(self):
        p, nc = self.p, self.nc
        es = self.es
        self.cm = self.sb(es, "cm", [128, 9, 128], F32)
        self.t_cm = p.tok("cm")
        p.dma("sp", self.cm[:], self.cmat.rearrange("c p n -> p c n"), writes=[self.t_cm])
        self.ones_bf = self.sb(es, "ones_bf", [128, 128], BF16)
        self.t_ones = p.tok("ones")
        p.op("pool", lambda e: e.memset(self.ones_bf[:], 1.0), writes=[self.t_ones])
        self.ident_bf = self.sb(es, "ident_bf", [128, 128], BF16)
        self.t_identb = p.tok("identb")
        p.op("dve", lambda e: e.tensor_copy(out=self.ident_bf[:], in_=self.cm[:, 0, :]),
             reads=[self.t_cm], writes=[self.t_identb])
        self.vecs = self.sb(es, "vecs", [128, self.L, NV], F32)
        self.t_vec = p.tok("vec")
        p.dma("sp", self.vecs[:], self.vec.rearrange("l p n -> p l n"), writes=[self.t_vec])
        self.epsc = self.sb(es, "epsc", [128, 1], F32)
        self.t_eps = p.tok("eps")
        p.op("pool", lambda e: e.memset(self.epsc[:], EPS), writes=[self.t_eps])

    def freg(self, e, val):
        if not hasattr(self, "_fregs"):
            self._fregs = {}
        if val not in self._fregs:
            self._fregs[val] = e.to_reg(val)
        return self._fregs[val]

    def vcol(self, l, name, j=0, n=1):
        o, k = VC[name]
        return self.vecs[:, l, o + j:o + j + n]

    def load_weight(self, dst, dst_toks, src, K, N, stg, stg_toks, gain=None, col0=0, engs=("dve", "pool"),
                    rows=128):
        p = self.p
        for kc in range(K):
            i = self._wl % len(stg)
            e = engs[self._wl % len(engs)]
            self._wl += 1
            st, stt = stg[i], stg_toks[i]
            p.dma("sp", st[0:rows, 0:N], src[kc * rows:(kc + 1) * rows, :], writes=[stt], pool="w")
            o = dst[0:rows, kc, col0:col0 + N]
            if gain is not None:
                g = gain(kc)
                p.op(e, (lambda o=o, st=st, g=g: lambda en: en.tensor_scalar(
                    out=o, in0=st[0:rows, 0:N], scalar1=g, scalar2=1.0, op0=ALU.mult, op1=ALU.mult))(),
                    reads=[stt, self.t_vec], swrites=[dst_toks[kc]])
            else:
                p.op(e, (lambda o=o, st=st: lambda en: en.tensor_copy(out=o, in_=st[0:rows, 0:N]))(),
                     reads=[stt], swrites=[dst_toks[kc]])

    _wl = 0
    _cc = 0

    def phase1(self, l):
        p, nc, S = self.p, self.nc, self.S
        hin = self.xT if l == 0 else self.H
        cm = self.cm
        with ExitStack() as es:
            W = self.sb(es, "p1W", [128, 8, DIN], BF16)
            tW = p.toks(8, "p1W")
            Wdi = self.sb(es, "p1Wdi", [128, 8, 256], BF16)
            tWdi = p.tok("Wdi")
            with ExitStack() as es2:
                stg = [self.sb(es2, "p1stg", [128, DIN], F32) for _ in range(2)]
                tstg = p.toks(2, "stg")
                self.load_weight(W, tW, self.w_in[l], 8, DIN, stg, tstg,
                                 gain=lambda kc: self.vcol(l, "gmix", kc))
                p.op("pool", lambda e: e.memset(Wdi[:], 0.0), writes=[tWdi])
                for kc in range(8):
                    p.op("dve", lambda e, kc=kc: e.tensor_copy(out=Wdi[:, kc, 0:128], in_=W[:, kc, 4224:4352]),
                         reads=[tW[kc]], swrites=[tWdi])
                    p.op("dve", lambda e, kc=kc: e.tensor_copy(out=Wdi[:, kc, 128:136], in_=W[:, kc, 4640:4648]),
                         reads=[tW[kc]], swrites=[tWdi])
                p.barrier()
            hb = [self.sb(es, "hb", [128, 8, 512], F32) for _ in range(2)]
            thb = p.toks(2, "hb")
            rp = [self.sb(es, "rp", [128, 6, 512], F32) for _ in range(1)] * 2
            trp = [p.tok("rp")] * 2
            sq = [self.sb(es, "sq", [128, 512], F32) for _ in range(2)]
            tsq = p.toks(2, "sq")
            rstd = self.sb(es, "rstd", [128, 512], F32)
            trstd = p.tok("rstd")
            xn = [self.sb(es, "xn", [128, 8, 512], BF16) for _ in range(2)]
            txn = p.toks(2, "xn")
            raw = self.sb(es, "raw", [128, 8, 512], F32)
            traw = p.tok("raw")
            NE = 3
            ev = [self.sb(es, "ev", [128, 512], F32) for _ in range(NE)]
            tev = p.toks(NE, "ev")
            sq2 = [self.sb(es, "sq2", [128, 512], F32) for _ in range(NE)]
            tsq2 = p.toks(NE, "sq2")
            rs2 = [self.sb(es, "rs2", [128, 512], F32) for _ in range(NE)]
            trs2 = p.toks(NE, "rs2")
            xg = [self.sb(es, "xg", [128, 512], F32) for _ in range(NE)]
            txg = p.toks(NE, "xg")
            t1, tt1 = sq2, tsq2
            t2, tt2 = rs2, trs2
            ob = [self.sb(es, "ob", [128, 512], BF16) for _ in range(NE)]
            tob = p.toks(NE, "ob")
            tv = [self.sb(es, "tv", [128, 512], BF16)] * 2
            ttv = [p.tok("tv")] * 2
            tdv = [self.sb(es, "tdv", [128, 128], BF16) for _ in range(2)]
            ttdv = p.toks(2, "tdv")
            tiw = [self.sb(es, "tiw", [128, 128], F32) for _ in range(2)]
            ttiw = p.toks(2, "tiw")
            pst = self.ps(es, "pst", [128, 512])
            tpst = p.tok("pst")
            pm = [self.ps(es, "pm", [128, 512]) for _ in range(3)]
            tpm = p.toks(3, "pm")
            ps2 = self.ps(es, "ps2", [128, 512])
            tps2 = p.tok("ps2")
            ps3 = self.ps(es, "ps3", [128, 512])
            tps3 = p.tok("ps3")
            ptk = self.ps(es, "ptk", [128, 512])
            tptk = p.tok("ptk")
            ptd = self.ps(es, "ptd", [128, 256])
            tptd = p.tok("ptd")

            def loads(c):
                b = c % 2
                sl = slice(c * 512, (c + 1) * 512)
                rd = [self.tH[c]] if l > 0 else []
                p.dma("sp", hb[b][:], hin[:, sl].rearrange("(k p) t -> p k t", p=128), reads=rd, writes=[thb[b]])

            def load_rp(c):
                sl = slice(c * 512, (c + 1) * 512)
                p.dma("sp", rp[0][:], self.rope[:, :, :, sl].rearrange("c s p t -> p (c s) t"), writes=[trp[0]])

            qk_tiles = []
            for i in range(4):
                qk_tiles.append((16 + i, 128, 2, "dqn", 0, self.CQ, i * 128))
            for i in range(4):
                qk_tiles.append((20 + i, 128, 2, "dkn", 0, self.CK, i * 128))
            for i in range(4):
                qk_tiles.append((28 + i, 128, 3, "sqn", 1, self.DQ, i * 128))
            qk_tiles.append((32, 128, 3, "skn", 1, self.DK, 0))
            qk_tiles.append((34, 128, None, None, 2, self.IQ, 0))
            qk_tiles.append((35, 128, None, None, 2, self.IQ, 128))
            qk_tiles.append((36, 32, 4, "ikn", 2, self.IK, 0))

            loads(0)
            cnt = [0, 0]
            for c in range(self.NCH):
                b = c % 2
                sl = slice(c * 512, (c + 1) * 512)
                if c + 1 < self.NCH:
                    loads(c + 1)
                load_rp(c)
                hbb, xnb, rpb = hb[b], xn[b], rp[b]
                STOP = 9
                for kc in range(8):
                    si = kc % 2
                    p.op("act", lambda e, hbb=hbb, kc=kc, si=si: e.activation(out=sq[si][:], in_=hbb[:, kc, :],
                                                                              func=AF.Square),
                         reads=[thb[b]], writes=[tsq[si]])
                    p.op("pe", lambda e, kc=kc, si=si: e.matmul(pst[:], lhsT=cm[:, 1, :], rhs=sq[si][:],
                                                               start=(kc == 0), stop=(kc == 7)),
                         reads=[tsq[si], self.t_cm], writes=[tpst])
                p.op("act", lambda e: e.activation(out=rstd[:], in_=pst[:], func=AF.Sqrt, bias=self.epsc[:]),
                     reads=[tpst, self.t_eps], writes=[trstd])
                p.op("dve", lambda e: e.reciprocal(out=rstd[:], in_=rstd[:]), reads=[trstd], writes=[trstd])
                for kc in range(8):
                    eng = "dve" if kc % 2 == 0 else "pool"
                    p.op(eng, lambda e, kc=kc, hbb=hbb, xnb=xnb: e.tensor_tensor(
                        out=xnb[:, kc, :], in0=hbb[:, kc, :], in1=rstd[:], op=ALU.mult),
                        reads=[thb[b], trstd], swrites=[txn[b]])
                p.dma("sp", self.XN[:, sl].rearrange("(k p) t -> p k t", p=128), xnb[:],
                      reads=[txn[b]], writes=[self.tXN[c]], pool="st")

                def mm_fm(m, M, pidx):
                    c0 = m * 128
                    for kc in range(8):
                        p.op("pe", lambda e, kc=kc, c0=c0, M=M, pidx=pidx, xnb=xnb: e.matmul(
                            pm[pidx][0:M, :], lhsT=W[:, kc, c0:c0 + M], rhs=xnb[:, kc, :],
                            start=(kc == 0), stop=(kc == 7)),
                            reads=[tW[kc], txn[b]], writes=[tpm[pidx]])

                for m in range(16 if STOP > 2 else 0):
                    pidx = cnt[0] % 3
                    cnt[0] += 1
                    mm_fm(m, 128, pidx)
                    if m % 2 == 0:
                        p.op("act", lambda e, m=m, pidx=pidx: e.copy(out=raw[:, m % 8, :], in_=pm[pidx][:]),
                             reads=[tpm[pidx]], swrites=[traw])
                    else:
                        p.op("dve", lambda e, m=m, pidx=pidx: e.tensor_copy(out=raw[:, m % 8, :], in_=pm[pidx][:]),
                             reads=[tpm[pidx]], swrites=[traw])
                    if m % 8 == 7:
                        m0 = (m // 8) * 1024
                        p.dma("sp", self.U16[m0:m0 + 1024, sl].rearrange("(m p) t -> p m t", p=128), raw[:],
                              reads=[traw], swrites=[self.tU[c]], pool="st")
                nq = len(qk_tiles)
                pid = {}

                def stA(n):
                    (m, M, gi, gname, rc, dst, r0) = qk_tiles[n]
                    pidx = cnt[0] % 3
                    cnt[0] += 1
                    i = (cnt[1] + n) % NE
                    mm_fm(m, M, pidx)
                    p.op("act", lambda e, i=i, pidx=pidx, M=M: e.copy(out=ev[i][0:M, :], in_=pm[pidx][0:M, :]),
                         reads=[tpm[pidx]], writes=[tev[i]])
                    if gi is not None:
                        p.op("act", lambda e, i=i, pidx=pidx, M=M: e.activation(out=sq2[i][0:M, :],
                                                                               in_=pm[pidx][0:M, :], func=AF.Square),
                             reads=[tpm[pidx]], writes=[tsq2[i]])

                def stB(n):
                    (m, M, gi, gname, rc, dst, r0) = qk_tiles[n]
                    i = (cnt[1] + n) % NE
                    if gi is None:
                        return
                    p.op("pe", lambda e, i=i, M=M, gi=gi: e.matmul(ps2[0:M, :], lhsT=cm[0:M, gi, 0:M],
                                                                  rhs=sq2[i][0:M, :], start=True, stop=True),
                         reads=[tsq2[i], self.t_cm], writes=[tps2])
                    p.op("act", lambda e, i=i, M=M: e.activation(out=rs2[i][0:M, :], in_=ps2[0:M, :],
                                                                 func=AF.Sqrt, bias=self.epsc[0:M, :]),
                         reads=[tps2, self.t_eps], writes=[trs2[i]])
                    p.op("dve", lambda e, i=i, M=M: e.reciprocal(out=rs2[i][0:M, :], in_=rs2[i][0:M, :]),
                         reads=[trs2[i]], writes=[trs2[i]])
                    g = self.vcol(l, gname)[0:M, :]
                    p.op("dve", lambda e, i=i, M=M, g=g: e.scalar_tensor_tensor(
                        out=xg[i][0:M, :], in0=ev[i][0:M, :], scalar=g, in1=rs2[i][0:M, :],
                        op0=ALU.mult, op1=ALU.mult),
                        reads=[tev[i], trs2[i], self.t_vec], writes=[txg[i]])

                def stC(n):
                    (m, M, gi, gname, rc, dst, r0) = qk_tiles[n]
                    i = (cnt[1] + n) % NE
                    xs, txs = (xg[i], txg[i]) if gi is not None else (ev[i], tev[i])
                    p.op("pe", lambda e, xs=xs, M=M, rc=rc: e.matmul(ps3[0:M, :], lhsT=cm[0:M, 6 + rc, 0:M],
                                                                    rhs=xs[0:M, :], start=True, stop=True),
                         reads=[txs, self.t_cm], writes=[tps3])
                    p.op("pool", lambda e, i=i, xs=xs, M=M, rc=rc, rpb=rpb: e.tensor_tensor(
                        out=t1[i][0:M, :], in0=xs[0:M, :], in1=rpb[0:M, 2 * rc, :], op=ALU.mult),
                        reads=[txs, trp[b]], writes=[tt1[i]])
                    p.op("dve", lambda e, i=i, M=M, rc=rc, rpb=rpb: e.tensor_tensor(
                        out=t2[i][0:M, :], in0=ps3[0:M, :], in1=rpb[0:M, 2 * rc + 1, :], op=ALU.mult),
                        reads=[tps3, trp[b]], writes=[tt2[i]])
                    p.op("pool", lambda e, i=i, M=M: e.tensor_tensor(
                        out=ob[i][0:M, :], in0=t1[i][0:M, :], in1=t2[i][0:M, :], op=ALU.add),
                        reads=[tt1[i], tt2[i]], writes=[tob[i]])
                    p.dma("sp", dst[r0:r0 + M, sl], ob[i][0:M, :], reads=[tob[i]], swrites=[self.tQK[c]], pool="st")

                for step in range(nq + 2):
                    if step < nq:
                        stA(step)
                    if 0 <= step - 1 < nq:
                        stB(step - 1)
                    if 0 <= step - 2 < nq:
                        stC(step - 2)
                cnt[1] += nq
                for j in range(4 if STOP > 4 else 0):
                    jb = j % 2
                    ts = slice(j * 128, (j + 1) * 128)
                    r0 = c * 512 + j * 128
                    for kc in range(8):
                        p.op("pe", lambda e, kc=kc, ts=ts, xnb=xnb: e.matmul(
                            ptk[:], lhsT=xnb[:, kc, ts], rhs=W[:, kc, 3072:3584], start=(kc == 0), stop=(kc == 7)),
                            reads=[tW[kc], txn[b]], writes=[tptk])
                    for kc in range(8):
                        p.op("pe", lambda e, kc=kc, ts=ts, xnb=xnb: e.matmul(
                            ptd[:, 0:256], lhsT=xnb[:, kc, ts], rhs=Wdi[:, kc, :], start=(kc == 0),
                            stop=(kc == 7)), reads=[tWdi, txn[b]], writes=[tptd])
                    p.op("act", lambda e, jb=jb: e.copy(out=tv[jb][:], in_=ptk[:]), reads=[tptk], writes=[ttv[jb]])
                    p.op("dve", lambda e, jb=jb: e.tensor_copy(out=tdv[jb][:], in_=ptd[:, 0:128]),
                         reads=[tptd], writes=[ttdv[jb]])
                    p.op("dve", lambda e, jb=jb: e.tensor_scalar(out=tiw[jb][:], in0=ptd[:, 128:256], scalar1=1.0 / 16.0,
                                                                 scalar2=None, op0=ALU.mult),
                         reads=[tptd], writes=[ttiw[jb]])
                    p.dma("sp", self.CV[r0:r0 + 128, :], tv[jb][:], reads=[ttv[jb]], swrites=[self.tQK[c]], pool="st")
                    p.dma("sp", self.DV[r0:r0 + 128, :], tdv[jb][:], reads=[ttdv[jb]], swrites=[self.tQK[c]],
                          pool="st")
                    p.dma("sp", self.IW[r0:r0 + 128, :], tiw[jb][:], reads=[ttiw[jb]], swrites=[self.tQK[c]],
                          pool="st")
            p.barrier()

    def build(self):
        nc = self.nc
        with nc.allow_low_precision("bf16 matmul operands, fp32 accumulation"):
            with self.es:
                self.load_consts()
                ph = self.phases
                for l in range(self.L):
                    if ph is None or "p1" in ph:
                        self.phase1(l)
                    if ph is None or "a" in ph or "b" in ph:
                        self.phaseAB(l)
                    if ph is None or "c" in ph:
                        self.phaseC(l)
                    if ph is None or "d" in ph:
                        self.phaseD(l)
                    if ph is None or "p3" in ph:
                        self.phase3(l)
                    if ph is None or "p4" in ph:
                        self.phase4(l)
                self.p.build()
        return nc


def host_inputs(inp, S, L):
    cm, rope = make_consts(S)
    vec = np.stack([pack_vec(inp, l) for l in range(L)])
    lru = lru_blockdiag(inp)
    common = dict(vec=vec, cmat=cm, rope=rope, lru=lru[:L])
    for k_dev, k_in in (("w_in", "w_in"), ("w_gate", "w_gate"), ("w_branch", "w_branch"), ("w_out", "w_out"),
                        ("w_up", "w_up"), ("w_down", "w_down"), ("w_ple_gate", "w_ple_gate"),
                        ("w_ple_in", "w_ple_in")):
        common[k_dev] = np.ascontiguousarray(inp[k_in][:L], dtype=np.float32)
    maps = []
    nb = inp["x"].shape[0]
    for b in range(nb):
        m = dict(common)
        m["xT"] = np.ascontiguousarray(inp["x"][b].T)
        m["pT"] = np.ascontiguousarray(np.transpose(inp["p"][:L, b], (0, 2, 1)))
        maps.append(m)
    return maps


def kernel(**inputs):
    inp = {k: np.asarray(v) for k, v in inputs.items()}
    nb, S, _ = inp["x"].shape
    L = inp["w_in"].shape[0]
    bld = B(S, L)
    nc = bld.build()
    maps = host_inputs(inp, S, L)
    res = run_bass_kernel_spmd(nc, maps, core_ids=list(range(nb)))
    out = np.stack([np.ascontiguousarray(res.results[b]["outT"].T) for b in range(nb)])
    return out.astype(np.float32)


def genA(self, l, es):
    p, nc, S = self.p, self.nc, self.S
    TC = 1024
    NT = S // TC
    if True:
        lst = self.sb(es, "lrust", [128, 8, 128], F32)
        tlst = p.tok("lrust")
        lw = self.sb(es, "lruw", [128, 8, 128], BF16)
        tlw = p.tok("lruw")
        p.dma("sp", lst[:], self.lru[l].rearrange("g c p n -> p (g c) n"), writes=[tlst])
        p.op("dve", lambda e: e.tensor_copy(out=lw[:], in_=lst[:]), reads=[tlst], writes=[tlw])
        cc = self.sb(es, "lruc", [128, 12], F32)
        tcc = p.tok("lruc")
        onec = self.sb(es, "onec", [128, 1], F32)
        tone = p.tok("onec")
        p.op("pool", lambda e: e.memset(onec[:], 1.0 + 2.0 ** -23), writes=[tone])
        lam = self.vcol(l, "lam", 0, 4)
        p.op("act", lambda e: e.activation(out=cc[:, 0:4], in_=lam, func=AF.Exp, scale=-1.0),
             reads=[self.t_vec], writes=[tcc])
        p.op("act", lambda e: e.activation(out=cc[:, 0:4], in_=cc[:, 0:4], func=AF.Ln, bias=1.0),
             reads=[tcc], writes=[tcc])
        p.op("dve", lambda e: e.tensor_scalar(out=cc[:, 4:8], in0=cc[:, 0:4], scalar1=-8.0, scalar2=None,
                                              op0=ALU.mult), reads=[tcc], writes=[tcc])
        p.op("dve", lambda e: e.tensor_scalar(out=cc[:, 8:12], in0=cc[:, 0:4], scalar1=-16.0, scalar2=None,
                                              op0=ALU.mult), reads=[tcc], writes=[tcc])

        def f32buf(name, n=TC):
            return self.sb(es, name, [128, n], F32), p.tok(name)

        xa, txa = f32buf("xa", TC + 3)
        xc, txc = f32buf("xc")
        xcb = self.sb(es, "xcb", [128, TC], BF16)
        txcb = p.tok("xcb")
        r, tr = f32buf("r")
        ig, tig = f32buf("ig")
        a, ta = f32buf("a")
        dr, tdr = f32buf("dr")
        gx, tgx = f32buf("gx")
        h, th = f32buf("h")
        ag, tag = f32buf("ag")
        tg, ttg = f32buf("tg")
        sg, tsg = f32buf("sg")
        ob = self.sb(es, "oba", [128, TC], BF16)
        tob = p.tok("oba")
        hst = self.sb(es, "hst", [128, 1], F32)
        thst = p.tok("hst")
        pr = self.ps(es, "pr", [128, TC])
        tpr = p.tok("pr")
        pi = self.ps(es, "pi", [128, TC])
        tpi = p.tok("pi")
        nsub = TC // 512
        for ct in range(4):
            rows = slice(ct * 128, (ct + 1) * 128)
            p.op("pool", lambda e: e.memset(hst[:], 0.0), writes=[thst])
            for t in range(NT):
                t0 = t * TC
                chs = list(range(t0 // 512, (t0 + TC) // 512))
                rd = [self.tU[c] for c in chs]
                if t == 0:
                    p.op("pool", lambda e: e.memset(xa[:, 0:3], 0.0), writes=[txa])
                    p.dma("sp", xa[:, 3:TC + 3], self.U16[rows, 0:TC], reads=rd, swrites=[txa])
                else:
                    p.dma("sp", xa[:, :], self.U16[rows, t0 - 3:t0 + TC], reads=rd + [self.tU[chs[0] - 1]],
                          writes=[txa])
                p.dma("sp", ag[:], self.U16[512 + ct * 128:512 + (ct + 1) * 128, t0:t0 + TC], reads=rd, writes=[tag])
                w = lambda k: self.vcol(l, "caw", k * 4 + ct)
                p.op("dve", lambda e, w0=w(0), bb=self.vcol(l, "cab", ct): e.tensor_scalar(
                    out=xc[:], in0=xa[:, 0:TC], scalar1=w0, scalar2=bb, op0=ALU.mult, op1=ALU.add),
                    reads=[txa, self.t_vec], writes=[txc])
                for k in range(1, 4):
                    p.op("dve", lambda e, k=k, wk=w(k): e.scalar_tensor_tensor(
                        out=xc[:], in0=xa[:, k:k + TC], scalar=wk, in1=xc[:], op0=ALU.mult, op1=ALU.add),
                        reads=[txa, txc, self.t_vec], writes=[txc])
                p.op("pool", lambda e: e.tensor_copy(out=xcb[:], in_=xc[:]), reads=[txc], writes=[txcb])
                for s in range(nsub):
                    ss = slice(s * 512, (s + 1) * 512)
                    p.op("pe", lambda e, ss=ss, ct=ct: e.matmul(pr[:, ss], lhsT=lw[:, ct, :], rhs=xcb[:, ss],
                                                               start=True, stop=True),
                         reads=[tlw, txcb], swrites=[tpr])
                    p.op("pe", lambda e, ss=ss, ct=ct: e.matmul(pi[:, ss], lhsT=lw[:, 4 + ct, :], rhs=xcb[:, ss],
                                                               start=True, stop=True),
                         reads=[tlw, txcb], swrites=[tpi])
                p.op("act", lambda e, bb=self.vcol(l, "lbr", ct): e.activation(out=r[:], in_=pr[:], func=AF.Sigmoid,
                                                                              bias=bb),
                     reads=[tpr, self.t_vec], writes=[tr])
                p.op("act", lambda e, bb=self.vcol(l, "lbi", ct): e.activation(out=ig[:], in_=pi[:], func=AF.Sigmoid,
                                                                              bias=bb),
                     reads=[tpi, self.t_vec], writes=[tig])
                p.op("act", lambda e, ct=ct: e.activation(out=a[:], in_=r[:], func=AF.Exp, scale=cc[:, 4 + ct:5 + ct]),
                     reads=[tr, tcc], writes=[ta])
                p.op("act", lambda e, ct=ct: e.activation(out=dr[:], in_=r[:], func=AF.Exp,
                                                          scale=cc[:, 8 + ct:9 + ct]),
                     reads=[tr, tcc], writes=[tdr])
                p.op("act", lambda e: e.activation(out=dr[:], in_=dr[:], func=AF.Sqrt, scale=-1.0, bias=onec[:]),
                     reads=[tdr, tone], writes=[tdr])
                p.op("pool", lambda e: e.tensor_tensor(out=gx[:], in0=ig[:], in1=xc[:], op=ALU.mult),
                     reads=[tig, txc], writes=[tgx])
                p.op("pool", lambda e: e.tensor_tensor(out=gx[:], in0=gx[:], in1=dr[:], op=ALU.mult),
                     reads=[tgx, tdr], writes=[tgx])
                p.op("dve", lambda e: e.tensor_tensor_scan(out=h[:], data0=a[:], data1=gx[:], initial=hst[:],
                                                           op0=ALU.mult, op1=ALU.add),
                     reads=[ta, tgx, thst], writes=[th])
                p.op("act", lambda e: e.copy(out=hst[:], in_=h[:, TC - 1:TC]), reads=[th], writes=[thst])
                p.op("pool", lambda e: e.tensor_tensor(out=tg[:], in0=ag[:], in1=ag[:], op=ALU.mult),
                     reads=[tag], writes=[ttg])
                p.op("pool", lambda e: e.tensor_scalar(out=tg[:], in0=tg[:], scalar1=0.044715, scalar2=1.0,
                                                       op0=ALU.mult, op1=ALU.add), reads=[ttg], writes=[ttg])
                p.op("pool", lambda e: e.tensor_tensor(out=tg[:], in0=tg[:], in1=ag[:], op=ALU.mult),
                     reads=[ttg, tag], writes=[ttg])
                p.op("act", lambda e: e.activation(out=sg[:], in_=tg[:], func=AF.Sigmoid,
                                                   scale=2.0 * math.sqrt(2.0 / math.pi)),
                     reads=[ttg], writes=[tsg])
                p.op("pool", lambda e: e.tensor_tensor(out=sg[:], in0=sg[:], in1=ag[:], op=ALU.mult),
                     reads=[tsg, tag], writes=[tsg])
                p.op("dve", lambda e: e.tensor_tensor(out=ob[:], in0=h[:], in1=sg[:], op=ALU.mult),
                     reads=[th, tsg], writes=[tob])
                for c in chs:
                    o = c * 512 - t0
                    p.dma("sp", self.OBR[0, rows, c * 512:(c + 1) * 512], ob[:, o:o + 512], reads=[tob],
                          swrites=[self.tOB[0][c]], pool="st")
                yield


B.genA = genA


def genB(self, l, es):
    p, nc, S = self.p, self.nc, self.S
    TC = 1024
    NT = S // TC
    KC = 31
    if True:
        vb = [self.sb(es, "vb", [128, TC], F32) for _ in range(2)]
        tvb = p.toks(2, "vb")
        gb = [self.sb(es, "gb", [128, TC], F32) for _ in range(2)]
        tgb = p.toks(2, "gb")
        y = [self.sb(es, "y", [128, TC + KC - 1], F32) for _ in range(4)]
        ty = p.toks(4, "y")
        acc = [self.sb(es, "acc", [128, TC], F32) for _ in range(4)]
        tacc = p.toks(4, "acc")
        sqb = [self.sb(es, "sqb", [128, TC], F32) for _ in range(2)]
        tsqb = p.toks(2, "sqb")
        rs = self.sb(es, "rsb", [128, TC], F32)
        trs = p.tok("rsb")
        zb = [self.sb(es, "zb", [128, TC], F32) for _ in range(2)]
        tzb = p.toks(2, "zb")
        ob = [self.sb(es, "obb", [128, TC], BF16) for _ in range(2)]
        tob = p.toks(2, "obb")
        pn = self.ps(es, "pn", [128, TC])
        tpn = p.tok("pn")
        nsub = TC // 512
        for ct in range(4):
            p.op("pool", lambda e, ct=ct: e.memset(y[ct][:, 0:KC - 1], 0.0), writes=[ty[ct]])
        for t in range(NT):
            t0 = t * TC
            chs = list(range(t0 // 512, (t0 + TC) // 512))
            rd = [self.tU[c] for c in chs]
            for ct in range(4):
                b = ct % 2
                p.dma("sp", vb[b][:], self.U16[1024 + ct * 128:1024 + (ct + 1) * 128, t0:t0 + TC], reads=rd,
                      writes=[tvb[b]])
                p.dma("sp", gb[b][:], self.U16[1536 + ct * 128:1536 + (ct + 1) * 128, t0:t0 + TC], reads=rd,
                      writes=[tgb[b]])
                p.op("act", lambda e, b=b: e.activation(out=gb[b][:], in_=gb[b][:], func=AF.Sigmoid),
                     reads=[tgb[b]], writes=[tgb[b]])
                if t > 0:
                    p.op("act", lambda e, ct=ct: e.copy(out=y[ct][:, 0:KC - 1], in_=y[ct][:, TC:TC + KC - 1]),
                         reads=[ty[ct]], writes=[ty[ct]])
                p.op("pool", lambda e, ct=ct, b=b: e.tensor_tensor(out=y[ct][:, KC - 1:KC - 1 + TC], in0=vb[b][:],
                                                                  in1=gb[b][:], op=ALU.mult),
                     reads=[tvb[b], tgb[b], ty[ct]], writes=[ty[ct]])
                w = lambda k: self.vcol(l, "cbw", k * 4 + ct)
                p.op("dve", lambda e, ct=ct, w0=w(0), bb=self.vcol(l, "cbb", ct): e.tensor_scalar(
                    out=acc[ct][:], in0=y[ct][:, 0:TC], scalar1=w0, scalar2=bb, op0=ALU.mult, op1=ALU.add),
                    reads=[ty[ct], self.t_vec], writes=[tacc[ct]])
                for k in range(1, KC):
                    p.op("dve", lambda e, ct=ct, k=k, wk=w(k): e.scalar_tensor_tensor(
                        out=acc[ct][:], in0=y[ct][:, k:k + TC], scalar=wk, in1=acc[ct][:], op0=ALU.mult,
                        op1=ALU.add), reads=[ty[ct], tacc[ct], self.t_vec], writes=[tacc[ct]])
                p.op("act", lambda e, ct=ct, b=b: e.activation(out=sqb[b][:], in_=acc[ct][:], func=AF.Square),
                     reads=[tacc[ct]], writes=[tsqb[b]])
                for s in range(nsub):
                    ss = slice(s * 512, (s + 1) * 512)
                    p.op("pe", lambda e, ss=ss, b=b, ct=ct: e.matmul(pn[:, ss], lhsT=self.cm[:, 5, :],
                                                                    rhs=sqb[b][:, ss], start=(ct == 0),
                                                                    stop=(ct == 3)),
                         reads=[tsqb[b], self.t_cm], swrites=[tpn])
                yield
            p.op("act", lambda e: e.activation(out=rs[:], in_=pn[:], func=AF.Sqrt, bias=self.epsc[:]),
                 reads=[tpn, self.t_eps], writes=[trs])
            p.op("dve", lambda e: e.reciprocal(out=rs[:], in_=rs[:]), reads=[trs], writes=[trs])
            for ct in range(4):
                b = ct % 2
                p.op("dve", lambda e, ct=ct, b=b, g=self.vcol(l, "cbn", ct): e.scalar_tensor_tensor(
                    out=zb[b][:], in0=acc[ct][:], scalar=g, in1=rs[:], op0=ALU.mult, op1=ALU.mult),
                    reads=[tacc[ct], trs, self.t_vec], writes=[tzb[b]])
                p.op("act", lambda e, b=b: e.activation(out=ob[b][:], in_=zb[b][:], func=AF.Silu),
                     reads=[tzb[b]], writes=[tob[b]])
                for c in chs:
                    o = c * 512 - t0
                    p.dma("sp", self.OBR[1, ct * 128:(ct + 1) * 128, c * 512:(c + 1) * 512], ob[b][:, o:o + 512],
                          reads=[tob[b]], swrites=[self.tOB[1][c]], pool="st")
            yield


B.genB = genB


def phaseAB(self, l):
    p = self.p
    NT = self.S // 1024
    with ExitStack() as es:
        gens = [(self.genA(l, es), 4 * NT), (self.genB(l, es), 5 * NT)]
        prog = [0] * len(gens)
        alive = [True] * len(gens)
        while any(alive):
            i = min((k for k in range(len(gens)) if alive[k]), key=lambda k: prog[k] / gens[k][1])
            try:
                next(gens[i][0])
                prog[i] += 1
            except StopIteration:
                alive[i] = False
    p.barrier()


B.phaseAB = phaseAB


def phaseC(self, l):
    p, nc, S = self.p, self.nc, self.S
    NQ = S // 512
    NKT = S // 128
    lam_init = 0.8 - 0.6 * math.exp(-0.3 * l)
    with ExitStack() as es:
        pr16 = self.sb(es, "pr16", [128, 16], F32)
        tpr16 = p.tok("pr16")
        lamt = self.sb(es, "lamt", [128, 4], F32)
        tlam = p.tok("lamt")
        sub2 = self.sb(es, "sub2", [128, 1], F32)
        NP = 2
        pss = [self.ps(es, "pss", [128, 1024]) for _ in range(NP)]
        tpss = p.toks(NP, "pss")
        pO = [self.ps(es, "pO", [128, 512]) for _ in range(2)]
        tpO = p.toks(2, "pO")
        pL = [self.ps(es, "pL", [128, 512]) for _ in range(2)]
        tpL = p.toks(2, "pL")
        pstat = pL[1][:, :]
        tpstat = tpL[1]
        p.op("pool", lambda e: e.memset(pr16[:], 0.0), writes=[tpr16])
        p.op("dve", lambda e: e.tensor_tensor(out=pr16[:, 0:1], in0=self.vcol(l, "lq1"), in1=self.vcol(l, "lk1"),
                                              op=ALU.mult), reads=[self.t_vec], swrites=[tpr16])
        p.op("dve", lambda e: e.tensor_tensor(out=pr16[:, 1:2], in0=self.vcol(l, "lq2"), in1=self.vcol(l, "lk2"),
                                              op=ALU.mult), reads=[self.t_vec], swrites=[tpr16])
        p.op("pe", lambda e: e.matmul(pstat[:, 0:16], lhsT=self.cm[:, 3, :], rhs=pr16[:], start=True, stop=True),
             reads=[tpr16, self.t_cm], writes=[tpstat])
        p.op("act", lambda e: e.activation(out=lamt[:, 0:2], in_=pstat[:, 0:2], func=AF.Exp, scale=128.0),
             reads=[tpstat], writes=[tlam])
        p.op("dve", lambda e: e.tensor_tensor(out=lamt[:, 2:3], in0=lamt[:, 1:2], in1=lamt[:, 0:1], op=ALU.subtract),
             reads=[tlam], writes=[tlam])
        p.op("dve", lambda e: e.tensor_scalar(out=lamt[:, 2:3], in0=lamt[:, 2:3], scalar1=-lam_init, scalar2=None,
                                              op0=ALU.add), reads=[tlam], writes=[tlam])
        p.op("dve", lambda e: e.tensor_scalar(out=lamt[:, 3:4], in0=self.vcol(l, "sub"), scalar1=1.0 - lam_init,
                                              scalar2=None, op0=ALU.mult), reads=[tlam, self.t_vec], writes=[tlam])
        kT = [self.sb(es, "kT", [128, S], BF16) for _ in range(2)]
        qT = [self.sb(es, "qT", [128, S], BF16) for _ in range(2)]
        vT = [self.sb(es, "vT", [128, NKT, 128], BF16) for _ in range(2)]
        thd = p.toks(2, "hd")
        pt = [self.sb(es, "pt", [128, 1024], BF16) for _ in range(3)]
        tpt = p.toks(3, "pt")
        rl = [self.sb(es, "rl", [128, 512], F32) for _ in range(2)]
        trl = p.toks(2, "rl")
        on = [self.sb(es, "on", [128, 512], F32) for _ in range(2)]
        ton = p.toks(2, "on")
        o = self.sb(es, "oc", [128, 512], F32)
        to = p.tok("oc")
        sq = self.sb(es, "sqc", [128, 512], F32)
        tsq = p.tok("sqc")
        rs = self.sb(es, "rsc", [128, 512], F32)
        trs = p.tok("rsc")
        ob = [self.sb(es, "obc", [128, 512], BF16) for _ in range(2)]
        tob = p.toks(2, "obc")
        allqk = list(self.tQK)
        mb = self.sb(es, "maskb", [128, 4, 512], BF16)
        tmb = p.tok("maskb")
        p.op("pool", lambda e: e.memset(mb[:], 0.0), writes=[tmb])
        for j in range(4):
            p.op("pool", lambda e, j=j: e.affine_select(
                out=mb[:, j, :], in_=mb[:, j, :], pattern=[[1, 512]], compare_op=ALU.is_ge,
                fill=self.freg(e, -30000.0), base=-128 * j, channel_multiplier=-1), reads=[tmb], writes=[tmb])

        def load_head(hd):
            b = hd % 2
            rows = slice(hd * 128, (hd + 1) * 128)
            p.dma("sp", kT[b][:], self.CK[rows, :], reads=allqk, writes=[thd[b]])
            p.dma("sp", qT[b][:], self.CQ[rows, :], reads=allqk, swrites=[thd[b]])
            p.dma("sp", vT[b][:], self.CV[:, rows].rearrange("(t p) d -> p t d", p=128), reads=allqk,
                  swrites=[thd[b]])

        load_head(0)
        for hd in range(4):
            b = hd % 2
            if hd + 1 < 4:
                load_head(hd + 1)
            units = [(qc, c, kp) for qc in range(NQ) for c in range(2) for kp in range(2 * qc + 2)]
            base = self._cc
            self._cc += len(units)

            def emit_S(u, idx):
                qc, c, kp = u
                i = idx % NP
                ds = slice(c * 64, (c + 1) * 64)
                qs = slice(qc * 512, (qc + 1) * 512)
                for t in range(2):
                    kt = 2 * kp + t
                    j = kt - 4 * qc
                    p.op("pe", lambda e, i=i, b=b, ds=ds, kt=kt, qs=qs, t=t, j=j: e.matmul(
                        pss[i][:, t * 512:(t + 1) * 512], lhsT=kT[b][ds, kt * 128:(kt + 1) * 128], rhs=qT[b][ds, qs],
                        start=True, stop=(j < 0)), reads=[thd[b]], swrites=[tpss[i]])
                    if j >= 0:
                        p.op("pe", lambda e, i=i, t=t, j=j: e.matmul(
                            pss[i][:, t * 512:(t + 1) * 512], lhsT=self.ident_bf[:], rhs=mb[:, j, :],
                            start=False, stop=True), reads=[tmb, self.t_identb], swrites=[tpss[i]])

            emit_S(units[0], base)
            for n, u in enumerate(units):
                qc, c, kp = u
                idx = base + n
                i = idx % NP
                ip = idx % 3
                nk = 4 * qc + 4
                qs = slice(qc * 512, (qc + 1) * 512)
                p.op("act", lambda e, i=i, ip=ip: e.activation(out=pt[ip][:], in_=pss[i][:], func=AF.Exp, scale=0.125),
                     reads=[tpss[i]], writes=[tpt[ip]])
                if n + 1 < len(units):
                    emit_S(units[n + 1], idx + 1)
                for t in range(2):
                    kt = 2 * kp + t
                    p.op("pe", lambda e, ip=ip, b=b, kt=kt, c=c, nk=nk, t=t: e.matmul(
                        pO[c][:], lhsT=vT[b][:, kt, :], rhs=pt[ip][:, t * 512:(t + 1) * 512], start=(kt == 0),
                        stop=(kt == nk - 1)), reads=[thd[b], tpt[ip]], writes=[tpO[c]])
                    p.op("pe", lambda e, ip=ip, kt=kt, c=c, nk=nk, t=t: e.matmul(
                        pL[c][:], lhsT=self.ones_bf[:], rhs=pt[ip][:, t * 512:(t + 1) * 512], start=(kt == 0),
                        stop=(kt == nk - 1)), reads=[self.t_ones, tpt[ip]], writes=[tpL[c]])
                if 2 * kp + 1 == nk - 1:
                    p.op("dve", lambda e, c=c: e.reciprocal(out=rl[c][:], in_=pL[c][:]), reads=[tpL[c]],
                         writes=[trl[c]])
                    p.op("dve", lambda e, c=c: e.tensor_tensor(out=on[c][:], in0=pO[c][:], in1=rl[c][:], op=ALU.mult),
                         reads=[tpO[c], trl[c]], writes=[ton[c]])
                    if c == 1:
                        ib = (hd * NQ + qc) % 2
                        p.op("dve", lambda e: e.scalar_tensor_tensor(out=o[:], in0=on[1][:], scalar=lamt[:, 2:3],
                                                                     in1=on[0][:], op0=ALU.mult, op1=ALU.add),
                             reads=[ton[0], ton[1], tlam], writes=[to])
                        p.op("pool", lambda e: e.tensor_tensor(out=sq[:], in0=o[:], in1=o[:], op=ALU.mult), reads=[to],
                             writes=[tsq])
                        p.op("pe", lambda e: e.matmul(pstat, lhsT=self.cm[:, 3, :], rhs=sq[:], start=True,
                                                      stop=True), reads=[tsq, self.t_cm], writes=[tpstat])
                        p.op("act", lambda e: e.activation(out=rs[:], in_=pstat, func=AF.Sqrt, bias=self.epsc[:]),
                             reads=[tpstat, self.t_eps], writes=[trs])
                        p.op("dve", lambda e: e.reciprocal(out=rs[:], in_=rs[:]), reads=[trs], writes=[trs])
                        p.op("dve", lambda e, ib=ib: e.scalar_tensor_tensor(out=ob[ib][:], in0=o[:],
                                                                           scalar=lamt[:, 3:4], in1=rs[:],
                                                                           op0=ALU.mult, op1=ALU.mult),
                             reads=[to, trs, tlam], writes=[tob[ib]])
                        p.dma("sp", self.OBR[2, hd * 128:(hd + 1) * 128, qs], ob[ib][:], reads=[tob[ib]],
                              swrites=[self.tOB[2][qc]], pool="st")
    p.barrier()


B.phaseC = phaseC


def phaseD(self, l):
    p, nc, S = self.p, self.nc, self.S
    NQT = S // 128
    NKT = S // 128
    NEG = -1.0e30
    with ExitStack() as es:
        ik4 = self.sb(es, "ik4", [128, S], BF16)
        iqt = [self.sb(es, "iqt", [128, 3, 128], BF16) for _ in range(2)]
        tiqt = p.toks(2, "iqt")
        dkT = self.sb(es, "dkT", [128, S], BF16)
        dvT = self.sb(es, "dvT", [128, NKT, 128], BF16)
        tres = p.tok("dres")
        allqk = list(self.tQK)
        for g in range(3):
            p.dma("sp", ik4[g * 32:(g + 1) * 32, :], self.IK[:, :], reads=allqk, swrites=[tres])
        p.dma("sp", dkT[:], self.DK[:, :], reads=allqk, swrites=[tres])
        p.dma("sp", dvT[:], self.DV.rearrange("(t p) d -> p t d", p=128), reads=allqk, swrites=[tres])
        SC = [self.sb(es, "SC", [128, S], F32) for _ in range(2)]
        tSC = p.toks(2, "SC")
        MK = [self.sb(es, "MK", [128, S], BF16) for _ in range(2)]
        tMK = p.toks(2, "MK")
        wq = [self.sb(es, "wq", [128, 128], F32) for _ in range(2)]
        twq = p.toks(2, "wq")
        dg = [self.sb(es, "dg", [128, 8, 128], BF16) for _ in range(2)]
        tdg = p.toks(2, "dg")
        T = [self.sb(es, "T", [128, 2, 512], BF16) for _ in range(2)]
        tT = p.toks(2, "T")
        st = [self.sb(es, "bst", [128, 8], F32) for _ in range(2)]
        tst = p.toks(2, "bst")
        W2 = [self.sb(es, "W2", [128, NBIS], F32) for _ in range(2)]
        tW2 = p.toks(2, "W2")
        ctab = self.sb(es, "ctab", [128, NBIS], F32)
        tctab = p.tok("ctab")
        for it in range(NBIS):
            p.op("pool", lambda e, it=it: e.memset(ctab[:, it:it + 1], 0.5 ** (it + 1)), swrites=[tctab])
        mTs = [self.sb(es, "mTs", [128, 128], BF16) for _ in range(2)]
        tmTs = p.toks(2, "mTs")
        dq = [self.sb(es, "dq", [128, 4, 128], BF16) for _ in range(2)]
        tdq = p.toks(2, "dq")
        E = [self.sb(es, "E", [128, 4, 128], BF16) for _ in range(2)]
        tE = p.toks(2, "E")
        P = [self.sb(es, "P", [128, 4, 128], BF16) for _ in range(2)]
        tP = p.toks(2, "P")
        rl = self.sb(es, "rld", [128, 512], F32)
        trl = p.tok("rld")
        od = [self.sb(es, "od", [128, 4, 128], BF16) for _ in range(2)]
        tod = p.toks(2, "od")
        pl = [self.ps(es, "pl", [128, 512]) for _ in range(2)]
        tpl = p.toks(2, "pl")
        pl1 = [self.ps(es, "pl1", [128, 512]) for _ in range(2)]
        tpl1 = p.toks(2, "pl1")
        psc = self.ps(es, "psc", [128, 512])
        tpsc = p.tok("psc")
        pmT = self.ps(es, "pmT", [128, 1024], BF16)
        tpmT = p.tok("pmT")
        mbT = self.sb(es, "mbT", [128, S], BF16)
        tmbT = p.tok("mbT")
        pO = self.ps(es, "pOd", [128, 512])
        tpO = p.tok("pOd")
        pL = self.ps(es, "pLd", [128, 512])
        tpL = p.tok("pLd")
        scale = 128.0 ** -0.5
        ctr = [0, 0]

        def part1(qt):
            qb = qt % 2
            qs = slice(qt * 128, (qt + 1) * 128)
            L = 128 * (qt + 1)
            nkc = (L + 511) // 512
            p.dma("sp", wq[qb][:], self.IW[qs, :], reads=[self.tQK[qt // 4]], writes=[twq[qb]])
            p.dma("sp", iqt[qb][0:96, 0, :], self.IQ[0:96, qs], reads=[self.tQK[qt // 4]], writes=[tiqt[qb]])
            p.dma("sp", iqt[qb][0:96, 1, :], self.IQ[96:192, qs], reads=[self.tQK[qt // 4]], swrites=[tiqt[qb]])
            p.dma("sp", iqt[qb][0:64, 2, :], self.IQ[192:256, qs], reads=[self.tQK[qt // 4]], swrites=[tiqt[qb]])
            for h in range(8):
                p.op("pool", lambda e, h=h, qb=qb: e.tensor_scalar(out=dg[qb][:, h, :], in0=self.ident_bf[:],
                                                                   scalar1=wq[qb][:, h:h + 1], scalar2=1.0,
                                                                   op0=ALU.mult, op1=ALU.mult),
                     reads=[twq[qb], self.t_identb], swrites=[tdg[qb]])
            subs = [(kc, h) for kc in range(nkc) for h in range(8)]
            x0 = ctr[0]
            ctr[0] += len(subs)

            def logits(n):
                kc, h = subs[n]
                x = (x0 + n) % 2
                N = min(512, L - kc * 512)
                ks = slice(kc * 512, kc * 512 + N)
                g, r = h // 3, (h % 3) * 32
                p.op("pe", lambda e, x=x, g=g, r=r, ks=ks, N=N, qb=qb: e.matmul(
                    pl1[x][:, 0:N], lhsT=iqt[qb][r:r + 32, g, :], rhs=ik4[r:r + 32, ks],
                    start=True, stop=True), reads=[tres, tiqt[qb]], writes=[tpl1[x]])

            logits(0)
            for n, (kc, h) in enumerate(subs):
                x = (x0 + n) % 2
                N = min(512, L - kc * 512)
                ks = slice(kc * 512, kc * 512 + N)
                p.op("act", lambda e, x=x, N=N: e.activation(out=T[x][:, 0, 0:N], in_=pl1[x][:, 0:N], func=AF.Relu),
                     reads=[tpl1[x]], writes=[tT[x]])
                if n + 1 < len(subs):
                    logits(n + 1)
                p.op("pe", lambda e, x=x, h=h, N=N, qb=qb: e.matmul(
                    psc[:, 0:N], lhsT=dg[qb][:, h, :], rhs=T[x][:, 0, 0:N], start=(h == 0), stop=(h == 7)),
                    reads=[tdg[qb], tT[x]], writes=[tpsc])
                if h == 7:
                    p.op("act", lambda e, ks=ks, N=N, qb=qb: e.copy(out=SC[qb][:, ks], in_=psc[:, 0:N]), reads=[tpsc],
                         swrites=[tSC[qb]])
                if h % 2 == 1:
                    yield

        def thresh(qt):
            qb = qt % 2
            L = 128 * (qt + 1)
            sc, mk, s_, ts_ = SC[qb], MK[qb], st[qb], tst[qb]
            w2 = W2[qb]
            if L > TOPK:
                p.op("dve", lambda e: e.tensor_reduce(out=s_[:, 1:2], in_=sc[:, 0:L], axis=AX.X, op=ALU.max),
                     reads=[tSC[qb]], writes=[ts_])
                p.op("dve", lambda e: e.tensor_reduce(out=s_[:, 0:1], in_=sc[:, 0:TOPK], axis=AX.X, op=ALU.min),
                     reads=[tSC[qb]], writes=[ts_])
                p.op("dve", lambda e: e.tensor_tensor(out=s_[:, 2:3], in0=s_[:, 1:2], in1=s_[:, 0:1],
                                                      op=ALU.subtract), reads=[ts_], writes=[ts_])
                p.op("dve", lambda e: e.tensor_scalar(out=w2[:], in0=ctab[:], scalar1=s_[:, 2:3], scalar2=None,
                                                      op0=ALU.mult), reads=[ts_, tctab], writes=[tW2[qb]])
                p.op("dve", lambda e: e.tensor_tensor(out=s_[:, 3:4], in0=s_[:, 0:1], in1=w2[:, 0:1], op=ALU.add),
                     reads=[ts_, tW2[qb]], writes=[ts_])
            else:
                p.op("dve", lambda e: e.memset(s_[:, 0:1], -1.0e29), writes=[ts_])
            p.op("pool", lambda e: e.affine_select(
                out=sc[:, qt * 128:(qt + 1) * 128], in_=sc[:, qt * 128:(qt + 1) * 128], pattern=[[-1, 128]],
                compare_op=ALU.is_ge, fill=self.freg(e, NEG), base=0, channel_multiplier=1),
                reads=[tSC[qb]], writes=[tSC[qb]])
            if L > TOPK:
                for it in range(NBIS):
                    p.op("dve", lambda e: e.tensor_scalar(out=mk[:, 0:L], in0=sc[:, 0:L], scalar1=s_[:, 3:4],
                                                          scalar2=None, op0=ALU.is_ge, op1=ALU.add,
                                                          accum_out=s_[:, 4:5]),
                         reads=[ts_, tSC[qb]], writes=[ts_], swrites=[tMK[qb]])
                    p.op("dve", lambda e: e.tensor_scalar(out=s_[:, 5:6], in0=s_[:, 4:5], scalar1=TOPK - 0.5,
                                                          scalar2=-0.5, op0=ALU.is_ge, op1=ALU.add),
                         reads=[ts_], writes=[ts_])
                    p.op("dve", lambda e, it=it: e.scalar_tensor_tensor(out=s_[:, 3:4], in0=s_[:, 5:6],
                                                                       scalar=w2[:, it:it + 1], in1=s_[:, 3:4],
                                                                       op0=ALU.mult, op1=ALU.add),
                         reads=[ts_, tW2[qb]], writes=[ts_])
                    yield
                p.op("dve", lambda e: e.scalar_tensor_tensor(out=s_[:, 0:1], in0=w2[:, NBIS - 1:NBIS], scalar=-0.5,
                                                             in1=s_[:, 3:4], op0=ALU.mult, op1=ALU.add),
                     reads=[ts_, tW2[qb]], writes=[ts_])
            p.op("dve", lambda e: e.tensor_scalar(out=mk[:, 0:L], in0=sc[:, 0:L], scalar1=s_[:, 0:1],
                                                  scalar2=None, op0=ALU.is_ge),
                 reads=[ts_, tSC[qb]], writes=[tMK[qb]])

        def part2(qt):
            qb = qt % 2
            qs = slice(qt * 128, (qt + 1) * 128)
            nkt = qt + 1
            p.dma("sp", dq[qb][:], self.DQ[:, qs].rearrange("(h d) q -> d h q", d=128),
                  reads=[self.tQK[qt // 4]], writes=[tdq[qb]])
            for g0 in range(0, nkt, 8):
                n8 = min(8, nkt - g0)
                for t in range(n8):
                    kt = g0 + t
                    p.op("pe", lambda e, kt=kt, t=t, qb=qb: e.transpose(
                        out=pmT[:, t * 128:(t + 1) * 128], in_=MK[qb][:, kt * 128:(kt + 1) * 128],
                        identity=self.ident_bf[:]), reads=[tMK[qb], self.t_identb], swrites=[tpmT])
                p.op("act", lambda e, g0=g0, n8=n8: e.activation(
                    out=mbT[:, g0 * 128:(g0 + n8) * 128], in_=pmT[:, 0:n8 * 128], func=AF.Identity,
                    scale=30000.0, bias=-30000.0), reads=[tpmT], swrites=[tmbT])
                yield
            z0 = ctr[1]
            ctr[1] += nkt

            def front(kt):
                z = (z0 + kt) % 2
                kts = slice(kt * 128, (kt + 1) * 128)
                p.op("pe", lambda e, z=z, kts=kts, qb=qb: e.matmul(
                    pl[z][:, 0:512], lhsT=dkT[:, kts], rhs=dq[qb][:].rearrange("p h q -> p (h q)"), start=True,
                    stop=False), reads=[tres, tdq[qb]], writes=[tpl[z]])
                p.op("pe", lambda e, z=z, kts=kts: e.matmul(
                    pl[z][:, 0:512].rearrange("p (h q) -> p h q", h=4), lhsT=self.ident_bf[:],
                    rhs=mbT[:, kts].rearrange("p (o q) -> p o q", o=1).to_broadcast([128, 4, 128]),
                    start=False, stop=True), reads=[tmbT, self.t_identb], writes=[tpl[z]])

            front(0)
            for kt in range(nkt):
                z = (z0 + kt) % 2
                p.op("act", lambda e, z=z: e.activation(out=P[z][:].rearrange("p h q -> p (h q)"), in_=pl[z][:, 0:512],
                                                        func=AF.Exp, scale=scale),
                     reads=[tpl[z]], writes=[tP[z]])
                if kt + 1 < nkt:
                    front(kt + 1)
                p.op("pe", lambda e, z=z, kt=kt, qt=qt: e.matmul(
                    pO[:], lhsT=dvT[:, kt, :], rhs=P[z][:].rearrange("p h q -> p (h q)"), start=(kt == 0),
                    stop=(kt == qt)), reads=[tres, tP[z]], writes=[tpO])
                p.op("pe", lambda e, z=z, kt=kt, qt=qt: e.matmul(
                    pL[:], lhsT=self.ones_bf[:], rhs=P[z][:].rearrange("p h q -> p (h q)"), start=(kt == 0),
                    stop=(kt == qt)), reads=[self.t_ones, tP[z]], writes=[tpL])
                yield
            p.op("dve", lambda e: e.reciprocal(out=rl[:], in_=pL[:]), reads=[tpL], writes=[trl])
            p.op("dve", lambda e, qb=qb: e.tensor_tensor(out=od[qb][:].rearrange("p h q -> p (h q)"), in0=pO[:],
                                                         in1=rl[:], op=ALU.mult),
                 reads=[tpO, trl], writes=[tod[qb]])
            p.dma("sp", self.OBR[3, :, qs].rearrange("(h d) q -> d h q", d=128), od[qb][:], reads=[tod[qb]],
                  swrites=[self.tOB[3][qt // 4]], pool="st")

        def merge(gens):
            prog = [0] * len(gens)
            alive = [True] * len(gens)
            while any(alive):
                i = min((k for k in range(len(gens)) if alive[k]), key=lambda k: prog[k] / gens[k][1])
                try:
                    next(gens[i][0])
                    prog[i] += 1
                except StopIteration:
                    alive[i] = False

        def ensure_gen(g):
            return g

        for stage in range(NQT + 2):
            gens = []
            t1, t2, t3 = stage, stage - 1, stage - 2
            if 0 <= t1 < NQT:
                gens.append((part1(t1), ((128 * (t1 + 1) + 511) // 512) * 4 + 1))
            if 0 <= t2 < NQT:
                gens.append((thresh(t2), NBIS + 1))
            if 0 <= t3 < NQT:
                gens.append((part2(t3), t3 + 2 + (t3 + 8) // 8))
            merge(gens)
    p.barrier()


B.phaseD = phaseD


def phase3(self, l):
    p, nc, S = self.p, self.nc, self.S
    hin = self.xT if l == 0 else self.H
    with ExitStack() as es:
        Wg = self.sb(es, "Wg", [128, 32, D], BF16)
        tWg = p.toks(32, "Wg")
        Wb = self.sb(es, "Wb", [128, 16, D], BF16)
        tWb = p.toks(16, "Wb")
        Wo = self.sb(es, "Wo", [128, 8, D], BF16)
        tWo = p.toks(8, "Wo")
        with ExitStack() as es2:
            stg = [self.sb(es2, "p3stg", [128, D], F32) for _ in range(2)]
            tstg = p.toks(2, "p3stg")
            for j in range(4):
                self.load_weight(Wg[:, j * 8:(j + 1) * 8, :], tWg[j * 8:(j + 1) * 8], self.w_gate[l, j], 8, D, stg,
                                 tstg, gain=lambda kc: self.vcol(l, "gmix", kc))
                self.load_weight(Wb[:, j * 4:(j + 1) * 4, :], tWb[j * 4:(j + 1) * 4], self.w_branch[l, j], 4, D, stg,
                                 tstg)
            self.load_weight(Wo, tWo, self.w_out[l], 8, D, stg, tstg)
            p.barrier()
        xn = [self.sb(es, "xn3", [128, 8, 512], BF16) for _ in range(2)]
        txn = p.toks(2, "xn3")
        obr = [self.sb(es, "obr3", [128, 16, 512], BF16) for _ in range(2)]
        tobr = p.toks(2, "obr3")
        hb = self.sb(es, "hb3", [128, 8, 512], F32)
        thb = p.tok("hb3")
        mixed = self.sb(es, "mixed", [128, 8, 512], BF16)
        tmixed = p.toks(8, "mixed")
        sg = [self.sb(es, "sg3", [128, 512], F32) for _ in range(2)]
        tsg = p.toks(2, "sg3")
        pr = [self.sb(es, "pr3", [128, 512], F32) for _ in range(2)]
        tpr = p.toks(2, "pr3")
        mix = self.sb(es, "mix3", [128, 512], F32)
        tmix = p.tok("mix3")
        pg = [self.ps(es, "pg", [128, 512]) for _ in range(2)]
        tpg = p.toks(2, "pg")
        pp = [self.ps(es, "pp", [128, 512]) for _ in range(2)]
        tpp = p.toks(2, "pp")
        po = [self.ps(es, "po", [128, 512]) for _ in range(2)]
        tpo = p.toks(2, "po")

        def loads(c):
            b = c % 2
            sl = slice(c * 512, (c + 1) * 512)
            p.dma("sp", xn[b][:], self.XN[:, sl].rearrange("(k p) t -> p k t", p=128), reads=[self.tXN[c]],
                  writes=[txn[b]])
            for j in range(4):
                p.dma("sp", obr[b][:, j * 4:(j + 1) * 4, :],
                      self.OBR[j, :, sl].rearrange("(k p) t -> p k t", p=128), reads=[self.tOB[j][c]],
                      swrites=[tobr[b]])

        loads(0)
        cnt = 0
        for c in range(self.NCH):
            b = c % 2
            sl = slice(c * 512, (c + 1) * 512)
            if c + 1 < self.NCH:
                loads(c + 1)
            rd = [self.tH[c]] if l > 0 else []
            p.dma("sp", hb[:], hin[:, sl].rearrange("(k p) t -> p k t", p=128), reads=rd, writes=[thb])
            for m in range(8):
                ms = slice(m * 128, (m + 1) * 128)
                for j in range(4):
                    i = cnt % 2
                    cnt += 1
                    for kc in range(8):
                        p.op("pe", lambda e, i=i, j=j, kc=kc, ms=ms, b=b: e.matmul(
                            pg[i][:], lhsT=Wg[:, j * 8 + kc, ms], rhs=xn[b][:, kc, :], start=(kc == 0), stop=(kc == 7)),
                            reads=[tWg[j * 8 + kc], txn[b]], writes=[tpg[i]])
                    for kc in range(4):
                        p.op("pe", lambda e, i=i, j=j, kc=kc, ms=ms, b=b: e.matmul(
                            pp[i][:], lhsT=Wb[:, j * 4 + kc, ms], rhs=obr[b][:, j * 4 + kc, :], start=(kc == 0),
                            stop=(kc == 3)), reads=[tWb[j * 4 + kc], tobr[b]], writes=[tpp[i]])
                    p.op("act", lambda e, i=i: e.activation(out=sg[i][:], in_=pg[i][:], func=AF.Sigmoid),
                         reads=[tpg[i]], writes=[tsg[i]])
                    if j == 0:
                        p.op("dve", lambda e, i=i: e.tensor_tensor(out=mix[:], in0=pp[i][:], in1=sg[i][:], op=ALU.mult),
                             reads=[tpp[i], tsg[i]], writes=[tmix])
                    else:
                        p.op("dve", lambda e, i=i: e.tensor_tensor(out=pr[i][:], in0=pp[i][:], in1=sg[i][:],
                                                                   op=ALU.mult),
                             reads=[tpp[i], tsg[i]], writes=[tpr[i]])
                        if j < 3:
                            p.op("pool", lambda e, i=i: e.tensor_tensor(out=mix[:], in0=mix[:], in1=pr[i][:],
                                                                        op=ALU.add),
                                 reads=[tmix, tpr[i]], writes=[tmix])
                        else:
                            p.op("pool", lambda e, i=i, m=m: e.tensor_tensor(out=mixed[:, m, :], in0=mix[:],
                                                                             in1=pr[i][:], op=ALU.add),
                                 reads=[tmix, tpr[i]], writes=[tmixed[m]])
            for m2 in range(8):
                i = m2 % 2
                ms = slice(m2 * 128, (m2 + 1) * 128)
                for m in range(8):
                    p.op("pe", lambda e, i=i, m=m, ms=ms: e.matmul(po[i][:], lhsT=Wo[:, m, ms], rhs=mixed[:, m, :],
                                                                  start=(m == 0), stop=(m == 7)),
                         reads=[tWo[m], tmixed[m]], writes=[tpo[i]])
                p.op("dve", lambda e, i=i, m2=m2: e.tensor_tensor(out=hb[:, m2, :], in0=hb[:, m2, :], in1=po[i][:],
                                                                  op=ALU.add),
                     reads=[tpo[i]], swrites=[thb])
            p.dma("sp", self.H[:, sl].rearrange("(k p) t -> p k t", p=128), hb[:], reads=[thb],
                  writes=[self.tH[c]], pool="st")
    p.barrier()


B.phase3 = phase3


def phase4(self, l):
    p, nc, S = self.p, self.nc, self.S
    NF = 44
    with ExitStack() as es:
        Wu = self.sb(es, "Wu", [128, 8, 2 * DFF], BF16)
        tWu = p.toks(8, "Wu")
        Wd = self.sb(es, "Wd", [128, 22, D], BF16)
        tWd = p.toks(22, "Wd")
        with ExitStack() as es2:
            stg = [self.sb(es2, "p4stg", [128, 2 * DFF], F32) for _ in range(2)]
            tstg = p.toks(2, "p4stg")
            self.load_weight(Wu, tWu, self.w_up[l], 8, 2 * DFF, stg, tstg, gain=lambda kc: self.vcol(l, "gffn", kc))
            self.load_weight(Wd, tWd, self.w_down[l], 22, D, stg, tstg)
            p.barrier()
        hb = self.sb(es, "hb4", [128, 8, 512], F32)
        thb = p.tok("hb4")
        sq = [self.sb(es, "sq4", [128, 512], F32) for _ in range(2)]
        tsq = p.toks(2, "sq4")
        rstd = self.sb(es, "rstd4", [128, 512], F32)
        trstd = p.tok("rstd4")
        xf = self.sb(es, "xf", [128, 8, 512], BF16)
        txf = p.tok("xf")
        F = [self.sb(es, "F4", [128, 514], F32) for _ in range(2)]
        tF = p.toks(2, "F4")
        y = [self.sb(es, "y4", [128, 512], F32) for _ in range(4)]
        ty = p.toks(4, "y4")
        A = self.sb(es, "A4", [128, 22, 512], BF16)
        tA = p.toks(22, "A4")
        carry = [self.sb(es, "carry", [128, NF, 2], F32) for _ in range(2)]
        tcar = [p.toks(NF, "carry") for _ in range(2)]
        pst = self.ps(es, "pst4", [128, 512])
        tpst = p.tok("pst4")
        pu = [self.ps(es, "pu", [128, 512]) for _ in range(3)]
        tpu = p.toks(3, "pu")
        pd = [self.ps(es, "pd", [128, 512]) for _ in range(2)]
        tpd = p.toks(2, "pd")
        p.op("pool", lambda e: e.memset(carry[0][:], 0.0), writes=tcar[0])
        cnt = [0, 0]
        for c in range(self.NCH):
            sl = slice(c * 512, (c + 1) * 512)
            p.dma("sp", hb[:], self.H[:, sl].rearrange("(k p) t -> p k t", p=128), reads=[self.tH[c]], writes=[thb])
            for kc in range(8):
                i = kc % 2
                p.op("act", lambda e, i=i, kc=kc: e.activation(out=sq[i][:], in_=hb[:, kc, :], func=AF.Square),
                     reads=[thb], writes=[tsq[i]])
                p.op("pe", lambda e, i=i, kc=kc: e.matmul(pst[:], lhsT=self.cm[:, 1, :], rhs=sq[i][:], start=(kc == 0),
                                                         stop=(kc == 7)),
                     reads=[tsq[i], self.t_cm], writes=[tpst])
            p.op("act", lambda e: e.activation(out=rstd[:], in_=pst[:], func=AF.Sqrt, bias=self.epsc[:]),
                 reads=[tpst, self.t_eps], writes=[trstd])
            p.op("dve", lambda e: e.reciprocal(out=rstd[:], in_=rstd[:]), reads=[trstd], writes=[trstd])
            for kc in range(8):
                eng = "dve" if kc % 2 == 0 else "pool"
                p.op(eng, lambda e, kc=kc: e.tensor_tensor(out=xf[:, kc, :], in0=hb[:, kc, :], in1=rstd[:], op=ALU.mult),
                     reads=[thb, trstd], swrites=[txf])

            def up_tile(ft, yb):
                i = cnt[0] % 3
                cnt[0] += 1
                fb = cnt[1] % 2
                cnt[1] += 1
                cs = slice(ft * 128, (ft + 1) * 128)
                for kc in range(8):
                    p.op("pe", lambda e, i=i, kc=kc, cs=cs: e.matmul(pu[i][:], lhsT=Wu[:, kc, cs], rhs=xf[:, kc, :],
                                                                    start=(kc == 0), stop=(kc == 7)),
                         reads=[tWu[kc], txf], writes=[tpu[i]])
                w = lambda k: self.vcol(l, "cfw", k * NF + ft)
                cin, cout = carry[c % 2], carry[(c + 1) % 2]
                tcin, tcout = tcar[c % 2], tcar[(c + 1) % 2]
                p.op("act", lambda e, fb=fb, ft=ft, cin=cin: e.copy(out=F[fb][:, 0:2], in_=cin[:, ft, :]),
                     reads=[tcin[ft]], writes=[tF[fb]])
                p.op("act", lambda e, fb=fb, i=i: e.copy(out=F[fb][:, 2:514], in_=pu[i][:]), reads=[tpu[i]],
                     swrites=[tF[fb]])
                p.op("act", lambda e, i=i, ft=ft, cout=cout: e.copy(out=cout[:, ft, :], in_=pu[i][:, 510:512]),
                     reads=[tpu[i]], writes=[tcout[ft]])
                p.op("act", lambda e, i=i, yb=yb, w2=w(2), bb=self.vcol(l, "cfb", ft): e.activation(
                    out=y[yb][:], in_=pu[i][:], func=AF.Identity, scale=w2, bias=bb),
                    reads=[tpu[i], self.t_vec], writes=[ty[yb]])
                for k in (1, 0):
                    p.op("dve", lambda e, fb=fb, yb=yb, k=k, wk=w(k): e.scalar_tensor_tensor(
                        out=y[yb][:], in0=F[fb][:, k:k + 512], scalar=wk, in1=y[yb][:], op0=ALU.mult, op1=ALU.add),
                        reads=[tF[fb], ty[yb], self.t_vec], writes=[ty[yb]])

            for j in range(22):
                yg = (2 * j) % 4
                yv = (2 * j + 1) % 4
                up_tile(j, yg)
                up_tile(j + 22, yv)
                p.op("act", lambda e, yg=yg: e.activation(out=y[yg][:], in_=y[yg][:], func=AF.Silu),
                     reads=[ty[yg]], writes=[ty[yg]])
                p.op("pool", lambda e, yg=yg, yv=yv, j=j: e.tensor_tensor(out=A[:, j, :], in0=y[yg][:], in1=y[yv][:],
                                                                         op=ALU.mult),
                     reads=[ty[yg], ty[yv]], writes=[tA[j]])
            for m2 in range(8):
                i = m2 % 2
                ms = slice(m2 * 128, (m2 + 1) * 128)
                for j in range(22):
                    p.op("pe", lambda e, i=i, j=j, ms=ms: e.matmul(pd[i][:], lhsT=Wd[:, j, ms], rhs=A[:, j, :],
                                                                  start=(j == 0), stop=(j == 21)),
                         reads=[tWd[j], tA[j]], writes=[tpd[i]])
                p.op("dve", lambda e, i=i, m2=m2: e.tensor_tensor(out=hb[:, m2, :], in0=hb[:, m2, :], in1=pd[i][:],
                                                                  op=ALU.add),
                     reads=[tpd[i]], swrites=[thb])
            p.dma("sp", self.H[:, sl].rearrange("(k p) t -> p k t", p=128), hb[:], reads=[thb],
                  writes=[self.tH[c]], pool="st")
    p.barrier()
    self.phase5(l)


B.phase4 = phase4


def phase5(self, l):
    p, nc, S = self.p, self.nc, self.S
    last = (l == self.L - 1)
    dst = self.outT if last else self.H
    with ExitStack() as es:
        Wpg = self.sb(es, "Wpg", [128, 8, D], BF16)
        tWpg = p.toks(8, "Wpg")
        Wpi = self.sb(es, "Wpi", [128, 2, D], BF16)
        tWpi = p.toks(2, "Wpi")
        stg = [self.sb(es, "p5stg", [128, D], F32) for _ in range(2)]
        tstg = p.toks(2, "p5stg")
        self.load_weight(Wpg, tWpg, self.w_pg[l], 8, D, stg, tstg, gain=lambda kc: self.vcol(l, "gpg", kc))
        self.load_weight(Wpi, tWpi, self.w_pi[l], 2, D, stg, tstg)
        hb = [self.sb(es, "hb5", [128, 8, 512], F32) for _ in range(2)]
        thb = p.toks(2, "hb5")
        pb = [self.sb(es, "pb5", [128, 2, 512], F32) for _ in range(2)]
        tpb = p.toks(2, "pb5")
        pbb = self.sb(es, "pbb5", [128, 2, 512], BF16)
        tpbb = p.tok("pbb5")
        sq = [self.sb(es, "sq5", [128, 512], F32) for _ in range(2)]
        tsq = p.toks(2, "sq5")
        rstd = self.sb(es, "rstd5", [128, 512], F32)
        trstd = p.tok("rstd5")
        rse = self.sb(es, "rse5", [128, 512], F32)
        trse = p.tok("rse5")
        xg = self.sb(es, "xg5", [128, 8, 512], BF16)
        txg = p.tok("xg5")
        ee = self.sb(es, "ee5", [128, 8, 512], F32)
        tee = p.toks(8, "ee5")
        sg = [self.sb(es, "sg5", [128, 512], F32) for _ in range(2)]
        tsg = p.toks(2, "sg5")
        t1 = [self.sb(es, "t15", [128, 512], F32) for _ in range(2)]
        tt1 = p.toks(2, "t15")
        pst = self.ps(es, "pst5", [128, 512])
        tpst = p.tok("pst5")
        pse = self.ps(es, "pse5", [128, 512])
        tpse = p.tok("pse5")
        pe_ = [self.ps(es, "pe5", [128, 512]) for _ in range(2)]
        tpe = p.toks(2, "pe5")
        pg = [self.ps(es, "pg5", [128, 512]) for _ in range(2)]
        tpg = p.toks(2, "pg5")

        def loads(c):
            b = c % 2
            sl = slice(c * 512, (c + 1) * 512)
            p.dma("sp", hb[b][:], self.H[:, sl].rearrange("(k p) t -> p k t", p=128), reads=[self.tH[c]],
                  writes=[thb[b]])
            p.dma("sp", pb[b][:], self.pT[l, :, sl].rearrange("(k p) t -> p k t", p=128), writes=[tpb[b]])

        loads(0)
        for c in range(self.NCH):
            b = c % 2
            sl = slice(c * 512, (c + 1) * 512)
            if c + 1 < self.NCH:
                loads(c + 1)
            hbb = hb[b]
            for kc in range(8):
                i = kc % 2
                p.op("act", lambda e, i=i, kc=kc, hbb=hbb: e.activation(out=sq[i][:], in_=hbb[:, kc, :], func=AF.Square),
                     reads=[thb[b]], writes=[tsq[i]])
                p.op("pe", lambda e, i=i, kc=kc: e.matmul(pst[:], lhsT=self.cm[:, 1, :], rhs=sq[i][:], start=(kc == 0),
                                                         stop=(kc == 7)),
                     reads=[tsq[i], self.t_cm], writes=[tpst])
            p.op("act", lambda e: e.activation(out=rstd[:], in_=pst[:], func=AF.Sqrt, bias=self.epsc[:]),
                 reads=[tpst, self.t_eps], writes=[trstd])
            p.op("dve", lambda e: e.reciprocal(out=rstd[:], in_=rstd[:]), reads=[trstd], writes=[trstd])
            for kc in range(8):
                eng = "dve" if kc % 2 == 0 else "pool"
                p.op(eng, lambda e, kc=kc, hbb=hbb: e.tensor_tensor(out=xg[:, kc, :], in0=hbb[:, kc, :], in1=rstd[:],
                                                                   op=ALU.mult),
                     reads=[thb[b], trstd], swrites=[txg])
            p.op("pool", lambda e, b=b: e.tensor_copy(out=pbb[:], in_=pb[b][:]), reads=[tpb[b]], writes=[tpbb])
            for m in range(8):
                i = m % 2
                ms = slice(m * 128, (m + 1) * 128)
                for kc in range(2):
                    p.op("pe", lambda e, i=i, kc=kc, ms=ms: e.matmul(pe_[i][:], lhsT=Wpi[:, kc, ms], rhs=pbb[:, kc, :],
                                                                    start=(kc == 0), stop=(kc == 1)),
                         reads=[tWpi[kc], tpbb], writes=[tpe[i]])
                p.op("act", lambda e, i=i, m=m: e.copy(out=ee[:, m, :], in_=pe_[i][:]), reads=[tpe[i]], writes=[tee[m]])
                p.op("pool", lambda e, i=i, m=m: e.tensor_tensor(out=sq[i][:], in0=ee[:, m, :], in1=ee[:, m, :],
                                                                 op=ALU.mult),
                     reads=[tee[m]], writes=[tsq[i]])
                p.op("pe", lambda e, i=i, m=m: e.matmul(pse[:], lhsT=self.cm[:, 1, :], rhs=sq[i][:], start=(m == 0),
                                                       stop=(m == 7)),
                     reads=[tsq[i], self.t_cm], writes=[tpse])
            p.op("act", lambda e: e.activation(out=rse[:], in_=pse[:], func=AF.Sqrt, bias=self.epsc[:]),
                 reads=[tpse, self.t_eps], writes=[trse])
            p.op("dve", lambda e: e.reciprocal(out=rse[:], in_=rse[:]), reads=[trse], writes=[trse])
            for m in range(8):
                i = m % 2
                ms = slice(m * 128, (m + 1) * 128)
                for kc in range(8):
                    p.op("pe", lambda e, i=i, kc=kc, ms=ms: e.matmul(pg[i][:], lhsT=Wpg[:, kc, ms], rhs=xg[:, kc, :],
                                                                    start=(kc == 0), stop=(kc == 7)),
                         reads=[tWpg[kc], txg], writes=[tpg[i]])
                p.op("act", lambda e, i=i: e.activation(out=sg[i][:], in_=pg[i][:], func=AF.Sigmoid),
                     reads=[tpg[i]], writes=[tsg[i]])
                p.op("dve", lambda e, i=i, m=m, g=self.vcol(l, "gple", m): e.scalar_tensor_tensor(
                    out=t1[i][:], in0=ee[:, m, :], scalar=g, in1=rse[:], op0=ALU.mult, op1=ALU.mult),
                    reads=[tee[m], trse, self.t_vec], writes=[tt1[i]])
                p.op("pool", lambda e, i=i: e.tensor_tensor(out=t1[i][:], in0=t1[i][:], in1=sg[i][:], op=ALU.mult),
                     reads=[tt1[i], tsg[i]], writes=[tt1[i]])
                p.op("dve", lambda e, i=i, m=m, hbb=hbb: e.tensor_tensor(out=hbb[:, m, :], in0=hbb[:, m, :],
                                                                        in1=t1[i][:], op=ALU.add),
                     reads=[tt1[i], txg], swrites=[thb[b]])
            p.dma("sp", dst[:, sl].rearrange("(k p) t -> p k t", p=128), hbb[:], reads=[thb[b]],
                  writes=[self.tH[c]], pool="st")
    p.barrier()


B.phase5 = phase5
```

```python
import math
from contextlib import ExitStack

import numpy as np
import concourse.bass as bass
import concourse.mybir as mybir
from concourse.bass_utils import run_bass_kernel_spmd

F32 = mybir.dt.float32
BF16 = mybir.dt.bfloat16
I32 = mybir.dt.int32
AF = mybir.ActivationFunctionType
ALU = mybir.AluOpType
AX = mybir.AxisListType

ENGS = ("pe", "act", "dve", "pool", "sp")

D = 1024
DB = 512
DIN = 4648
DFF = 2816
PLE = 256
EPS = 1e-6
TOPK = 256
NBIS = 12


class Tok:
    __slots__ = ("name", "W", "R", "prev")

    def __init__(self, name=""):
        self.name = name
        self.W = set()
        self.R = set()
        self.prev = set()


class Ins:
    __slots__ = ("eng", "fn", "deps", "dsem", "needs_inc", "val", "waits")

    def __init__(self, eng, fn, deps, dsem):
        self.eng = eng
        self.fn = fn
        self.deps = deps
        self.dsem = dsem
        self.needs_inc = False
        self.val = None
        self.waits = None


class Prog:
    DMA_POOL = 8

    def __init__(self, nc):
        self.nc = nc
        self.ins = []
        self.dpool = {}
        self.last = {}
        self.bar = {}

    def tok(self, name=""):
        return Tok(name)

    def toks(self, n, name=""):
        return [Tok(f"{name}{i}") for i in range(n)]

    def _deps(self, eng, reads, writes, swrites=()):
        deps = set()
        idx = len(self.ins)
        for t in reads:
            deps |= t.W
            t.R.add(idx)
        for t in writes:
            deps |= t.W
            deps |= t.R
            deps |= t.prev
            t.W = {idx}
            t.R = set()
            t.prev = {idx}
        for t in swrites:
            if t.R:
                t.prev = t.R | t.W
                t.W = set()
                t.R = set()
            deps |= t.prev
            t.W.add(idx)
        deps.discard(idx)
        if eng in self.bar:
            deps |= self.bar.pop(eng)
        self.last[eng] = idx
        return deps

    def barrier(self):
        b = set(self.last.values())
        for hist in self.dpool.values():
            b |= set(hist[-self.DMA_POOL:])
        for e in ENGS:
            self.bar[e] = set(b) | self.bar.get(e, set())

    def op(self, eng, fn, reads=(), writes=(), swrites=()):
        deps = self._deps(eng, reads, writes, swrites)
        self.ins.append(Ins(eng, fn, deps, None))

    def dma(self, eng, out, in_, reads=(), writes=(), swrites=(), pool="ld", slow=False):
        deps = self._deps(eng, reads, writes, swrites)
        hist = self.dpool.setdefault(pool, [])
        i = len(hist)
        if i >= self.DMA_POOL:
            deps.add(hist[i - self.DMA_POOL])
        hist.append(len(self.ins))
        if slow:
            fn = lambda e: e.dma_start(out=out, in_=in_, allow_slow_non_contiguous=True)
        else:
            fn = lambda e: e.dma_start(out=out, in_=in_)
        self.ins.append(Ins(eng, fn, deps, f"{pool}{i % self.DMA_POOL}"))

    def build(self, final_pools=("st",)):
        ins = self.ins
        n = len(ins)

        def skip(p, it):
            return p.eng == "pe" and it.eng == "pe" and p.dsem is None and it.dsem is None

        for it in ins:
            for d in it.deps:
                p = ins[d]
                if not skip(p, it):
                    p.needs_inc = True
        final_ids = []
        for pl in final_pools:
            final_ids += self.dpool.get(pl, [])[-self.DMA_POOL:]
        cnt = {}
        for it in ins:
            if it.dsem is not None:
                key = "D_" + it.dsem
                cnt[key] = cnt.get(key, 0) + 16
                it.val = (key, cnt[key])
            elif it.needs_inc:
                key = "E_" + it.eng
                cnt[key] = cnt.get(key, 0) + 1
                it.val = (key, cnt[key])
        known = {e: {} for e in ENGS}
        evclock = {}
        nwaits = 0
        for it in ins:
            kn = known[it.eng]
            need = {}
            for d in it.deps:
                p = ins[d]
                if p.val is None or skip(p, it):
                    continue
                s, v = p.val
                if kn.get(s, 0) >= v:
                    continue
                if need.get(s, 0) < v:
                    need[s] = v
            waits = []
            for s, v in sorted(need.items(), key=lambda kv: -kv[1]):
                if kn.get(s, 0) >= v:
                    continue
                waits.append((s, v))
                ck = evclock.get((s, v))
                if ck:
                    for ks, kv in ck.items():
                        if kn.get(ks, 0) < kv:
                            kn[ks] = kv
                if kn.get(s, 0) < v:
                    kn[s] = v
            it.waits = waits
            nwaits += len(waits)
            if it.val is not None:
                ck = dict(kn)
                ck[it.val[0]] = it.val[1]
                evclock[it.val] = ck
        self.stats = dict(n=n, nwaits=nwaits, sems=dict(cnt))
        nc = self.nc
        semnames = sorted({it.val[0] for it in ins if it.val is not None})
        with ExitStack() as es:
            sems = {s: es.enter_context(nc.semaphore(s)) for s in semnames}
            block = es.enter_context(nc.Block())
            per = {e: [it for it in ins if it.eng == e] for e in ENGS}
            finals = [ins[d].val for d in final_ids]

            def run(engobj, lst, is_last=False):
                for it in lst:
                    for s, v in it.waits[1:]:
                        engobj.wait_ge(sems[s], v)
                    r = it.fn(engobj)
                    if it.waits:
                        r._wait_ge(sems[it.waits[0][0]], it.waits[0][1])
                    if it.val is not None:
                        r.then_inc(sems[it.val[0]], 16 if it.dsem is not None else 1)
                if is_last:
                    fm = {}
                    for s, v in finals:
                        fm[s] = max(fm.get(s, 0), v)
                    for s, v in fm.items():
                        engobj.wait_ge(sems[s], v)

            @block.tensor
            def _(e):
                run(e, per["pe"])

            @block.scalar
            def _(e):
                run(e, per["act"])

            @block.vector
            def _(e):
                run(e, per["dve"])

            @block.gpsimd
            def _(e):
                run(e, per["pool"], is_last=True)

            @block.sync
            def _(e):
                run(e, per["sp"])


VEC_FIELDS = [("gmix", 8), ("gffn", 8), ("gpg", 8), ("gple", 8), ("caw", 16), ("cab", 4), ("lbr", 4),
              ("lbi", 4), ("lam", 4), ("cbw", 124), ("cbb", 4), ("cbn", 4), ("dqn", 1), ("dkn", 1),
              ("sqn", 1), ("skn", 1), ("ikn", 1), ("sub", 1), ("lq1", 1), ("lk1", 1), ("lq2", 1),
              ("lk2", 1), ("cfw", 132), ("cfb", 44)]
VC = {}
_o = 0
for _n, _k in VEC_FIELDS:
    VC[_n] = (_o, _k)
    _o += _k
NV = _o


def _cols(v, n):
    return np.ascontiguousarray(v.reshape(n, 128).T)


def pack_vec(inp, l):
    out = np.zeros((128, NV), np.float32)

    def put(name, arr):
        o, k = VC[name]
        out[:, o:o + k] = arr.reshape(128, k)

    put("gmix", _cols(inp["norm_mix"][l], 8))
    put("gffn", _cols(inp["norm_ffn"][l], 8))
    put("gpg", _cols(inp["ple_gate_norm"][l], 8))
    put("gple", _cols(inp["ple_norm"][l], 8))
    caw = inp["conv_a_w"][l]
    put("caw", np.stack([_cols(caw[k], 4) for k in range(4)], axis=1).reshape(128, 16))
    put("cab", _cols(inp["conv_a_b"][l], 4))
    put("lbr", _cols(inp["lru_b_r"][l], 4))
    put("lbi", _cols(inp["lru_b_i"][l], 4))
    put("lam", _cols(inp["lru_lambda"][l], 4))
    cbw = inp["conv_b_w"][l]
    put("cbw", np.stack([_cols(cbw[k], 4) for k in range(31)], axis=1).reshape(128, 124))
    put("cbb", _cols(inp["conv_b_b"][l], 4))
    put("cbn", _cols(inp["conv_b_norm"][l], 4))
    p = np.arange(128)
    put("dqn", inp["diff_q_norm"][l][p % 64])
    put("dkn", inp["diff_k_norm"][l][p % 64])
    put("sqn", inp["spa_q_norm"][l][p])
    put("skn", inp["spa_k_norm"][l][p])
    put("ikn", inp["idx_k_norm"][l][p % 32])
    put("sub", inp["diff_subln"][l][p])
    for nm, key in (("lq1", "diff_lq1"), ("lk1", "diff_lk1"), ("lq2", "diff_lq2"), ("lk2", "diff_lk2")):
        v = np.zeros(128, np.float32)
        v[:64] = inp[key][l]
        put(nm, v)
    cfw = inp["conv_f_w"][l]
    put("cfw", np.stack([_cols(cfw[k], 44) for k in range(3)], axis=1).reshape(128, 132))
    put("cfb", _cols(inp["conv_f_b"][l], 44))
    return out


def rope_inv(rot):
    return (np.float32(500000.0) ** (-np.arange(0, rot, 2, dtype=np.float32) / np.float32(rot))).astype(np.float32)


def make_consts(S):
    cm = np.zeros((9, 128, 128), np.float32)
    cm[0] = np.eye(128)
    cm[1] = 1.0 / 1024
    p = np.arange(128)
    cm[2] = (p[:, None] // 64 == p[None, :] // 64) / 64.0
    cm[3] = 1.0 / 128
    cm[4] = (p[:, None] // 32 == p[None, :] // 32) / 32.0
    cm[5] = 1.0 / 512
    rope = np.zeros((3, 2, 128, S), np.float32)
    t = np.arange(S, dtype=np.float32)
    for ci, hd in enumerate((64, 128, 32)):
        rot = hd // 4
        half = rot // 2
        inv = rope_inv(rot)
        ang = (t[:, None] * inv[None, :]).astype(np.float32)
        cos = np.cos(ang).astype(np.float32)
        sin = np.sin(ang).astype(np.float32)
        R = np.zeros((128, 128), np.float32)
        for q in range(128):
            d = q % hd
            if d < half:
                R[q, q + half] = -1.0
                rope[ci, 0, q] = cos[:, d]
                rope[ci, 1, q] = sin[:, d]
            elif d < 2 * half:
                R[q, q - half] = 1.0
                rope[ci, 0, q] = cos[:, d - half]
                rope[ci, 1, q] = sin[:, d - half]
            else:
                rope[ci, 0, q] = 1.0
        cm[6 + ci] = R.T
    return cm, rope


def lru_blockdiag(inp):
    L = inp["lru_w_r"].shape[0]
    out = np.zeros((L, 2, 4, 128, 128), np.float32)
    for l in range(L):
        for gi, key in enumerate(("lru_w_r", "lru_w_i")):
            w = inp[key][l]
            for ct in range(4):
                out[l, gi, ct, :64, :64] = w[2 * ct]
                out[l, gi, ct, 64:, 64:] = w[2 * ct + 1]
    return out


class B:
    def __init__(self, S, L, dbg=False, phases=None):
        self.S, self.L, self.dbg = S, L, dbg
        self.phases = phases
        nc = self.nc = bass.Bass("TRN2", target_bir_lowering=False)
        self.p = Prog(nc)
        self.NCH = S // 512
        dt = nc.dram_tensor

        def inp(name, shape, dtype=F32):
            return dt(name, list(shape), dtype, kind="ExternalInput").ap()

        self.xT = inp("xT", [D, S])
        self.pT = inp("pT", [L, PLE, S])
        self.vec = inp("vec", [L, 128, NV])
        self.cmat = inp("cmat", [9, 128, 128])
        self.rope = inp("rope", [3, 2, 128, S])
        self.lru = inp("lru", [L, 2, 4, 128, 128])
        self.w_in = inp("w_in", [L, D, DIN])
        self.w_gate = inp("w_gate", [L, 4, D, D])
        self.w_branch = inp("w_branch", [L, 4, DB, D])
        self.w_out = inp("w_out", [L, D, D])
        self.w_up = inp("w_up", [L, D, 2 * DFF])
        self.w_down = inp("w_down", [L, DFF, D])
        self.w_pg = inp("w_ple_gate", [L, D, D])
        self.w_pi = inp("w_ple_in", [L, PLE, D])
        self.outT = dt("outT", [D, S], F32, kind="ExternalOutput").ap()

        def scr(name, shape, dtype):
            kind = "ExternalOutput" if dbg else "Internal"
            return dt(name, list(shape), dtype, kind=kind).ap()

        self.H = scr("H", [D, S], F32)
        self.XN = scr("XN", [D, S], BF16)
        self.U16 = scr("U16", [2048, S], F32)
        self.CQ = scr("CQ", [512, S], BF16)
        self.CK = scr("CK", [512, S], BF16)
        self.CV = scr("CV", [S, 512], BF16)
        self.DQ = scr("DQ", [512, S], BF16)
        self.DK = scr("DK", [128, S], BF16)
        self.DV = scr("DV", [S, 128], BF16)
        self.IQ = scr("IQ", [256, S], BF16)
        self.IK = scr("IK", [32, S], BF16)
        self.IW = scr("IW", [S, 128], F32)
        self.OBR = scr("OBR", [4, 512, S], BF16)
        p = self.p
        n = self.NCH
        self.tH = p.toks(n, "H")
        self.tXN = p.toks(n, "XN")
        self.tU = p.toks(n, "U")
        self.tQK = p.toks(n, "QK")
        self.tOB = [p.toks(n, f"OB{j}_") for j in range(4)]
        self.es = ExitStack()
        self._uid = 0

    def sb(self, es, name, shape, dtype):
        self._uid += 1
        return es.enter_context(self.nc.sbuf_tensor(f"{name}_{self._uid}", list(shape), dtype))

    def ps(self, es, name, shape, dtype=F32):
        self._uid += 1
        return es.enter_context(self.nc.psum_tensor(f"{name}_{self._uid}", list(shape), dtype))

    def load_consts(self):
        p, nc = self.p, self.nc
        es = self.es
        self.cm = self.sb(es, "cm", [128, 9, 128], F32)
        self.t_cm = p.tok("cm")
        p.dma("sp", self.cm[:], self.cmat.rearrange("c p n -> p c n"), writes=[self.t_cm])
        self.ones_bf = self.sb(es, "ones_bf", [128, 128], BF16)
        self.t_ones = p.tok("ones")
        p.op("pool", lambda e: e.memset(self.ones_bf[:], 1.0), writes=[self.t_ones])
        self.ident_bf = self.sb(es, "ident_bf", [128, 128], BF16)
        self.t_identb = p.tok("identb")
        p.op("dve", lambda e: e.tensor_copy(out=self.ident_bf[:], in_=self.cm[:, 0, :]),
             reads=[self.t_cm], writes=[self.t_identb])
        self.vecs = self.sb(es, "vecs", [128, self.L, NV], F32)
        self.t_vec = p.tok("vec")
        p.dma("sp", self.vecs[:], self.vec.rearrange("l p n -> p l n"), writes=[self.t_vec])
        self.epsc = self.sb(es, "epsc", [128, 1], F32)
        self.t_eps = p.tok("eps")
        p.op("pool", lambda e: e.memset(self.epsc[:], EPS), writes=[self.t_eps])

    def freg(self, e, val):
        if not hasattr(self, "_fregs"):
            self._fregs = {}
        if val not in self._fregs:
            self._fregs[val] = e.to_reg(val)
        return self._fregs[val]

    def vcol(self, l, name, j=0, n=1):
        o, k = VC[name]
        return self.vecs[:, l, o + j:o + j + n]

    def load_weight(self, dst, dst_toks, src, K, N, stg, stg_toks, gain=None, col0=0, engs=("dve", "pool"),
                    rows=128):
        p = self.p
        for kc in range(K):
            i = self._wl % len(stg)
            e = engs[self._wl % len(engs)]
            self._wl += 1
            st, stt = stg[i], stg_toks[i]
            p.dma("sp", st[0:rows, 0:N], src[kc * rows:(kc + 1) * rows, :], writes=[stt], pool="w")
            o = dst[0:rows, kc, col0:col0 + N]
            if gain is not None:
                g = gain(kc)
                p.op(e, (lambda o=o, st=st, g=g: lambda en: en.tensor_scalar(
                    out=o, in0=st[0:rows, 0:N], scalar1=g, scalar2=1.0, op0=ALU.mult, op1=ALU.mult))(),
                    reads=[stt, self.t_vec], swrites=[dst_toks[kc]])
            else:
                p.op(e, (lambda o=o, st=st: lambda en: en.tensor_copy(out=o, in_=st[0:rows, 0:N]))(),
                     reads=[stt], swrites=[dst_toks[kc]])

    _wl = 0
    _cc = 0

    def phase1(self, l):
        p, nc, S = self.p, self.nc, self.S
        hin = self.xT if l == 0 else self.H
        cm = self.cm
        with ExitStack() as es:
            W = self.sb(es, "p1W", [128, 8, DIN], BF16)
            tW = p.toks(8, "p1W")
            Wdi = self.sb(es, "p1Wdi", [128, 8, 256], BF16)
            tWdi = p.tok("Wdi")
            with ExitStack() as es2:
                stg = [self.sb(es2, "p1stg", [128, DIN], F32) for _ in range(2)]
                tstg = p.toks(2, "stg")
                self.load_weight(W, tW, self.w_in[l], 8, DIN, stg, tstg,
                                 gain=lambda kc: self.vcol(l, "gmix", kc))
                p.op("pool", lambda e: e.memset(Wdi[:], 0.0), writes=[tWdi])
                for kc in range(8):
                    p.op("dve", lambda e, kc=kc: e.tensor_copy(out=Wdi[:, kc, 0:128], in_=W[:, kc, 4224:4352]),
                         reads=[tW[kc]], swrites=[tWdi])
                    p.op("dve", lambda e, kc=kc: e.tensor_copy(out=Wdi[:, kc, 128:136], in_=W[:, kc, 4640:4648]),
                         reads=[tW[kc]], swrites=[tWdi])
                p.barrier()
            hb = [self.sb(es, "hb", [128, 8, 512], F32) for _ in range(2)]
            thb = p.toks(2, "hb")
            rp = [self.sb(es, "rp", [128, 6, 512], F32) for _ in range(1)] * 2
            trp = [p.tok("rp")] * 2
            sq = [self.sb(es, "sq", [128, 512], F32) for _ in range(2)]
            tsq = p.toks(2, "sq")
            rstd = self.sb(es, "rstd", [128, 512], F32)
            trstd = p.tok("rstd")
            xn = [self.sb(es, "xn", [128, 8, 512], BF16) for _ in range(2)]
            txn = p.toks(2, "xn")
            raw = self.sb(es, "raw", [128, 8, 512], F32)
            traw = p.tok("raw")
            NE = 3
            ev = [self.sb(es, "ev", [128, 512], F32) for _ in range(NE)]
            tev = p.toks(NE, "ev")
            sq2 = [self.sb(es, "sq2", [128, 512], F32) for _ in range(NE)]
            tsq2 = p.toks(NE, "sq2")
            rs2 = [self.sb(es, "rs2", [128, 512], F32) for _ in range(NE)]
            trs2 = p.toks(NE, "rs2")
            xg = [self.sb(es, "xg", [128, 512], F32) for _ in range(NE)]
            txg = p.toks(NE, "xg")
            t1, tt1 = sq2, tsq2
            t2, tt2 = rs2, trs2
            ob = [self.sb(es, "ob", [128, 512], BF16) for _ in range(NE)]
            tob = p.toks(NE, "ob")
            tv = [self.sb(es, "tv", [128, 512], BF16)] * 2
            ttv = [p.tok("tv")] * 2
            tdv = [self.sb(es, "tdv", [128, 128], BF16) for _ in range(2)]
            ttdv = p.toks(2, "tdv")
            tiw = [self.sb(es, "tiw", [128, 128], F32) for _ in range(2)]
            ttiw = p.toks(2, "tiw")
            pst = self.ps(es, "pst", [128, 512])
            tpst = p.tok("pst")
            pm = [self.ps(es, "pm", [128, 512]) for _ in range(3)]
            tpm = p.toks(3, "pm")
            ps2 = self.ps(es, "ps2", [128, 512])
            tps2 = p.tok("ps2")
            ps3 = self.ps(es, "ps3", [128, 512])
            tps3 = p.tok("ps3")
            ptk = self.ps(es, "ptk", [128, 512])
            tptk = p.tok("ptk")
            ptd = self.ps(es, "ptd", [128, 256])
            tptd = p.tok("ptd")

            def loads(c):
                b = c % 2
                sl = slice(c * 512, (c + 1) * 512)
                rd = [self.tH[c]] if l > 0 else []
                p.dma("sp", hb[b][:], hin[:, sl].rearrange("(k p) t -> p k t", p=128), reads=rd, writes=[thb[b]])

            def load_rp(c):
                sl = slice(c * 512, (c + 1) * 512)
                p.dma("sp", rp[0][:], self.rope[:, :, :, sl].rearrange("c s p t -> p (c s) t"), writes=[trp[0]])

            qk_tiles = []
            for i in range(4):
                qk_tiles.append((16 + i, 128, 2, "dqn", 0, self.CQ, i * 128))
            for i in range(4):
                qk_tiles.append((20 + i, 128, 2, "dkn", 0, self.CK, i * 128))
            for i in range(4):
                qk_tiles.append((28 + i, 128, 3, "sqn", 1, self.DQ, i * 128))
            qk_tiles.append((32, 128, 3, "skn", 1, self.DK, 0))
            qk_tiles.append((34, 128, None, None, 2, self.IQ, 0))
            qk_tiles.append((35, 128, None, None, 2, self.IQ, 128))
            qk_tiles.append((36, 32, 4, "ikn", 2, self.IK, 0))

            loads(0)
            cnt = [0, 0]
            for c in range(self.NCH):
                b = c % 2
                sl = slice(c * 512, (c + 1) * 512)
                if c + 1 < self.NCH:
                    loads(c + 1)
                load_rp(c)
                hbb, xnb, rpb = hb[b], xn[b], rp[b]
                STOP = 9
                for kc in range(8):
                    si = kc % 2
                    p.op("act", lambda e, hbb=hbb, kc=kc, si=si: e.activation(out=sq[si][:], in_=hbb[:, kc, :],
                                                                              func=AF.Square),
                         reads=[thb[b]], writes=[tsq[si]])
                    p.op("pe", lambda e, kc=kc, si=si: e.matmul(pst[:], lhsT=cm[:, 1, :], rhs=sq[si][:],
                                                               start=(kc == 0), stop=(kc == 7)),
                         reads=[tsq[si], self.t_cm], writes=[tpst])
                p.op("act", lambda e: e.activation(out=rstd[:], in_=pst[:], func=AF.Sqrt, bias=self.epsc[:]),
                     reads=[tpst, self.t_eps], writes=[trstd])
                p.op("dve", lambda e: e.reciprocal(out=rstd[:], in_=rstd[:]), reads=[trstd], writes=[trstd])
                for kc in range(8):
                    eng = "dve" if kc % 2 == 0 else "pool"
                    p.op(eng, lambda e, kc=kc, hbb=hbb, xnb=xnb: e.tensor_tensor(
                        out=xnb[:, kc, :], in0=hbb[:, kc, :], in1=rstd[:], op=ALU.mult),
                        reads=[thb[b], trstd], swrites=[txn[b]])
                p.dma("sp", self.XN[:, sl].rearrange("(k p) t -> p k t", p=128), xnb[:],
                      reads=[txn[b]], writes=[self.tXN[c]], pool="st")

                def mm_fm(m, M, pidx):
                    c0 = m * 128
                    for kc in range(8):
                        p.op("pe", lambda e, kc=kc, c0=c0, M=M, pidx=pidx, xnb=xnb: e.matmul(
                            pm[pidx][0:M, :], lhsT=W[:, kc, c0:c0 + M], rhs=xnb[:, kc, :],
                            start=(kc == 0), stop=(kc == 7)),
                            reads=[tW[kc], txn[b]], writes=[tpm[pidx]])

                for m in range(16 if STOP > 2 else 0):
                    pidx = cnt[0] % 3
                    cnt[0] += 1
                    mm_fm(m, 128, pidx)
                    if m % 2 == 0:
                        p.op("act", lambda e, m=m, pidx=pidx: e.copy(out=raw[:, m % 8, :], in_=pm[pidx][:]),
                             reads=[tpm[pidx]], swrites=[traw])
                    else:
                        p.op("dve", lambda e, m=m, pidx=pidx: e.tensor_copy(out=raw[:, m % 8, :], in_=pm[pidx][:]),
                             reads=[tpm[pidx]], swrites=[traw])
                    if m % 8 == 7:
                        m0 = (m // 8) * 1024
                        p.dma("sp", self.U16[m0:m0 + 1024, sl].rearrange("(m p) t -> p m t", p=128), raw[:],
                              reads=[traw], swrites=[self.tU[c]], pool="st")
                nq = len(qk_tiles)
                pid = {}

                def stA(n):
                    (m, M, gi, gname, rc, dst, r0) = qk_tiles[n]
                    pidx = cnt[0] % 3
                    cnt[0] += 1
                    i = (cnt[1] + n) % NE
                    mm_fm(m, M, pidx)
                    p.op("act", lambda e, i=i, pidx=pidx, M=M: e.copy(out=ev[i][0:M, :], in_=pm[pidx][0:M, :]),
                         reads=[tpm[pidx]], writes=[tev[i]])
                    if gi is not None:
                        p.op("act", lambda e, i=i, pidx=pidx, M=M: e.activation(out=sq2[i][0:M, :],
                                                                               in_=pm[pidx][0:M, :], func=AF.Square),
                             reads=[tpm[pidx]], writes=[tsq2[i]])

                def stB(n):
                    (m, M, gi, gname, rc, dst, r0) = qk_tiles[n]
                    i = (cnt[1] + n) % NE
                    if gi is None:
                        return
                    p.op("pe", lambda e, i=i, M=M, gi=gi: e.matmul(ps2[0:M, :], lhsT=cm[0:M, gi, 0:M],
                                                                  rhs=sq2[i][0:M, :], start=True, stop=True),
                         reads=[tsq2[i], self.t_cm], writes=[tps2])
                    p.op("act", lambda e, i=i, M=M: e.activation(out=rs2[i][0:M, :], in_=ps2[0:M, :],
                                                                 func=AF.Sqrt, bias=self.epsc[0:M, :]),
                         reads=[tps2, self.t_eps], writes=[trs2[i]])
                    p.op("dve", lambda e, i=i, M=M: e.reciprocal(out=rs2[i][0:M, :], in_=rs2[i][0:M, :]),
                         reads=[trs2[i]], writes=[trs2[i]])
                    g = self.vcol(l, gname)[0:M, :]
                    p.op("dve", lambda e, i=i, M=M, g=g: e.scalar_tensor_tensor(
                        out=xg[i][0:M, :], in0=ev[i][0:M, :], scalar=g, in1=rs2[i][0:M, :],
                        op0=ALU.mult, op1=ALU.mult),
                        reads=[tev[i], trs2[i], self.t_vec], writes=[txg[i]])

                def stC(n):
                    (m, M, gi, gname, rc, dst, r0) = qk_tiles[n]
                    i = (cnt[1] + n) % NE
                    xs, txs = (xg[i], txg[i]) if gi is not None else (ev[i], tev[i])
                    p.op("pe", lambda e, xs=xs, M=M, rc=rc: e.matmul(ps3[0:M, :], lhsT=cm[0:M, 6 + rc, 0:M],
                                                                    rhs=xs[0:M, :], start=True, stop=True),
                         reads=[txs, self.t_cm], writes=[tps3])
                    p.op("pool", lambda e, i=i, xs=xs, M=M, rc=rc, rpb=rpb: e.tensor_tensor(
                        out=t1[i][0:M, :], in0=xs[0:M, :], in1=rpb[0:M, 2 * rc, :], op=ALU.mult),
                        reads=[txs, trp[b]], writes=[tt1[i]])
                    p.op("dve", lambda e, i=i, M=M, rc=rc, rpb=rpb: e.tensor_tensor(
                        out=t2[i][0:M, :], in0=ps3[0:M, :], in1=rpb[0:M, 2 * rc + 1, :], op=ALU.mult),
                        reads=[tps3, trp[b]], writes=[tt2[i]])
                    p.op("pool", lambda e, i=i, M=M: e.tensor_tensor(
                        out=ob[i][0:M, :], in0=t1[i][0:M, :], in1=t2[i][0:M, :], op=ALU.add),
                        reads=[tt1[i], tt2[i]], writes=[tob[i]])
                    p.dma("sp", dst[r0:r0 + M, sl], ob[i][0:M, :], reads=[tob[i]], swrites=[self.tQK[c]], pool="st")

                for step in range(nq + 2):
                    if step < nq:
                        stA(step)
                    if 0 <= step - 1 < nq:
                        stB(step - 1)
                    if 0 <= step - 2 < nq:
                        stC(step - 2)
                cnt[1] += nq
                for j in range(4 if STOP > 4 else 0):
                    jb = j % 2
                    ts = slice(j * 128, (j + 1) * 128)
                    r0 = c * 512 + j * 128
                    for kc in range(8):
                        p.op("pe", lambda e, kc=kc, ts=ts, xnb=xnb: e.matmul(
                            ptk[:], lhsT=xnb[:, kc, ts], rhs=W[:, kc, 3072:3584], start=(kc == 0), stop=(kc == 7)),
                            reads=[tW[kc], txn[b]], writes=[tptk])
                    for kc in range(8):
                        p.op("pe", lambda e, kc=kc, ts=ts, xnb=xnb: e.matmul(
                            ptd[:, 0:256], lhsT=xnb[:, kc, ts], rhs=Wdi[:, kc, :], start=(kc == 0),
                            stop=(kc == 7)), reads=[tWdi, txn[b]], writes=[tptd])
                    p.op("act", lambda e, jb=jb: e.copy(out=tv[jb][:], in_=ptk[:]), reads=[tptk], writes=[ttv[jb]])
                    p.op("dve", lambda e, jb=jb: e.tensor_copy(out=tdv[jb][:], in_=ptd[:, 0:128]),
                         reads=[tptd], writes=[ttdv[jb]])
                    p.op("dve", lambda e, jb=jb: e.tensor_scalar(out=tiw[jb][:], in0=ptd[:, 128:256], scalar1=1.0 / 16.0,
                                                                 scalar2=None, op0=ALU.mult),
                         reads=[tptd], writes=[ttiw[jb]])
                    p.dma("sp", self.CV[r0:r0 + 128, :], tv[jb][:], reads=[ttv[jb]], swrites=[self.tQK[c]], pool="st")
                    p.dma("sp", self.DV[r0:r0 + 128, :], tdv[jb][:], reads=[ttdv[jb]], swrites=[self.tQK[c]],
                          pool="st")
                    p.dma("sp", self.IW[r0:r0 + 128, :], tiw[jb][:], reads=[ttiw[jb]], swrites=[self.tQK[c]],
                          pool="st")
            p.barrier()

    def build(self):
        nc = self.nc
        with nc.allow_low_precision("bf16 matmul operands, fp32 accumulation"):
            with self.es:
                self.load_consts()
                ph = self.phases
                for l in range(self.L):
                    if ph is None or "p1" in ph:
                        self.phase1(l)
                    if ph is None or "a" in ph or "b" in ph:
                        self.phaseAB(l)
                    if ph is None or "c" in ph:
                        self.phaseC(l)
                    if ph is None or "d" in ph:
                        self.phaseD(l)
                    if ph is None or "p3" in ph:
                        self.phase3(l)
                    if ph is None or "p4" in ph:
                        self.phase4(l)
                self.p.build()
        return nc


def host_inputs(inp, S, L):
    cm, rope = make_consts(S)
    vec = np.stack([pack_vec(inp, l) for l in range(L)])
    lru = lru_blockdiag(inp)
    common = dict(vec=vec, cmat=cm, rope=rope, lru=lru[:L])
    for k_dev, k_in in (("w_in", "w_in"), ("w_gate", "w_gate"), ("w_branch", "w_branch"), ("w_out", "w_out"),
                        ("w_up", "w_up"), ("w_down", "w_down"), ("w_ple_gate", "w_ple_gate"),
                        ("w_ple_in", "w_ple_in")):
        common[k_dev] = np.ascontiguousarray(inp[k_in][:L], dtype=np.float32)
    maps = []
    nb = inp["x"].shape[0]
    for b in range(nb):
        m = dict(common)
        m["xT"] = np.ascontiguousarray(inp["x"][b].T)
        m["pT"] = np.ascontiguousarray(np.transpose(inp["p"][:L, b], (0, 2, 1)))
        maps.append(m)
    return maps


def kernel(**inputs):
    inp = {k: np.asarray(v) for k, v in inputs.items()}
    nb, S, _ = inp["x"].shape
    L = inp["w_in"].shape[0]
    bld = B(S, L)
    nc = bld.build()
    maps = host_inputs(inp, S, L)
    res = run_bass_kernel_spmd(nc, maps, core_ids=list(range(nb)))
    out = np.stack([np.ascontiguousarray(res.results[b]["outT"].T) for b in range(nb)])
    return out.astype(np.float32)


def genA(self, l, es):
    p, nc, S = self.p, self.nc, self.S
    TC = 1024
    NT = S // TC
    if True:
        lst = self.sb(es, "lrust", [128, 8, 128], F32)
        tlst = p.tok("lrust")
        lw = self.sb(es, "lruw", [128, 8, 128], BF16)
        tlw = p.tok("lruw")
        p.dma("sp", lst[:], self.lru[l].rearrange("g c p n -> p (g c) n"), writes=[tlst])
        p.op("dve", lambda e: e.tensor_copy(out=lw[:], in_=lst[:]), reads=[tlst], writes=[tlw])
        cc = self.sb(es, "lruc", [128, 12], F32)
        tcc = p.tok("lruc")
        onec = self.sb(es, "onec", [128, 1], F32)
        tone = p.tok("onec")
        p.op("pool", lambda e: e.memset(onec[:], 1.0 + 2.0 ** -23), writes=[tone])
        lam = self.vcol(l, "lam", 0, 4)
        p.op("act", lambda e: e.activation(out=cc[:, 0:4], in_=lam, func=AF.Exp, scale=-1.0),
             reads=[self.t_vec], writes=[tcc])
        p.op("act", lambda e: e.activation(out=cc[:, 0:4], in_=cc[:, 0:4], func=AF.Ln, bias=1.0),
             reads=[tcc], writes=[tcc])
        p.op("dve", lambda e: e.tensor_scalar(out=cc[:, 4:8], in0=cc[:, 0:4], scalar1=-8.0, scalar2=None,
                                              op0=ALU.mult), reads=[tcc], writes=[tcc])
        p.op("dve", lambda e: e.tensor_scalar(out=cc[:, 8:12], in0=cc[:, 0:4], scalar1=-16.0, scalar2=None,
                                              op0=ALU.mult), reads=[tcc], writes=[tcc])

        def f32buf(name, n=TC):
            return self.sb(es, name, [128, n], F32), p.tok(name)

        xa, txa = f32buf("xa", TC + 3)
        xc, txc = f32buf("xc")
        xcb = self.sb(es, "xcb", [128, TC], BF16)
        txcb = p.tok("xcb")
        r, tr = f32buf("r")
        ig, tig = f32buf("ig")
        a, ta = f32buf("a")
        dr, tdr = f32buf("dr")
        gx, tgx = f32buf("gx")
        h, th = f32buf("h")
        ag, tag = f32buf("ag")
        tg, ttg = f32buf("tg")
        sg, tsg = f32buf("sg")
        ob = self.sb(es, "oba", [128, TC], BF16)
        tob = p.tok("oba")
        hst = self.sb(es, "hst", [128, 1], F32)
        thst = p.tok("hst")
        pr = self.ps(es, "pr", [128, TC])
        tpr = p.tok("pr")
        pi = self.ps(es, "pi", [128, TC])
        tpi = p.tok("pi")
        nsub = TC // 512
        for ct in range(4):
            rows = slice(ct * 128, (ct + 1) * 128)
            p.op("pool", lambda e: e.memset(hst[:], 0.0), writes=[thst])
            for t in range(NT):
                t0 = t * TC
                chs = list(range(t0 // 512, (t0 + TC) // 512))
                rd = [self.tU[c] for c in chs]
                if t == 0:
                    p.op("pool", lambda e: e.memset(xa[:, 0:3], 0.0), writes=[txa])
                    p.dma("sp", xa[:, 3:TC + 3], self.U16[rows, 0:TC], reads=rd, swrites=[txa])
                else:
                    p.dma("sp", xa[:, :], self.U16[rows, t0 - 3:t0 + TC], reads=rd + [self.tU[chs[0] - 1]],
                          writes=[txa])
                p.dma("sp", ag[:], self.U16[512 + ct * 128:512 + (ct + 1) * 128, t0:t0 + TC], reads=rd, writes=[tag])
                w = lambda k: self.vcol(l, "caw", k * 4 + ct)
                p.op("dve", lambda e, w0=w(0), bb=self.vcol(l, "cab", ct): e.tensor_scalar(
                    out=xc[:], in0=xa[:, 0:TC], scalar1=w0, scalar2=bb, op0=ALU.mult, op1=ALU.add),
                    reads=[txa, self.t_vec], writes=[txc])
                for k in range(1, 4):
                    p.op("dve", lambda e, k=k, wk=w(k): e.scalar_tensor_tensor(
                        out=xc[:], in0=xa[:, k:k + TC], scalar=wk, in1=xc[:], op0=ALU.mult, op1=ALU.add),
                        reads=[txa, txc, self.t_vec], writes=[txc])
                p.op("pool", lambda e: e.tensor_copy(out=xcb[:], in_=xc[:]), reads=[txc], writes=[txcb])
                for s in range(nsub):
                    ss = slice(s * 512, (s + 1) * 512)
                    p.op("pe", lambda e, ss=ss, ct=ct: e.matmul(pr[:, ss], lhsT=lw[:, ct, :], rhs=xcb[:, ss],
                                                               start=True, stop=True),
                         reads=[tlw, txcb], swrites=[tpr])
                    p.op("pe", lambda e, ss=ss, ct=ct: e.matmul(pi[:, ss], lhsT=lw[:, 4 + ct, :], rhs=xcb[:, ss],
                                                               start=True, stop=True),
                         reads=[tlw, txcb], swrites=[tpi])
                p.op("act", lambda e, bb=self.vcol(l, "lbr", ct): e.activation(out=r[:], in_=pr[:], func=AF.Sigmoid,
                                                                              bias=bb),
                     reads=[tpr, self.t_vec], writes=[tr])
                p.op("act", lambda e, bb=self.vcol(l, "lbi", ct): e.activation(out=ig[:], in_=pi[:], func=AF.Sigmoid,
                                                                              bias=bb),
                     reads=[tpi, self.t_vec], writes=[tig])
                p.op("act", lambda e, ct=ct: e.activation(out=a[:], in_=r[:], func=AF.Exp, scale=cc[:, 4 + ct:5 + ct]),
                     reads=[tr, tcc], writes=[ta])
                p.op("act", lambda e, ct=ct: e.activation(out=dr[:], in_=r[:], func=AF.Exp,
                                                          scale=cc[:, 8 + ct:9 + ct]),
                     reads=[tr, tcc], writes=[tdr])
                p.op("act", lambda e: e.activation(out=dr[:], in_=dr[:], func=AF.Sqrt, scale=-1.0, bias=onec[:]),
                     reads=[tdr, tone], writes=[tdr])
                p.op("pool", lambda e: e.tensor_tensor(out=gx[:], in0=ig[:], in1=xc[:], op=ALU.mult),
                     reads=[tig, txc], writes=[tgx])
                p.op("pool", lambda e: e.tensor_tensor(out=gx[:], in0=gx[:], in1=dr[:], op=ALU.mult),
                     reads=[tgx, tdr], writes=[tgx])
                p.op("dve", lambda e: e.tensor_tensor_scan(out=h[:], data0=a[:], data1=gx[:], initial=hst[:],
                                                           op0=ALU.mult, op1=ALU.add),
                     reads=[ta, tgx, thst], writes=[th])
                p.op("act", lambda e: e.copy(out=hst[:], in_=h[:, TC - 1:TC]), reads=[th], writes=[thst])
                p.op("pool", lambda e: e.tensor_tensor(out=tg[:], in0=ag[:], in1=ag[:], op=ALU.mult),
                     reads=[tag], writes=[ttg])
                p.op("pool", lambda e: e.tensor_scalar(out=tg[:], in0=tg[:], scalar1=0.044715, scalar2=1.0,
                                                       op0=ALU.mult, op1=ALU.add), reads=[ttg], writes=[ttg])
                p.op("pool", lambda e: e.tensor_tensor(out=tg[:], in0=tg[:], in1=ag[:], op=ALU.mult),
                     reads=[ttg, tag], writes=[ttg])
                p.op("act", lambda e: e.activation(out=sg[:], in_=tg[:], func=AF.Sigmoid,
                                                   scale=2.0 * math.sqrt(2.0 / math.pi)),
                     reads=[ttg], writes=[tsg])
                p.op("pool", lambda e: e.tensor_tensor(out=sg[:], in0=sg[:], in1=ag[:], op=ALU.mult),
                     reads=[tsg, tag], writes=[tsg])
                p.op("dve", lambda e: e.tensor_tensor(out=ob[:], in0=h[:], in1=sg[:], op=ALU.mult),
                     reads=[th, tsg], writes=[tob])
                for c in chs:
                    o = c * 512 - t0
                    p.dma("sp", self.OBR[0, rows, c * 512:(c + 1) * 512], ob[:, o:o + 512], reads=[tob],
                          swrites=[self.tOB[0][c]], pool="st")
                yield


B.genA = genA


def genB(self, l, es):
    p, nc, S = self.p, self.nc, self.S
    TC = 1024
    NT = S // TC
    KC = 31
    if True:
        vb = [self.sb(es, "vb", [128, TC], F32) for _ in range(2)]
        tvb = p.toks(2, "vb")
        gb = [self.sb(es, "gb", [128, TC], F32) for _ in range(2)]
        tgb = p.toks(2, "gb")
        y = [self.sb(es, "y", [128, TC + KC - 1], F32) for _ in range(4)]
        ty = p.toks(4, "y")
        acc = [self.sb(es, "acc", [128, TC], F32) for _ in range(4)]
        tacc = p.toks(4, "acc")
        sqb = [self.sb(es, "sqb", [128, TC], F32) for _ in range(2)]
        tsqb = p.toks(2, "sqb")
        rs = self.sb(es, "rsb", [128, TC], F32)
        trs = p.tok("rsb")
        zb = [self.sb(es, "zb", [128, TC], F32) for _ in range(2)]
        tzb = p.toks(2, "zb")
        ob = [self.sb(es, "obb", [128, TC], BF16) for _ in range(2)]
        tob = p.toks(2, "obb")
        pn = self.ps(es, "pn", [128, TC])
        tpn = p.tok("pn")
        nsub = TC // 512
        for ct in range(4):
            p.op("pool", lambda e, ct=ct: e.memset(y[ct][:, 0:KC - 1], 0.0), writes=[ty[ct]])
        for t in range(NT):
            t0 = t * TC
            chs = list(range(t0 // 512, (t0 + TC) // 512))
            rd = [self.tU[c] for c in chs]
            for ct in range(4):
                b = ct % 2
                p.dma("sp", vb[b][:], self.U16[1024 + ct * 128:1024 + (ct + 1) * 128, t0:t0 + TC], reads=rd,
                      writes=[tvb[b]])
                p.dma("sp", gb[b][:], self.U16[1536 + ct * 128:1536 + (ct + 1) * 128, t0:t0 + TC], reads=rd,
                      writes=[tgb[b]])
                p.op("act", lambda e, b=b: e.activation(out=gb[b][:], in_=gb[b][:], func=AF.Sigmoid),
                     reads=[tgb[b]], writes=[tgb[b]])
                if t > 0:
                    p.op("act", lambda e, ct=ct: e.copy(out=y[ct][:, 0:KC - 1], in_=y[ct][:, TC:TC + KC - 1]),
                         reads=[ty[ct]], writes=[ty[ct]])
                p.op("pool", lambda e, ct=ct, b=b: e.tensor_tensor(out=y[ct][:, KC - 1:KC - 1 + TC], in0=vb[b][:],
                                                                  in1=gb[b][:], op=ALU.mult),
                     reads=[tvb[b], tgb[b], ty[ct]], writes=[ty[ct]])
                w = lambda k: self.vcol(l, "cbw", k * 4 + ct)
                p.op("dve", lambda e, ct=ct, w0=w(0), bb=self.vcol(l, "cbb", ct): e.tensor_scalar(
                    out=acc[ct][:], in0=y[ct][:, 0:TC], scalar1=w0, scalar2=bb, op0=ALU.mult, op1=ALU.add),
                    reads=[ty[ct], self.t_vec], writes=[tacc[ct]])
                for k in range(1, KC):
                    p.op("dve", lambda e, ct=ct, k=k, wk=w(k): e.scalar_tensor_tensor(
                        out=acc[ct][:], in0=y[ct][:, k:k + TC], scalar=wk, in1=acc[ct][:], op0=ALU.mult,
                        op1=ALU.add), reads=[ty[ct], tacc[ct], self.t_vec], writes=[tacc[ct]])
                p.op("act", lambda e, ct=ct, b=b: e.activation(out=sqb[b][:], in_=acc[ct][:], func=AF.Square),
                     reads=[tacc[ct]], writes=[tsqb[b]])
                for s in range(nsub):
                    ss = slice(s * 512, (s + 1) * 512)
                    p.op("pe", lambda e, ss=ss, b=b, ct=ct: e.matmul(pn[:, ss], lhsT=self.cm[:, 5, :],
                                                                    rhs=sqb[b][:, ss], start=(ct == 0),
                                                                    stop=(ct == 3)),
                         reads=[tsqb[b], self.t_cm], swrites=[tpn])
                yield
            p.op("act", lambda e: e.activation(out=rs[:], in_=pn[:], func=AF.Sqrt, bias=self.epsc[:]),
                 reads=[tpn, self.t_eps], writes=[trs])
            p.op("dve", lambda e: e.reciprocal(out=rs[:], in_=rs[:]), reads=[trs], writes=[trs])
            for ct in range(4):
                b = ct % 2
                p.op("dve", lambda e, ct=ct, b=b, g=self.vcol(l, "cbn", ct): e.scalar_tensor_tensor(
                    out=zb[b][:], in0=acc[ct][:], scalar=g, in1=rs[:], op0=ALU.mult, op1=ALU.mult),
                    reads=[tacc[ct], trs, self.t_vec], writes=[tzb[b]])
                p.op("act", lambda e, b=b: e.activation(out=ob[b][:], in_=zb[b][:], func=AF.Silu),
                     reads=[tzb[b]], writes=[tob[b]])
                for c in chs:
                    o = c * 512 - t0
                    p.dma("sp", self.OBR[1, ct * 128:(ct + 1) * 128, c * 512:(c + 1) * 512], ob[b][:, o:o + 512],
                          reads=[tob[b]], swrites=[self.tOB[1][c]], pool="st")
            yield


B.genB = genB


def phaseAB(self, l):
    p = self.p
    NT = self.S // 1024
    with ExitStack() as es:
        gens = [(self.genA(l, es), 4 * NT), (self.genB(l, es), 5 * NT)]
        prog = [0] * len(gens)
        alive = [True] * len(gens)
        while any(alive):
            i = min((k for k in range(len(gens)) if alive[k]), key=lambda k: prog[k] / gens[k][1])
            try:
                next(gens[i][0])
                prog[i] += 1
            except StopIteration:
                alive[i] = False
    p.barrier()


B.phaseAB = phaseAB


def phaseC(self, l):
    p, nc, S = self.p, self.nc, self.S
    NQ = S // 512
    NKT = S // 128
    lam_init = 0.8 - 0.6 * math.exp(-0.3 * l)
    with ExitStack() as es:
        pr16 = self.sb(es, "pr16", [128, 16], F32)
        tpr16 = p.tok("pr16")
        lamt = self.sb(es, "lamt", [128, 4], F32)
        tlam = p.tok("lamt")
        sub2 = self.sb(es, "sub2", [128, 1], F32)
        NP = 2
        pss = [self.ps(es, "pss", [128, 1024]) for _ in range(NP)]
        tpss = p.toks(NP, "pss")
        pO = [self.ps(es, "pO", [128, 512]) for _ in range(2)]
        tpO = p.toks(2, "pO")
        pL = [self.ps(es, "pL", [128, 512]) for _ in range(2)]
        tpL = p.toks(2, "pL")
        pstat = pL[1][:, :]
        tpstat = tpL[1]
        p.op("pool", lambda e: e.memset(pr16[:], 0.0), writes=[tpr16])
        p.op("dve", lambda e: e.tensor_tensor(out=pr16[:, 0:1], in0=self.vcol(l, "lq1"), in1=self.vcol(l, "lk1"),
                                              op=ALU.mult), reads=[self.t_vec], swrites=[tpr16])
        p.op("dve", lambda e: e.tensor_tensor(out=pr16[:, 1:2], in0=self.vcol(l, "lq2"), in1=self.vcol(l, "lk2"),
                                              op=ALU.mult), reads=[self.t_vec], swrites=[tpr16])
        p.op("pe", lambda e: e.matmul(pstat[:, 0:16], lhsT=self.cm[:, 3, :], rhs=pr16[:], start=True, stop=True),
             reads=[tpr16, self.t_cm], writes=[tpstat])
        p.op("act", lambda e: e.activation(out=lamt[:, 0:2], in_=pstat[:, 0:2], func=AF.Exp, scale=128.0),
             reads=[tpstat], writes=[tlam])
        p.op("dve", lambda e: e.tensor_tensor(out=lamt[:, 2:3], in0=lamt[:, 1:2], in1=lamt[:, 0:1], op=ALU.subtract),
             reads=[tlam], writes=[tlam])
        p.op("dve", lambda e: e.tensor_scalar(out=lamt[:, 2:3], in0=lamt[:, 2:3], scalar1=-lam_init, scalar2=None,
                                              op0=ALU.add), reads=[tlam], writes=[tlam])
        p.op("dve", lambda e: e.tensor_scalar(out=lamt[:, 3:4], in0=self.vcol(l, "sub"), scalar1=1.0 - lam_init,
                                              scalar2=None, op0=ALU.mult), reads=[tlam, self.t_vec], writes=[tlam])
        kT = [self.sb(es, "kT", [128, S], BF16) for _ in range(2)]
        qT = [self.sb(es, "qT", [128, S], BF16) for _ in range(2)]
        vT = [self.sb(es, "vT", [128, NKT, 128], BF16) for _ in range(2)]
        thd = p.toks(2, "hd")
        pt = [self.sb(es, "pt", [128, 1024], BF16) for _ in range(3)]
        tpt = p.toks(3, "pt")
        rl = [self.sb(es, "rl", [128, 512], F32) for _ in range(2)]
        trl = p.toks(2, "rl")
        on = [self.sb(es, "on", [128, 512], F32) for _ in range(2)]
        ton = p.toks(2, "on")
        o = self.sb(es, "oc", [128, 512], F32)
        to = p.tok("oc")
        sq = self.sb(es, "sqc", [128, 512], F32)
        tsq = p.tok("sqc")
        rs = self.sb(es, "rsc", [128, 512], F32)
        trs = p.tok("rsc")
        ob = [self.sb(es, "obc", [128, 512], BF16) for _ in range(2)]
        tob = p.toks(2, "obc")
        allqk = list(self.tQK)
        mb = self.sb(es, "maskb", [128, 4, 512], BF16)
        tmb = p.tok("maskb")
        p.op("pool", lambda e: e.memset(mb[:], 0.0), writes=[tmb])
        for j in range(4):
            p.op("pool", lambda e, j=j: e.affine_select(
                out=mb[:, j, :], in_=mb[:, j, :], pattern=[[1, 512]], compare_op=ALU.is_ge,
                fill=self.freg(e, -30000.0), base=-128 * j, channel_multiplier=-1), reads=[tmb], writes=[tmb])

        def load_head(hd):
            b = hd % 2
            rows = slice(hd * 128, (hd + 1) * 128)
            p.dma("sp", kT[b][:], self.CK[rows, :], reads=allqk, writes=[thd[b]])
            p.dma("sp", qT[b][:], self.CQ[rows, :], reads=allqk, swrites=[thd[b]])
            p.dma("sp", vT[b][:], self.CV[:, rows].rearrange("(t p) d -> p t d", p=128), reads=allqk,
                  swrites=[thd[b]])

        load_head(0)
        for hd in range(4):
            b = hd % 2
            if hd + 1 < 4:
                load_head(hd + 1)
            units = [(qc, c, kp) for qc in range(NQ) for c in range(2) for kp in range(2 * qc + 2)]
            base = self._cc
            self._cc += len(units)

            def emit_S(u, idx):
                qc, c, kp = u
                i = idx % NP
                ds = slice(c * 64, (c + 1) * 64)
                qs = slice(qc * 512, (qc + 1) * 512)
                for t in range(2):
                    kt = 2 * kp + t
                    j = kt - 4 * qc
                    p.op("pe", lambda e, i=i, b=b, ds=ds, kt=kt, qs=qs, t=t, j=j: e.matmul(
                        pss[i][:, t * 512:(t + 1) * 512], lhsT=kT[b][ds, kt * 128:(kt + 1) * 128], rhs=qT[b][ds, qs],
                        start=True, stop=(j < 0)), reads=[thd[b]], swrites=[tpss[i]])
                    if j >= 0:
                        p.op("pe", lambda e, i=i, t=t, j=j: e.matmul(
                            pss[i][:, t * 512:(t + 1) * 512], lhsT=self.ident_bf[:], rhs=mb[:, j, :],
                            start=False, stop=True), reads=[tmb, self.t_identb], swrites=[tpss[i]])

            emit_S(units[0], base)
            for n, u in enumerate(units):
                qc, c, kp = u
                idx = base + n
                i = idx % NP
                ip = idx % 3
                nk = 4 * qc + 4
                qs = slice(qc * 512, (qc + 1) * 512)
                p.op("act", lambda e, i=i, ip=ip: e.activation(out=pt[ip][:], in_=pss[i][:], func=AF.Exp, scale=0.125),
                     reads=[tpss[i]], writes=[tpt[ip]])
                if n + 1 < len(units):
                    emit_S(units[n + 1], idx + 1)
                for t in range(2):
                    kt = 2 * kp + t
                    p.op("pe", lambda e, ip=ip, b=b, kt=kt, c=c, nk=nk, t=t: e.matmul(
                        pO[c][:], lhsT=vT[b][:, kt, :], rhs=pt[ip][:, t * 512:(t + 1) * 512], start=(kt == 0),
                        stop=(kt == nk - 1)), reads=[thd[b], tpt[ip]], writes=[tpO[c]])
                    p.op("pe", lambda e, ip=ip, kt=kt, c=c, nk=nk, t=t: e.matmul(
                        pL[c][:], lhsT=self.ones_bf[:], rhs=pt[ip][:, t * 512:(t + 1) * 512], start=(kt == 0),
                        stop=(kt == nk - 1)), reads=[self.t_ones, tpt[ip]], writes=[tpL[c]])
                if 2 * kp + 1 == nk - 1:
                    p.op("dve", lambda e, c=c: e.reciprocal(out=rl[c][:], in_=pL[c][:]), reads=[tpL[c]],
                         writes=[trl[c]])
                    p.op("dve", lambda e, c=c: e.tensor_tensor(out=on[c][:], in0=pO[c][:], in1=rl[c][:], op=ALU.mult),
                         reads=[tpO[c], trl[c]], writes=[ton[c]])
                    if c == 1:
                        ib = (hd * NQ + qc) % 2
                        p.op("dve", lambda e: e.scalar_tensor_tensor(out=o[:], in0=on[1][:], scalar=lamt[:, 2:3],
                                                                     in1=on[0][:], op0=ALU.mult, op1=ALU.add),
                             reads=[ton[0], ton[1], tlam], writes=[to])
                        p.op("pool", lambda e: e.tensor_tensor(out=sq[:], in0=o[:], in1=o[:], op=ALU.mult), reads=[to],
                             writes=[tsq])
                        p.op("pe", lambda e: e.matmul(pstat, lhsT=self.cm[:, 3, :], rhs=sq[:], start=True,
                                                      stop=True), reads=[tsq, self.t_cm], writes=[tpstat])
                        p.op("act", lambda e: e.activation(out=rs[:], in_=pstat, func=AF.Sqrt, bias=self.epsc[:]),
                             reads=[tpstat, self.t_eps], writes=[trs])
                        p.op("dve", lambda e: e.reciprocal(out=rs[:], in_=rs[:]), reads=[trs], writes=[trs])
                        p.op("dve", lambda e, ib=ib: e.scalar_tensor_tensor(out=ob[ib][:], in0=o[:],
                                                                           scalar=lamt[:, 3:4], in1=rs[:],
                                                                           op0=ALU.mult, op1=ALU.mult),
                             reads=[to, trs, tlam], writes=[tob[ib]])
                        p.dma("sp", self.OBR[2, hd * 128:(hd + 1) * 128, qs], ob[ib][:], reads=[tob[ib]],
                              swrites=[self.tOB[2][qc]], pool="st")
    p.barrier()


B.phaseC = phaseC


def phaseD(self, l):
    p, nc, S = self.p, self.nc, self.S
    NQT = S // 128
    NKT = S // 128
    NEG = -1.0e30
    with ExitStack() as es:
        ik4 = self.sb(es, "ik4", [128, S], BF16)
        iqt = [self.sb(es, "iqt", [128, 3, 128], BF16) for _ in range(2)]
        tiqt = p.toks(2, "iqt")
        dkT = self.sb(es, "dkT", [128, S], BF16)
        dvT = self.sb(es, "dvT", [128, NKT, 128], BF16)
        tres = p.tok("dres")
        allqk = list(self.tQK)
        for g in range(3):
            p.dma("sp", ik4[g * 32:(g + 1) * 32, :], self.IK[:, :], reads=allqk, swrites=[tres])
        p.dma("sp", dkT[:], self.DK[:, :], reads=allqk, swrites=[tres])
        p.dma("sp", dvT[:], self.DV.rearrange("(t p) d -> p t d", p=128), reads=allqk, swrites=[tres])
        SC = [self.sb(es, "SC", [128, S], F32) for _ in range(2)]
        tSC = p.toks(2, "SC")
        MK = [self.sb(es, "MK", [128, S], BF16) for _ in range(2)]
        tMK = p.toks(2, "MK")
        wq = [self.sb(es, "wq", [128, 128], F32) for _ in range(2)]
        twq = p.toks(2, "wq")
        dg = [self.sb(es, "dg", [128, 8, 128], BF16) for _ in range(2)]
        tdg = p.toks(2, "dg")
        T = [self.sb(es, "T", [128, 2, 512], BF16) for _ in range(2)]
        tT = p.toks(2, "T")
        st = [self.sb(es, "bst", [128, 8], F32) for _ in range(2)]
        tst = p.toks(2, "bst")
        W2 = [self.sb(es, "W2", [128, NBIS], F32) for _ in range(2)]
        tW2 = p.toks(2, "W2")
        ctab = self.sb(es, "ctab", [128, NBIS], F32)
        tctab = p.tok("ctab")
        for it in range(NBIS):
            p.op("pool", lambda e, it=it: e.memset(ctab[:, it:it + 1], 0.5 ** (it + 1)), swrites=[tctab])
        mTs = [self.sb(es, "mTs", [128, 128], BF16) for _ in range(2)]
        tmTs = p.toks(2, "mTs")
        dq = [self.sb(es, "dq", [128, 4, 128], BF16) for _ in range(2)]
        tdq = p.toks(2, "dq")
        E = [self.sb(es, "E", [128, 4, 128], BF16) for _ in range(2)]
        tE = p.toks(2, "E")
        P = [self.sb(es, "P", [128, 4, 128], BF16) for _ in range(2)]
        tP = p.toks(2, "P")
        rl = self.sb(es, "rld", [128, 512], F32)
        trl = p.tok("rld")
        od = [self.sb(es, "od", [128, 4, 128], BF16) for _ in range(2)]
        tod = p.toks(2, "od")
        pl = [self.ps(es, "pl", [128, 512]) for _ in range(2)]
        tpl = p.toks(2, "pl")
        NL1 = 3
        pl1 = [self.ps(es, "pl1", [128, 512]) for _ in range(NL1)]
        tpl1 = p.toks(NL1, "pl1")
        psc = self.ps(es, "psc", [128, 512])
        tpsc = p.tok("psc")
        mbT = self.sb(es, "mbT", [128, S], BF16)
        tmbT = p.tok("mbT")
        pO = self.ps(es, "pOd", [128, 512])
        tpO = p.tok("pOd")
        pL = self.ps(es, "pLd", [128, 512])
        tpL = p.tok("pLd")
        scale = 128.0 ** -0.5
        ctr = [0, 0]

        def part1(qt):
            qb = qt % 2
            qs = slice(qt * 128, (qt + 1) * 128)
            L = 128 * (qt + 1)
            nkc = (L + 511) // 512
            p.dma("sp", wq[qb][:], self.IW[qs, :], reads=[self.tQK[qt // 4]], writes=[twq[qb]])
            p.dma("sp", iqt[qb][0:96, 0, :], self.IQ[0:96, qs], reads=[self.tQK[qt // 4]], writes=[tiqt[qb]])
            p.dma("sp", iqt[qb][0:96, 1, :], self.IQ[96:192, qs], reads=[self.tQK[qt // 4]], swrites=[tiqt[qb]])
            p.dma("sp", iqt[qb][0:64, 2, :], self.IQ[192:256, qs], reads=[self.tQK[qt // 4]], swrites=[tiqt[qb]])
            for h in range(8):
                p.op("pool", lambda e, h=h, qb=qb: e.tensor_scalar(out=dg[qb][:, h, :], in0=self.ident_bf[:],
                                                                   scalar1=wq[qb][:, h:h + 1], scalar2=1.0,
                                                                   op0=ALU.mult, op1=ALU.mult),
                     reads=[twq[qb], self.t_identb], swrites=[tdg[qb]])
            subs = [(kc, h) for kc in range(nkc) for h in range(8)]
            x0 = ctr[0]
            ctr[0] += len(subs)

            def logits(n):
                kc, h = subs[n]
                x = (x0 + n) % NL1
                N = min(512, L - kc * 512)
                ks = slice(kc * 512, kc * 512 + N)
                g, r = h // 3, (h % 3) * 32
                p.op("pe", lambda e, x=x, g=g, r=r, ks=ks, N=N, qb=qb: e.matmul(
                    pl1[x][:, 0:N], lhsT=iqt[qb][r:r + 32, g, :], rhs=ik4[r:r + 32, ks],
                    start=True, stop=True), reads=[tres, tiqt[qb]], writes=[tpl1[x]])

            logits(0)
            if len(subs) > 1:
                logits(1)
            for n, (kc, h) in enumerate(subs):
                x = (x0 + n) % NL1
                xt = (x0 + n) % 2
                N = min(512, L - kc * 512)
                ks = slice(kc * 512, kc * 512 + N)
                p.op("act", lambda e, x=x, xt=xt, N=N: e.activation(out=T[xt][:, 0, 0:N], in_=pl1[x][:, 0:N],
                                                                   func=AF.Relu),
                     reads=[tpl1[x]], writes=[tT[xt]])
                if n + 2 < len(subs):
                    logits(n + 2)
                p.op("pe", lambda e, xt=xt, h=h, N=N, qb=qb: e.matmul(
                    psc[:, 0:N], lhsT=dg[qb][:, h, :], rhs=T[xt][:, 0, 0:N], start=(h == 0), stop=(h == 7)),
                    reads=[tdg[qb], tT[xt]], writes=[tpsc])
                if h == 7:
                    p.op("act", lambda e, ks=ks, N=N, qb=qb: e.copy(out=SC[qb][:, ks], in_=psc[:, 0:N]), reads=[tpsc],
                         swrites=[tSC[qb]])
                if h % 2 == 1:
                    yield

        def thresh(qt):
            qb = qt % 2
            L = 128 * (qt + 1)
            sc, mk, s_, ts_ = SC[qb], MK[qb], st[qb], tst[qb]
            w2 = W2[qb]
            if L > TOPK:
                p.op("dve", lambda e: e.tensor_reduce(out=s_[:, 1:2], in_=sc[:, 0:L], axis=AX.X, op=ALU.max),
                     reads=[tSC[qb]], writes=[ts_])
                p.op("dve", lambda e: e.tensor_reduce(out=s_[:, 0:1], in_=sc[:, 0:TOPK], axis=AX.X, op=ALU.min),
                     reads=[tSC[qb]], writes=[ts_])
                p.op("dve", lambda e: e.tensor_tensor(out=s_[:, 2:3], in0=s_[:, 1:2], in1=s_[:, 0:1],
                                                      op=ALU.subtract), reads=[ts_], writes=[ts_])
                p.op("dve", lambda e: e.tensor_scalar(out=w2[:], in0=ctab[:], scalar1=s_[:, 2:3], scalar2=None,
                                                      op0=ALU.mult), reads=[ts_, tctab], writes=[tW2[qb]])
                p.op("dve", lambda e: e.tensor_tensor(out=s_[:, 3:4], in0=s_[:, 0:1], in1=w2[:, 0:1], op=ALU.add),
                     reads=[ts_, tW2[qb]], writes=[ts_])
            else:
                p.op("dve", lambda e: e.memset(s_[:, 0:1], -1.0e29), writes=[ts_])
            p.op("pool", lambda e: e.affine_select(
                out=sc[:, qt * 128:(qt + 1) * 128], in_=sc[:, qt * 128:(qt + 1) * 128], pattern=[[-1, 128]],
                compare_op=ALU.is_ge, fill=self.freg(e, NEG), base=0, channel_multiplier=1),
                reads=[tSC[qb]], writes=[tSC[qb]])
            if L > TOPK:
                for it in range(NBIS):
                    p.op("dve", lambda e: e.tensor_scalar(out=mk[:, 0:L], in0=sc[:, 0:L], scalar1=s_[:, 3:4],
                                                          scalar2=None, op0=ALU.is_ge, op1=ALU.add,
                                                          accum_out=s_[:, 4:5]),
                         reads=[ts_, tSC[qb]], writes=[ts_], swrites=[tMK[qb]])
                    p.op("dve", lambda e: e.tensor_scalar(out=s_[:, 5:6], in0=s_[:, 4:5], scalar1=TOPK - 0.5,
                                                          scalar2=-0.5, op0=ALU.is_ge, op1=ALU.add),
                         reads=[ts_], writes=[ts_])
                    p.op("dve", lambda e, it=it: e.scalar_tensor_tensor(out=s_[:, 3:4], in0=s_[:, 5:6],
                                                                       scalar=w2[:, it:it + 1], in1=s_[:, 3:4],
                                                                       op0=ALU.mult, op1=ALU.add),
                         reads=[ts_, tW2[qb]], writes=[ts_])
                    yield
                p.op("dve", lambda e: e.scalar_tensor_tensor(out=s_[:, 0:1], in0=w2[:, NBIS - 1:NBIS], scalar=-0.5,
                                                             in1=s_[:, 3:4], op0=ALU.mult, op1=ALU.add),
                     reads=[ts_, tW2[qb]], writes=[ts_])
            p.op("dve", lambda e: e.tensor_scalar(out=mk[:, 0:L], in0=sc[:, 0:L], scalar1=s_[:, 0:1],
                                                  scalar2=None, op0=ALU.is_ge),
                 reads=[ts_, tSC[qb]], writes=[tMK[qb]])

        def part2(qt):
            qb = qt % 2
            qs = slice(qt * 128, (qt + 1) * 128)
            nkt = qt + 1
            p.dma("sp", dq[qb][:], self.DQ[:, qs].rearrange("(h d) q -> d h q", d=128),
                  reads=[self.tQK[qt // 4]], writes=[tdq[qb]])
            for gi, g0 in enumerate(range(0, nkt, 8)):
                n8 = min(8, nkt - g0)
                zz = gi % 2
                pmv = pl[zz][:, 0:512].bitcast(BF16)
                for t in range(n8):
                    kt = g0 + t
                    p.op("pe", lambda e, kt=kt, t=t, qb=qb, pmv=pmv: e.transpose(
                        out=pmv[:, t * 128:(t + 1) * 128], in_=MK[qb][:, kt * 128:(kt + 1) * 128],
                        identity=self.ident_bf[:]), reads=[tMK[qb], self.t_identb], swrites=[tpl[zz]])
                p.op("act", lambda e, g0=g0, n8=n8, pmv=pmv: e.activation(
                    out=mbT[:, g0 * 128:(g0 + n8) * 128], in_=pmv[:, 0:n8 * 128], func=AF.Identity,
                    scale=30000.0, bias=-30000.0), reads=[tpl[zz]], swrites=[tmbT])
                yield
            z0 = ctr[1]
            ctr[1] += nkt

            def front(kt):
                z = (z0 + kt) % 2
                kts = slice(kt * 128, (kt + 1) * 128)
                p.op("pe", lambda e, z=z, kts=kts, qb=qb: e.matmul(
                    pl[z][:, 0:512], lhsT=dkT[:, kts], rhs=dq[qb][:].rearrange("p h q -> p (h q)"), start=True,
                    stop=False), reads=[tres, tdq[qb]], writes=[tpl[z]])
                p.op("pe", lambda e, z=z, kts=kts: e.matmul(
                    pl[z][:, 0:512].rearrange("p (h q) -> p h q", h=4), lhsT=self.ident_bf[:],
                    rhs=mbT[:, kts].rearrange("p (o q) -> p o q", o=1).to_broadcast([128, 4, 128]),
                    start=False, stop=True), reads=[tmbT, self.t_identb], writes=[tpl[z]])

            front(0)
            for kt in range(nkt):
                z = (z0 + kt) % 2
                p.op("act", lambda e, z=z: e.activation(out=P[z][:].rearrange("p h q -> p (h q)"), in_=pl[z][:, 0:512],
                                                        func=AF.Exp, scale=scale),
                     reads=[tpl[z]], writes=[tP[z]])
                if kt + 1 < nkt:
                    front(kt + 1)
                p.op("pe", lambda e, z=z, kt=kt, qt=qt: e.matmul(
                    pO[:], lhsT=dvT[:, kt, :], rhs=P[z][:].rearrange("p h q -> p (h q)"), start=(kt == 0),
                    stop=(kt == qt)), reads=[tres, tP[z]], writes=[tpO])
                p.op("pe", lambda e, z=z, kt=kt, qt=qt: e.matmul(
                    pL[:], lhsT=self.ones_bf[:], rhs=P[z][:].rearrange("p h q -> p (h q)"), start=(kt == 0),
                    stop=(kt == qt)), reads=[self.t_ones, tP[z]], writes=[tpL])
                yield
            p.op("dve", lambda e: e.reciprocal(out=rl[:], in_=pL[:]), reads=[tpL], writes=[trl])
            p.op("dve", lambda e, qb=qb: e.tensor_tensor(out=od[qb][:].rearrange("p h q -> p (h q)"), in0=pO[:],
                                                         in1=rl[:], op=ALU.mult),
                 reads=[tpO, trl], writes=[tod[qb]])
            p.dma("sp", self.OBR[3, :, qs].rearrange("(h d) q -> d h q", d=128), od[qb][:], reads=[tod[qb]],
                  swrites=[self.tOB[3][qt // 4]], pool="st")

        def merge(gens):
            prog = [0] * len(gens)
            alive = [True] * len(gens)
            while any(alive):
                i = min((k for k in range(len(gens)) if alive[k]), key=lambda k: prog[k] / gens[k][1])
                try:
                    next(gens[i][0])
                    prog[i] += 1
                except StopIteration:
                    alive[i] = False

        def ensure_gen(g):
            return g

        for stage in range(NQT + 2):
            gens = []
            t1, t2, t3 = stage, stage - 1, stage - 2
            if 0 <= t1 < NQT:
                gens.append((part1(t1), ((128 * (t1 + 1) + 511) // 512) * 4 + 1))
            if 0 <= t2 < NQT:
                gens.append((thresh(t2), NBIS + 1))
            if 0 <= t3 < NQT:
                gens.append((part2(t3), t3 + 2 + (t3 + 8) // 8))
            merge(gens)
    p.barrier()


B.phaseD = phaseD


def phase3(self, l):
    p, nc, S = self.p, self.nc, self.S
    hin = self.xT if l == 0 else self.H
    with ExitStack() as es:
        Wg = self.sb(es, "Wg", [128, 32, D], BF16)
        tWg = p.toks(32, "Wg")
        Wb = self.sb(es, "Wb", [128, 16, D], BF16)
        tWb = p.toks(16, "Wb")
        Wo = self.sb(es, "Wo", [128, 8, D], BF16)
        tWo = p.toks(8, "Wo")
        with ExitStack() as es2:
            stg = [self.sb(es2, "p3stg", [128, D], F32) for _ in range(2)]
            tstg = p.toks(2, "p3stg")
            for j in range(4):
                self.load_weight(Wg[:, j * 8:(j + 1) * 8, :], tWg[j * 8:(j + 1) * 8], self.w_gate[l, j], 8, D, stg,
                                 tstg, gain=lambda kc: self.vcol(l, "gmix", kc))
                self.load_weight(Wb[:, j * 4:(j + 1) * 4, :], tWb[j * 4:(j + 1) * 4], self.w_branch[l, j], 4, D, stg,
                                 tstg)
            self.load_weight(Wo, tWo, self.w_out[l], 8, D, stg, tstg)
            p.barrier()
        xn = [self.sb(es, "xn3", [128, 8, 512], BF16) for _ in range(2)]
        txn = p.toks(2, "xn3")
        obr = [self.sb(es, "obr3", [128, 16, 512], BF16) for _ in range(2)]
        tobr = p.toks(2, "obr3")
        hb = self.sb(es, "hb3", [128, 8, 512], F32)
        thb = p.tok("hb3")
        mixed = self.sb(es, "mixed", [128, 8, 512], BF16)
        tmixed = p.toks(8, "mixed")
        sg = [self.sb(es, "sg3", [128, 512], F32) for _ in range(2)]
        tsg = p.toks(2, "sg3")
        pr = [self.sb(es, "pr3", [128, 512], F32) for _ in range(2)]
        tpr = p.toks(2, "pr3")
        mix = self.sb(es, "mix3", [128, 512], F32)
        tmix = p.tok("mix3")
        pg = [self.ps(es, "pg", [128, 512]) for _ in range(2)]
        tpg = p.toks(2, "pg")
        pp = [self.ps(es, "pp", [128, 512]) for _ in range(2)]
        tpp = p.toks(2, "pp")
        po = [self.ps(es, "po", [128, 512]) for _ in range(2)]
        tpo = p.toks(2, "po")

        def loads(c):
            b = c % 2
            sl = slice(c * 512, (c + 1) * 512)
            p.dma("sp", xn[b][:], self.XN[:, sl].rearrange("(k p) t -> p k t", p=128), reads=[self.tXN[c]],
                  writes=[txn[b]])
            for j in range(4):
                p.dma("sp", obr[b][:, j * 4:(j + 1) * 4, :],
                      self.OBR[j, :, sl].rearrange("(k p) t -> p k t", p=128), reads=[self.tOB[j][c]],
                      swrites=[tobr[b]])

        loads(0)
        cnt = 0
        for c in range(self.NCH):
            b = c % 2
            sl = slice(c * 512, (c + 1) * 512)
            if c + 1 < self.NCH:
                loads(c + 1)
            rd = [self.tH[c]] if l > 0 else []
            p.dma("sp", hb[:], hin[:, sl].rearrange("(k p) t -> p k t", p=128), reads=rd, writes=[thb])
            for m in range(8):
                ms = slice(m * 128, (m + 1) * 128)
                for j in range(4):
                    i = cnt % 2
                    cnt += 1
                    for kc in range(8):
                        p.op("pe", lambda e, i=i, j=j, kc=kc, ms=ms, b=b: e.matmul(
                            pg[i][:], lhsT=Wg[:, j * 8 + kc, ms], rhs=xn[b][:, kc, :], start=(kc == 0), stop=(kc == 7)),
                            reads=[tWg[j * 8 + kc], txn[b]], writes=[tpg[i]])
                    for kc in range(4):
                        p.op("pe", lambda e, i=i, j=j, kc=kc, ms=ms, b=b: e.matmul(
                            pp[i][:], lhsT=Wb[:, j * 4 + kc, ms], rhs=obr[b][:, j * 4 + kc, :], start=(kc == 0),
                            stop=(kc == 3)), reads=[tWb[j * 4 + kc], tobr[b]], writes=[tpp[i]])
                    p.op("act", lambda e, i=i: e.activation(out=sg[i][:], in_=pg[i][:], func=AF.Sigmoid),
                         reads=[tpg[i]], writes=[tsg[i]])
                    if j == 0:
                        p.op("dve", lambda e, i=i: e.tensor_tensor(out=mix[:], in0=pp[i][:], in1=sg[i][:], op=ALU.mult),
                             reads=[tpp[i], tsg[i]], writes=[tmix])
                    else:
                        p.op("dve", lambda e, i=i: e.tensor_tensor(out=pr[i][:], in0=pp[i][:], in1=sg[i][:],
                                                                   op=ALU.mult),
                             reads=[tpp[i], tsg[i]], writes=[tpr[i]])
                        if j < 3:
                            p.op("pool", lambda e, i=i: e.tensor_tensor(out=mix[:], in0=mix[:], in1=pr[i][:],
                                                                        op=ALU.add),
                                 reads=[tmix, tpr[i]], writes=[tmix])
                        else:
                            p.op("pool", lambda e, i=i, m=m: e.tensor_tensor(out=mixed[:, m, :], in0=mix[:],
                                                                             in1=pr[i][:], op=ALU.add),
                                 reads=[tmix, tpr[i]], writes=[tmixed[m]])
            for m2 in range(8):
                i = m2 % 2
                ms = slice(m2 * 128, (m2 + 1) * 128)
                for m in range(8):
                    p.op("pe", lambda e, i=i, m=m, ms=ms: e.matmul(po[i][:], lhsT=Wo[:, m, ms], rhs=mixed[:, m, :],
                                                                  start=(m == 0), stop=(m == 7)),
                         reads=[tWo[m], tmixed[m]], writes=[tpo[i]])
                p.op("dve", lambda e, i=i, m2=m2: e.tensor_tensor(out=hb[:, m2, :], in0=hb[:, m2, :], in1=po[i][:],
                                                                  op=ALU.add),
                     reads=[tpo[i]], swrites=[thb])
            p.dma("sp", self.H[:, sl].rearrange("(k p) t -> p k t", p=128), hb[:], reads=[thb],
                  writes=[self.tH[c]], pool="st")
    p.barrier()


B.phase3 = phase3


def phase4(self, l):
    p, nc, S = self.p, self.nc, self.S
    NF = 44
    with ExitStack() as es:
        Wu = self.sb(es, "Wu", [128, 8, 2 * DFF], BF16)
        tWu = p.toks(8, "Wu")
        Wd = self.sb(es, "Wd", [128, 22, D], BF16)
        tWd = p.toks(22, "Wd")
        with ExitStack() as es2:
            stg = [self.sb(es2, "p4stg", [128, 2 * DFF], F32) for _ in range(2)]
            tstg = p.toks(2, "p4stg")
            self.load_weight(Wu, tWu, self.w_up[l], 8, 2 * DFF, stg, tstg, gain=lambda kc: self.vcol(l, "gffn", kc))
            self.load_weight(Wd, tWd, self.w_down[l], 22, D, stg, tstg)
            p.barrier()
        hb = self.sb(es, "hb4", [128, 8, 512], F32)
        thb = p.tok("hb4")
        sq = [self.sb(es, "sq4", [128, 512], F32) for _ in range(2)]
        tsq = p.toks(2, "sq4")
        rstd = self.sb(es, "rstd4", [128, 512], F32)
        trstd = p.tok("rstd4")
        xf = self.sb(es, "xf", [128, 8, 512], BF16)
        txf = p.tok("xf")
        F = [self.sb(es, "F4", [128, 514], F32) for _ in range(2)]
        tF = p.toks(2, "F4")
        y = [self.sb(es, "y4", [128, 512], F32) for _ in range(4)]
        ty = p.toks(4, "y4")
        A = self.sb(es, "A4", [128, 22, 512], BF16)
        tA = p.toks(22, "A4")
        carry = [self.sb(es, "carry", [128, NF, 2], F32) for _ in range(2)]
        tcar = [p.toks(NF, "carry") for _ in range(2)]
        pst = self.ps(es, "pst4", [128, 512])
        tpst = p.tok("pst4")
        pu = [self.ps(es, "pu", [128, 512]) for _ in range(3)]
        tpu = p.toks(3, "pu")
        pd = [self.ps(es, "pd", [128, 512]) for _ in range(2)]
        tpd = p.toks(2, "pd")
        p.op("pool", lambda e: e.memset(carry[0][:], 0.0), writes=tcar[0])
        cnt = [0, 0]
        for c in range(self.NCH):
            sl = slice(c * 512, (c + 1) * 512)
            p.dma("sp", hb[:], self.H[:, sl].rearrange("(k p) t -> p k t", p=128), reads=[self.tH[c]], writes=[thb])
            for kc in range(8):
                i = kc % 2
                p.op("act", lambda e, i=i, kc=kc: e.activation(out=sq[i][:], in_=hb[:, kc, :], func=AF.Square),
                     reads=[thb], writes=[tsq[i]])
                p.op("pe", lambda e, i=i, kc=kc: e.matmul(pst[:], lhsT=self.cm[:, 1, :], rhs=sq[i][:], start=(kc == 0),
                                                         stop=(kc == 7)),
                     reads=[tsq[i], self.t_cm], writes=[tpst])
            p.op("act", lambda e: e.activation(out=rstd[:], in_=pst[:], func=AF.Sqrt, bias=self.epsc[:]),
                 reads=[tpst, self.t_eps], writes=[trstd])
            p.op("dve", lambda e: e.reciprocal(out=rstd[:], in_=rstd[:]), reads=[trstd], writes=[trstd])
            for kc in range(8):
                eng = "dve" if kc % 2 == 0 else "pool"
                p.op(eng, lambda e, kc=kc: e.tensor_tensor(out=xf[:, kc, :], in0=hb[:, kc, :], in1=rstd[:], op=ALU.mult),
                     reads=[thb, trstd], swrites=[txf])

            def up_tile(ft, yb):
                i = cnt[0] % 3
                cnt[0] += 1
                fb = cnt[1] % 2
                cnt[1] += 1
                cs = slice(ft * 128, (ft + 1) * 128)
                for kc in range(8):
                    p.op("pe", lambda e, i=i, kc=kc, cs=cs: e.matmul(pu[i][:], lhsT=Wu[:, kc, cs], rhs=xf[:, kc, :],
                                                                    start=(kc == 0), stop=(kc == 7)),
                         reads=[tWu[kc], txf], writes=[tpu[i]])
                w = lambda k: self.vcol(l, "cfw", k * NF + ft)
                cin, cout = carry[c % 2], carry[(c + 1) % 2]
                tcin, tcout = tcar[c % 2], tcar[(c + 1) % 2]
                p.op("act", lambda e, fb=fb, ft=ft, cin=cin: e.copy(out=F[fb][:, 0:2], in_=cin[:, ft, :]),
                     reads=[tcin[ft]], writes=[tF[fb]])
                p.op("act", lambda e, fb=fb, i=i: e.copy(out=F[fb][:, 2:514], in_=pu[i][:]), reads=[tpu[i]],
                     swrites=[tF[fb]])
                p.op("act", lambda e, i=i, ft=ft, cout=cout: e.copy(out=cout[:, ft, :], in_=pu[i][:, 510:512]),
                     reads=[tpu[i]], writes=[tcout[ft]])
                p.op("act", lambda e, i=i, yb=yb, w2=w(2), bb=self.vcol(l, "cfb", ft): e.activation(
                    out=y[yb][:], in_=pu[i][:], func=AF.Identity, scale=w2, bias=bb),
                    reads=[tpu[i], self.t_vec], writes=[ty[yb]])
                for k in (1, 0):
                    p.op("dve", lambda e, fb=fb, yb=yb, k=k, wk=w(k): e.scalar_tensor_tensor(
                        out=y[yb][:], in0=F[fb][:, k:k + 512], scalar=wk, in1=y[yb][:], op0=ALU.mult, op1=ALU.add),
                        reads=[tF[fb], ty[yb], self.t_vec], writes=[ty[yb]])

            for j in range(22):
                yg = (2 * j) % 4
                yv = (2 * j + 1) % 4
                up_tile(j, yg)
                up_tile(j + 22, yv)
                p.op("act", lambda e, yg=yg: e.activation(out=y[yg][:], in_=y[yg][:], func=AF.Silu),
                     reads=[ty[yg]], writes=[ty[yg]])
                p.op("pool", lambda e, yg=yg, yv=yv, j=j: e.tensor_tensor(out=A[:, j, :], in0=y[yg][:], in1=y[yv][:],
                                                                         op=ALU.mult),
                     reads=[ty[yg], ty[yv]], writes=[tA[j]])
            for m2 in range(8):
                i = m2 % 2
                ms = slice(m2 * 128, (m2 + 1) * 128)
                for j in range(22):
                    p.op("pe", lambda e, i=i, j=j, ms=ms: e.matmul(pd[i][:], lhsT=Wd[:, j, ms], rhs=A[:, j, :],
                                                                  start=(j == 0), stop=(j == 21)),
                         reads=[tWd[j], tA[j]], writes=[tpd[i]])
                p.op("dve", lambda e, i=i, m2=m2: e.tensor_tensor(out=hb[:, m2, :], in0=hb[:, m2, :], in1=pd[i][:],
                                                                  op=ALU.add),
                     reads=[tpd[i]], swrites=[thb])
            p.dma("sp", self.H[:, sl].rearrange("(k p) t -> p k t", p=128), hb[:], reads=[thb],
                  writes=[self.tH[c]], pool="st")
    p.barrier()
    self.phase5(l)


B.phase4 = phase4


def phase5(self, l):
    p, nc, S = self.p, self.nc, self.S
    last = (l == self.L - 1)
    dst = self.outT if last else self.H
    with ExitStack() as es:
        Wpg = self.sb(es, "Wpg", [128, 8, D], BF16)
        tWpg = p.toks(8, "Wpg")
        Wpi = self.sb(es, "Wpi", [128, 2, D], BF16)
        tWpi = p.toks(2, "Wpi")
        stg = [self.sb(es, "p5stg", [128, D], F32) for _ in range(2)]
        tstg = p.toks(2, "p5stg")
        self.load_weight(Wpg, tWpg, self.w_pg[l], 8, D, stg, tstg, gain=lambda kc: self.vcol(l, "gpg", kc))
        self.load_weight(Wpi, tWpi, self.w_pi[l], 2, D, stg, tstg)
        hb = [self.sb(es, "hb5", [128, 8, 512], F32) for _ in range(2)]
        thb = p.toks(2, "hb5")
        pb = [self.sb(es, "pb5", [128, 2, 512], F32) for _ in range(2)]
        tpb = p.toks(2, "pb5")
        pbb = self.sb(es, "pbb5", [128, 2, 512], BF16)
        tpbb = p.tok("pbb5")
        sq = [self.sb(es, "sq5", [128, 512], F32) for _ in range(2)]
        tsq = p.toks(2, "sq5")
        rstd = self.sb(es, "rstd5", [128, 512], F32)
        trstd = p.tok("rstd5")
        rse = self.sb(es, "rse5", [128, 512], F32)
        trse = p.tok("rse5")
        xg = self.sb(es, "xg5", [128, 8, 512], BF16)
        txg = p.tok("xg5")
        ee = self.sb(es, "ee5", [128, 8, 512], F32)
        tee = p.toks(8, "ee5")
        sg = [self.sb(es, "sg5", [128, 512], F32) for _ in range(2)]
        tsg = p.toks(2, "sg5")
        t1 = [self.sb(es, "t15", [128, 512], F32) for _ in range(2)]
        tt1 = p.toks(2, "t15")
        pst = self.ps(es, "pst5", [128, 512])
        tpst = p.tok("pst5")
        pse = self.ps(es, "pse5", [128, 512])
        tpse = p.tok("pse5")
        pe_ = [self.ps(es, "pe5", [128, 512]) for _ in range(2)]
        tpe = p.toks(2, "pe5")
        pg = [self.ps(es, "pg5", [128, 512]) for _ in range(2)]
        tpg = p.toks(2, "pg5")

        def loads(c):
            b = c % 2
            sl = slice(c * 512, (c + 1) * 512)
            p.dma("sp", hb[b][:], self.H[:, sl].rearrange("(k p) t -> p k t", p=128), reads=[self.tH[c]],
                  writes=[thb[b]])
            p.dma("sp", pb[b][:], self.pT[l, :, sl].rearrange("(k p) t -> p k t", p=128), writes=[tpb[b]])

        loads(0)
        for c in range(self.NCH):
            b = c % 2
            sl = slice(c * 512, (c + 1) * 512)
            if c + 1 < self.NCH:
                loads(c + 1)
            hbb = hb[b]
            for kc in range(8):
                i = kc % 2
                p.op("act", lambda e, i=i, kc=kc, hbb=hbb: e.activation(out=sq[i][:], in_=hbb[:, kc, :], func=AF.Square),
                     reads=[thb[b]], writes=[tsq[i]])
                p.op("pe", lambda e, i=i, kc=kc: e.matmul(pst[:], lhsT=self.cm[:, 1, :], rhs=sq[i][:], start=(kc == 0),
                                                         stop=(kc == 7)),
                     reads=[tsq[i], self.t_cm], writes=[tpst])
            p.op("act", lambda e: e.activation(out=rstd[:], in_=pst[:], func=AF.Sqrt, bias=self.epsc[:]),
                 reads=[tpst, self.t_eps], writes=[trstd])
            p.op("dve", lambda e: e.reciprocal(out=rstd[:], in_=rstd[:]), reads=[trstd], writes=[trstd])
            for kc in range(8):
                eng = "dve" if kc % 2 == 0 else "pool"
                p.op(eng, lambda e, kc=kc, hbb=hbb: e.tensor_tensor(out=xg[:, kc, :], in0=hbb[:, kc, :], in1=rstd[:],
                                                                   op=ALU.mult),
                     reads=[thb[b], trstd], swrites=[txg])
            p.op("pool", lambda e, b=b: e.tensor_copy(out=pbb[:], in_=pb[b][:]), reads=[tpb[b]], writes=[tpbb])
            for m in range(8):
                i = m % 2
                ms = slice(m * 128, (m + 1) * 128)
                for kc in range(2):
                    p.op("pe", lambda e, i=i, kc=kc, ms=ms: e.matmul(pe_[i][:], lhsT=Wpi[:, kc, ms], rhs=pbb[:, kc, :],
                                                                    start=(kc == 0), stop=(kc == 1)),
                         reads=[tWpi[kc], tpbb], writes=[tpe[i]])
                p.op("act", lambda e, i=i, m=m: e.copy(out=ee[:, m, :], in_=pe_[i][:]), reads=[tpe[i]], writes=[tee[m]])
                p.op("pool", lambda e, i=i, m=m: e.tensor_tensor(out=sq[i][:], in0=ee[:, m, :], in1=ee[:, m, :],
                                                                 op=ALU.mult),
                     reads=[tee[m]], writes=[tsq[i]])
                p.op("pe", lambda e, i=i, m=m: e.matmul(pse[:], lhsT=self.cm[:, 1, :], rhs=sq[i][:], start=(m == 0),
                                                       stop=(m == 7)),
                     reads=[tsq[i], self.t_cm], writes=[tpse])
            p.op("act", lambda e: e.activation(out=rse[:], in_=pse[:], func=AF.Sqrt, bias=self.epsc[:]),
                 reads=[tpse, self.t_eps], writes=[trse])
            p.op("dve", lambda e: e.reciprocal(out=rse[:], in_=rse[:]), reads=[trse], writes=[trse])
            for m in range(8):
                i = m % 2
                ms = slice(m * 128, (m + 1) * 128)
                for kc in range(8):
                    p.op("pe", lambda e, i=i, kc=kc, ms=ms: e.matmul(pg[i][:], lhsT=Wpg[:, kc, ms], rhs=xg[:, kc, :],
                                                                    start=(kc == 0), stop=(kc == 7)),
                         reads=[tWpg[kc], txg], writes=[tpg[i]])
                p.op("act", lambda e, i=i: e.activation(out=sg[i][:], in_=pg[i][:], func=AF.Sigmoid),
                     reads=[tpg[i]], writes=[tsg[i]])
                p.op("dve", lambda e, i=i, m=m, g=self.vcol(l, "gple", m): e.scalar_tensor_tensor(
                    out=t1[i][:], in0=ee[:, m, :], scalar=g, in1=rse[:], op0=ALU.mult, op1=ALU.mult),
                    reads=[tee[m], trse, self.t_vec], writes=[tt1[i]])
                p.op("pool", lambda e, i=i: e.tensor_tensor(out=t1[i][:], in0=t1[i][:], in1=sg[i][:], op=ALU.mult),
                     reads=[tt1[i], tsg[i]], writes=[tt1[i]])
                p.op("dve", lambda e, i=i, m=m, hbb=hbb: e.tensor_tensor(out=hbb[:, m, :], in0=hbb[:, m, :],
                                                                        in1=t1[i][:], op=ALU.add),
                     reads=[tt1[i], txg], swrites=[thb[b]])
            p.dma("sp", dst[:, sl].rearrange("(k p) t -> p k t", p=128), hbb[:], reads=[thb[b]],
                  writes=[self.tH[c]], pool="st")
    p.barrier()


B.phase5 = phase5
```

```python
import math
from contextlib import ExitStack

import numpy as np
import concourse.bass as bass
import concourse.mybir as mybir
from concourse.bass_utils import run_bass_kernel_spmd

F32 = mybir.dt.float32
BF16 = mybir.dt.bfloat16
I32 = mybir.dt.int32
AF = mybir.ActivationFunctionType
ALU = mybir.AluOpType
AX = mybir.AxisListType

ENGS = ("pe", "act", "dve", "pool", "sp")

D = 1024
DB = 512
DIN = 4648
DFF = 2816
PLE = 256
EPS = 1e-6
TOPK = 256
NBIS = 12


class Tok:
    __slots__ = ("name", "W", "R", "prev")

    def __init__(self, name=""):
        self.name = name
        self.W = set()
        self.R = set()
        self.prev = set()


class Ins:
    __slots__ = ("eng", "fn", "deps", "dsem", "needs_inc", "val", "waits")

    def __init__(self, eng, fn, deps, dsem):
        self.eng = eng
        self.fn = fn
        self.deps = deps
        self.dsem = dsem
        self.needs_inc = False
        self.val = None
        self.waits = None


class Prog:
    DMA_POOL = 8

    def __init__(self, nc):
        self.nc = nc
        self.ins = []
        self.dpool = {}
        self.last = {}
        self.bar = {}

    def tok(self, name=""):
        return Tok(name)

    def toks(self, n, name=""):
        return [Tok(f"{name}{i}") for i in range(n)]

    def _deps(self, eng, reads, writes, swrites=()):
        deps = set()
        idx = len(self.ins)
        for t in reads:
            deps |= t.W
            t.R.add(idx)
        for t in writes:
            deps |= t.W
            deps |= t.R
            deps |= t.prev
            t.W = {idx}
            t.R = set()
            t.prev = {idx}
        for t in swrites:
            if t.R:
                t.prev = t.R | t.W
                t.W = set()
                t.R = set()
            deps |= t.prev
            t.W.add(idx)
        deps.discard(idx)
        if eng in self.bar:
            deps |= self.bar.pop(eng)
        self.last[eng] = idx
        return deps

    def barrier(self):
        b = set(self.last.values())
        for hist in self.dpool.values():
            b |= set(hist[-self.DMA_POOL:])
        for e in ENGS:
            self.bar[e] = set(b) | self.bar.get(e, set())

    def op(self, eng, fn, reads=(), writes=(), swrites=()):
        deps = self._deps(eng, reads, writes, swrites)
        self.ins.append(Ins(eng, fn, deps, None))

    def dma(self, eng, out, in_, reads=(), writes=(), swrites=(), pool="ld", slow=False):
        deps = self._deps(eng, reads, writes, swrites)
        hist = self.dpool.setdefault(pool, [])
        i = len(hist)
        if i >= self.DMA_POOL:
            deps.add(hist[i - self.DMA_POOL])
        hist.append(len(self.ins))
        if slow:
            fn = lambda e: e.dma_start(out=out, in_=in_, allow_slow_non_contiguous=True)
        else:
            fn = lambda e: e.dma_start(out=out, in_=in_)
        self.ins.append(Ins(eng, fn, deps, f"{pool}{i % self.DMA_POOL}"))

    def build(self, final_pools=("st",)):
        ins = self.ins
        n = len(ins)

        def skip(p, it):
            return p.eng == "pe" and it.eng == "pe" and p.dsem is None and it.dsem is None

        for it in ins:
            for d in it.deps:
                p = ins[d]
                if not skip(p, it):
                    p.needs_inc = True
        final_ids = []
        for pl in final_pools:
            final_ids += self.dpool.get(pl, [])[-self.DMA_POOL:]
        cnt = {}
        for it in ins:
            if it.dsem is not None:
                key = "D_" + it.dsem
                cnt[key] = cnt.get(key, 0) + 16
                it.val = (key, cnt[key])
            elif it.needs_inc:
                key = "E_" + it.eng
                cnt[key] = cnt.get(key, 0) + 1
                it.val = (key, cnt[key])
        known = {e: {} for e in ENGS}
        evclock = {}
        nwaits = 0
        for it in ins:
            kn = known[it.eng]
            need = {}
            for d in it.deps:
                p = ins[d]
                if p.val is None or skip(p, it):
                    continue
                s, v = p.val
                if kn.get(s, 0) >= v:
                    continue
                if need.get(s, 0) < v:
                    need[s] = v
            waits = []
            for s, v in sorted(need.items(), key=lambda kv: -kv[1]):
                if kn.get(s, 0) >= v:
                    continue
                waits.append((s, v))
                ck = evclock.get((s, v))
                if ck:
                    for ks, kv in ck.items():
                        if kn.get(ks, 0) < kv:
                            kn[ks] = kv
                if kn.get(s, 0) < v:
                    kn[s] = v
            it.waits = waits
            nwaits += len(waits)
            if it.val is not None:
                ck = dict(kn)
                ck[it.val[0]] = it.val[1]
                evclock[it.val] = ck
        self.stats = dict(n=n, nwaits=nwaits, sems=dict(cnt))
        nc = self.nc
        semnames = sorted({it.val[0] for it in ins if it.val is not None})
        with ExitStack() as es:
            sems = {s: es.enter_context(nc.semaphore(s)) for s in semnames}
            block = es.enter_context(nc.Block())
            per = {e: [it for it in ins if it.eng == e] for e in ENGS}
            finals = [ins[d].val for d in final_ids]

            def run(engobj, lst, is_last=False):
                for it in lst:
                    for s, v in it.waits[1:]:
                        engobj.wait_ge(sems[s], v)
                    r = it.fn(engobj)
                    if it.waits:
                        r._wait_ge(sems[it.waits[0][0]], it.waits[0][1])
                    if it.val is not None:
                        r.then_inc(sems[it.val[0]], 16 if it.dsem is not None else 1)
                if is_last:
                    fm = {}
                    for s, v in finals:
                        fm[s] = max(fm.get(s, 0), v)
                    for s, v in fm.items():
                        engobj.wait_ge(sems[s], v)

            @block.tensor
            def _(e):
                run(e, per["pe"])

            @block.scalar
            def _(e):
                run(e, per["act"])

            @block.vector
            def _(e):
                run(e, per["dve"])

            @block.gpsimd
            def _(e):
                run(e, per["pool"], is_last=True)

            @block.sync
            def _(e):
                run(e, per["sp"])


VEC_FIELDS = [("gmix", 8), ("gffn", 8), ("gpg", 8), ("gple", 8), ("caw", 16), ("cab", 4), ("lbr", 4),
              ("lbi", 4), ("lam", 4), ("cbw", 124), ("cbb", 4), ("cbn", 4), ("dqn", 1), ("dkn", 1),
              ("sqn", 1), ("skn", 1), ("ikn", 1), ("sub", 1), ("lq1", 1), ("lk1", 1), ("lq2", 1),
              ("lk2", 1), ("cfw", 132), ("cfb", 44)]
VC = {}
_o = 0
for _n, _k in VEC_FIELDS:
    VC[_n] = (_o, _k)
    _o += _k
NV = _o


def _cols(v, n):
    return np.ascontiguousarray(v.reshape(n, 128).T)


def pack_vec(inp, l):
    out = np.zeros((128, NV), np.float32)

    def put(name, arr):
        o, k = VC[name]
        out[:, o:o + k] = arr.reshape(128, k)

    put("gmix", _cols(inp["norm_mix"][l], 8))
    put("gffn", _cols(inp["norm_ffn"][l], 8))
    put("gpg", _cols(inp["ple_gate_norm"][l], 8))
    put("gple", _cols(inp["ple_norm"][l], 8))
    caw = inp["conv_a_w"][l]
    put("caw", np.stack([_cols(caw[k], 4) for k in range(4)], axis=1).reshape(128, 16))
    put("cab", _cols(inp["conv_a_b"][l], 4))
    put("lbr", _cols(inp["lru_b_r"][l], 4))
    put("lbi", _cols(inp["lru_b_i"][l], 4))
    put("lam", _cols(inp["lru_lambda"][l], 4))
    cbw = inp["conv_b_w"][l]
    put("cbw", np.stack([_cols(cbw[k], 4) for k in range(31)], axis=1).reshape(128, 124))
    put("cbb", _cols(inp["conv_b_b"][l], 4))
    put("cbn", _cols(inp["conv_b_norm"][l], 4))
    p = np.arange(128)
    put("dqn", inp["diff_q_norm"][l][p % 64])
    put("dkn", inp["diff_k_norm"][l][p % 64])
    put("sqn", inp["spa_q_norm"][l][p])
    put("skn", inp["spa_k_norm"][l][p])
    put("ikn", inp["idx_k_norm"][l][p % 32])
    put("sub", inp["diff_subln"][l][p])
    for nm, key in (("lq1", "diff_lq1"), ("lk1", "diff_lk1"), ("lq2", "diff_lq2"), ("lk2", "diff_lk2")):
        v = np.zeros(128, np.float32)
        v[:64] = inp[key][l]
        put(nm, v)
    cfw = inp["conv_f_w"][l]
    put("cfw", np.stack([_cols(cfw[k], 44) for k in range(3)], axis=1).reshape(128, 132))
    put("cfb", _cols(inp["conv_f_b"][l], 44))
    return out


def rope_inv(rot):
    return (np.float32(500000.0) ** (-np.arange(0, rot, 2, dtype=np.float32) / np.float32(rot))).astype(np.float32)


def make_consts(S):
    cm = np.zeros((9, 128, 128), np.float32)
    cm[0] = np.eye(128)
    cm[1] = 1.0 / 1024
    p = np.arange(128)
    cm[2] = (p[:, None] // 64 == p[None, :] // 64) / 64.0
    cm[3] = 1.0 / 128
    cm[4] = (p[:, None] // 32 == p[None, :] // 32) / 32.0
    cm[5] = 1.0 / 512
    rope = np.zeros((3, 2, 128, S), np.float32)
    t = np.arange(S, dtype=np.float32)
    for ci, hd in enumerate((64, 128, 32)):
        rot = hd // 4
        half = rot // 2
        inv = rope_inv(rot)
        ang = (t[:, None] * inv[None, :]).astype(np.float32)
        cos = np.cos(ang).astype(np.float32)
        sin = np.sin(ang).astype(np.float32)
        R = np.zeros((128, 128), np.float32)
        for q in range(128):
            d = q % hd
            if d < half:
                R[q, q + half] = -1.0
                rope[ci, 0, q] = cos[:, d]
                rope[ci, 1, q] = sin[:, d]
            elif d < 2 * half:
                R[q, q - half] = 1.0
                rope[ci, 0, q] = cos[:, d - half]
                rope[ci, 1, q] = sin[:, d - half]
            else:
                rope[ci, 0, q] = 1.0
        cm[6 + ci] = R.T
    return cm, rope


def lru_blockdiag(inp):
    L = inp["lru_w_r"].shape[0]
    out = np.zeros((L, 2, 4, 128, 128), np.float32)
    for l in range(L):
        for gi, key in enumerate(("lru_w_r", "lru_w_i")):
            w = inp[key][l]
            for ct in range(4):
                out[l, gi, ct, :64, :64] = w[2 * ct]
                out[l, gi, ct, 64:, 64:] = w[2 * ct + 1]
    return out


class B:
    def __init__(self, S, L, dbg=False, phases=None):
        self.S, self.L, self.dbg = S, L, dbg
        self.phases = phases
        nc = self.nc = bass.Bass("TRN2", target_bir_lowering=False)
        self.p = Prog(nc)
        self.NCH = S // 512
        dt = nc.dram_tensor

        def inp(name, shape, dtype=F32):
            return dt(name, list(shape), dtype, kind="ExternalInput").ap()

        self.xT = inp("xT", [D, S])
        self.pT = inp("pT", [L, PLE, S])
        self.vec = inp("vec", [L, 128, NV])
        self.cmat = inp("cmat", [9, 128, 128])
        self.rope = inp("rope", [3, 2, 128, S])
        self.lru = inp("lru", [L, 2, 4, 128, 128])
        self.w_in = inp("w_in", [L, D, DIN])
        self.w_gate = inp("w_gate", [L, 4, D, D])
        self.w_branch = inp("w_branch", [L, 4, DB, D])
        self.w_out = inp("w_out", [L, D, D])
        self.w_up = inp("w_up", [L, D, 2 * DFF])
        self.w_down = inp("w_down", [L, DFF, D])
        self.w_pg = inp("w_ple_gate", [L, D, D])
        self.w_pi = inp("w_ple_in", [L, PLE, D])
        self.outT = dt("outT", [D, S], F32, kind="ExternalOutput").ap()

        def scr(name, shape, dtype):
            kind = "ExternalOutput" if dbg else "Internal"
            return dt(name, list(shape), dtype, kind=kind).ap()

        self.H = scr("H", [D, S], F32)
        self.XN = scr("XN", [D, S], BF16)
        self.U16 = scr("U16", [2048, S], F32)
        self.CQ = scr("CQ", [512, S], BF16)
        self.CK = scr("CK", [512, S], BF16)
        self.CV = scr("CV", [S, 512], BF16)
        self.DQ = scr("DQ", [512, S], BF16)
        self.DK = scr("DK", [128, S], BF16)
        self.DV = scr("DV", [S, 128], BF16)
        self.IQ = scr("IQ", [256, S], BF16)
        self.IK = scr("IK", [32, S], BF16)
        self.IW = scr("IW", [S, 128], F32)
        self.OBR = scr("OBR", [4, 512, S], BF16)
        p = self.p
        n = self.NCH
        self.tH = p.toks(n, "H")
        self.tXN = p.toks(n, "XN")
        self.tU = p.toks(n, "U")
        self.tQK = p.toks(n, "QK")
        self.tOB = [p.toks(n, f"OB{j}_") for j in range(4)]
        self.es = ExitStack()
        self._uid = 0

    def sb(self, es, name, shape, dtype):
        self._uid += 1
        return es.enter_context(self.nc.sbuf_tensor(f"{name}_{self._uid}", list(shape), dtype))

    def ps(self, es, name, shape, dtype=F32):
        self._uid += 1
        return es.enter_context(self.nc.psum_tensor(f"{name}_{self._uid}", list(shape), dtype))

    def load_consts(self):
        p, nc = self.p, self.nc
        es = self.es
        self.cm = self.sb(es, "cm", [128, 9, 128], F32)
        self.t_cm = p.tok("cm")
        p.dma("sp", self.cm[:], self.cmat.rearrange("c p n -> p c n"), writes=[self.t_cm])
        self.ones_bf = self.sb(es, "ones_bf", [128, 128], BF16)
        self.t_ones = p.tok("ones")
        p.op("pool", lambda e: e.memset(self.ones_bf[:], 1.0), writes=[self.t_ones])
        self.ident_bf = self.sb(es, "ident_bf", [128, 128], BF16)
        self.t_identb = p.tok("identb")
        p.op("dve", lambda e: e.tensor_copy(out=self.ident_bf[:], in_=self.cm[:, 0, :]),
             reads=[self.t_cm], writes=[self.t_identb])
        self.vecs = self.sb(es, "vecs", [128, self.L, NV], F32)
        self.t_vec = p.tok("vec")
        p.dma("sp", self.vecs[:], self.vec.rearrange("l p n -> p l n"), writes=[self.t_vec])
        self.epsc = self.sb(es, "epsc", [128, 1], F32)
        self.t_eps = p.tok("eps")
        p.op("pool", lambda e: e.memset(self.epsc[:], EPS), writes=[self.t_eps])

    def freg(self, e, val):
        if not hasattr(self, "_fregs"):
            self._fregs = {}
        if val not in self._fregs:
            self._fregs[val] = e.to_reg(val)
        return self._fregs[val]

    def vcol(self, l, name, j=0, n=1):
        o, k = VC[name]
        return self.vecs[:, l, o + j:o + j + n]

    def load_weight(self, dst, dst_toks, src, K, N, stg, stg_toks, gain=None, col0=0, engs=("dve", "pool"),
                    rows=128):
        p = self.p
        for kc in range(K):
            i = self._wl % len(stg)
            e = engs[self._wl % len(engs)]
            self._wl += 1
            st, stt = stg[i], stg_toks[i]
            p.dma("sp", st[0:rows, 0:N], src[kc * rows:(kc + 1) * rows, :], writes=[stt], pool="w")
            o = dst[0:rows, kc, col0:col0 + N]
            if gain is not None:
                g = gain(kc)
                p.op(e, (lambda o=o, st=st, g=g: lambda en: en.tensor_scalar(
                    out=o, in0=st[0:rows, 0:N], scalar1=g, scalar2=1.0, op0=ALU.mult, op1=ALU.mult))(),
                    reads=[stt, self.t_vec], swrites=[dst_toks[kc]])
            else:
                p.op(e, (lambda o=o, st=st: lambda en: en.tensor_copy(out=o, in_=st[0:rows, 0:N]))(),
                     reads=[stt], swrites=[dst_toks[kc]])

    _wl = 0
    _cc = 0

    def phase1(self, l):
        p, nc, S = self.p, self.nc, self.S
        hin = self.xT if l == 0 else self.H
        cm = self.cm
        with ExitStack() as es:
            W = self.sb(es, "p1W", [128, 8, DIN], BF16)
            tW = p.toks(8, "p1W")
            Wdi = self.sb(es, "p1Wdi", [128, 8, 256], BF16)
            tWdi = p.tok("Wdi")
            with ExitStack() as es2:
                stg = [self.sb(es2, "p1stg", [128, DIN], F32) for _ in range(2)]
                tstg = p.toks(2, "stg")
                self.load_weight(W, tW, self.w_in[l], 8, DIN, stg, tstg,
                                 gain=lambda kc: self.vcol(l, "gmix", kc))
                p.op("pool", lambda e: e.memset(Wdi[:], 0.0), writes=[tWdi])
                for kc in range(8):
                    p.op("dve", lambda e, kc=kc: e.tensor_copy(out=Wdi[:, kc, 0:128], in_=W[:, kc, 4224:4352]),
                         reads=[tW[kc]], swrites=[tWdi])
                    p.op("dve", lambda e, kc=kc: e.tensor_copy(out=Wdi[:, kc, 128:136], in_=W[:, kc, 4640:4648]),
                         reads=[tW[kc]], swrites=[tWdi])
                p.barrier()
            hb = [self.sb(es, "hb", [128, 8, 512], F32) for _ in range(2)]
            thb = p.toks(2, "hb")
            rp = [self.sb(es, "rp", [128, 6, 512], F32) for _ in range(1)] * 2
            trp = [p.tok("rp")] * 2
            sq = [self.sb(es, "sq", [128, 512], F32) for _ in range(2)]
            tsq = p.toks(2, "sq")
            rstd = self.sb(es, "rstd", [128, 512], F32)
            trstd = p.tok("rstd")
            xn = [self.sb(es, "xn", [128, 8, 512], BF16) for _ in range(2)]
            txn = p.toks(2, "xn")
            raw = self.sb(es, "raw", [128, 8, 512], F32)
            traw = p.tok("raw")
            NE = 3
            ev = [self.sb(es, "ev", [128, 512], F32) for _ in range(NE)]
            tev = p.toks(NE, "ev")
            sq2 = [self.sb(es, "sq2", [128, 512], F32) for _ in range(NE)]
            tsq2 = p.toks(NE, "sq2")
            rs2 = [self.sb(es, "rs2", [128, 512], F32) for _ in range(NE)]
            trs2 = p.toks(NE, "rs2")
            xg = [self.sb(es, "xg", [128, 512], F32) for _ in range(NE)]
            txg = p.toks(NE, "xg")
            t1, tt1 = sq2, tsq2
            t2, tt2 = rs2, trs2
            ob = [self.sb(es, "ob", [128, 512], BF16) for _ in range(NE)]
            tob = p.toks(NE, "ob")
            tv = [self.sb(es, "tv", [128, 512], BF16)] * 2
            ttv = [p.tok("tv")] * 2
            tdv = [self.sb(es, "tdv", [128, 128], BF16) for _ in range(2)]
            ttdv = p.toks(2, "tdv")
            tiw = [self.sb(es, "tiw", [128, 128], F32) for _ in range(2)]
            ttiw = p.toks(2, "tiw")
            pst = self.ps(es, "pst", [128, 512])
            tpst = p.tok("pst")
            pm = [self.ps(es, "pm", [128, 512]) for _ in range(3)]
            tpm = p.toks(3, "pm")
            ps2 = self.ps(es, "ps2", [128, 512])
            tps2 = p.tok("ps2")
            ps3 = self.ps(es, "ps3", [128, 512])
            tps3 = p.tok("ps3")
            ptk = self.ps(es, "ptk", [128, 512])
            tptk = p.tok("ptk")
            ptd = self.ps(es, "ptd", [128, 256])
            tptd = p.tok("ptd")

            def loads(c):
                b = c % 2
                sl = slice(c * 512, (c + 1) * 512)
                rd = [self.tH[c]] if l > 0 else []
                p.dma("sp", hb[b][:], hin[:, sl].rearrange("(k p) t -> p k t", p=128), reads=rd, writes=[thb[b]])

            def load_rp(c):
                sl = slice(c * 512, (c + 1) * 512)
                p.dma("sp", rp[0][:], self.rope[:, :, :, sl].rearrange("c s p t -> p (c s) t"), writes=[trp[0]])

            qk_tiles = []
            for i in range(4):
                qk_tiles.append((16 + i, 128, 2, "dqn", 0, self.CQ, i * 128))
            for i in range(4):
                qk_tiles.append((20 + i, 128, 2, "dkn", 0, self.CK, i * 128))
            for i in range(4):
                qk_tiles.append((28 + i, 128, 3, "sqn", 1, self.DQ, i * 128))
            qk_tiles.append((32, 128, 3, "skn", 1, self.DK, 0))
            qk_tiles.append((34, 128, None, None, 2, self.IQ, 0))
            qk_tiles.append((35, 128, None, None, 2, self.IQ, 128))
            qk_tiles.append((36, 32, 4, "ikn", 2, self.IK, 0))

            loads(0)
            cnt = [0, 0]
            for c in range(self.NCH):
                b = c % 2
                sl = slice(c * 512, (c + 1) * 512)
                if c + 1 < self.NCH:
                    loads(c + 1)
                load_rp(c)
                hbb, xnb, rpb = hb[b], xn[b], rp[b]
                STOP = 9
                for kc in range(8):
                    si = kc % 2
                    p.op("act", lambda e, hbb=hbb, kc=kc, si=si: e.activation(out=sq[si][:], in_=hbb[:, kc, :],
                                                                              func=AF.Square),
                         reads=[thb[b]], writes=[tsq[si]])
                    p.op("pe", lambda e, kc=kc, si=si: e.matmul(pst[:], lhsT=cm[:, 1, :], rhs=sq[si][:],
                                                               start=(kc == 0), stop=(kc == 7)),
                         reads=[tsq[si], self.t_cm], writes=[tpst])
                p.op("act", lambda e: e.activation(out=rstd[:], in_=pst[:], func=AF.Sqrt, bias=self.epsc[:]),
                     reads=[tpst, self.t_eps], writes=[trstd])
                p.op("dve", lambda e: e.reciprocal(out=rstd[:], in_=rstd[:]), reads=[trstd], writes=[trstd])
                for kc in range(8):
                    eng = "dve" if kc % 2 == 0 else "pool"
                    p.op(eng, lambda e, kc=kc, hbb=hbb, xnb=xnb: e.tensor_tensor(
                        out=xnb[:, kc, :], in0=hbb[:, kc, :], in1=rstd[:], op=ALU.mult),
                        reads=[thb[b], trstd], swrites=[txn[b]])
                p.dma("sp", self.XN[:, sl].rearrange("(k p) t -> p k t", p=128), xnb[:],
                      reads=[txn[b]], writes=[self.tXN[c]], pool="st")

                def mm_fm(m, M, pidx):
                    c0 = m * 128
                    for kc in range(8):
                        p.op("pe", lambda e, kc=kc, c0=c0, M=M, pidx=pidx, xnb=xnb: e.matmul(
                            pm[pidx][0:M, :], lhsT=W[:, kc, c0:c0 + M], rhs=xnb[:, kc, :],
                            start=(kc == 0), stop=(kc == 7)),
                            reads=[tW[kc], txn[b]], writes=[tpm[pidx]])

                for m in range(16 if STOP > 2 else 0):
                    pidx = cnt[0] % 3
                    cnt[0] += 1
                    mm_fm(m, 128, pidx)
                    if m % 2 == 0:
                        p.op("act", lambda e, m=m, pidx=pidx: e.copy(out=raw[:, m % 8, :], in_=pm[pidx][:]),
                             reads=[tpm[pidx]], swrites=[traw])
                    else:
                        p.op("dve", lambda e, m=m, pidx=pidx: e.tensor_copy(out=raw[:, m % 8, :], in_=pm[pidx][:]),
                             reads=[tpm[pidx]], swrites=[traw])
                    if m % 8 == 7:
                        m0 = (m // 8) * 1024
                        p.dma("sp", self.U16[m0:m0 + 1024, sl].rearrange("(m p) t -> p m t", p=128), raw[:],
                              reads=[traw], swrites=[self.tU[c]], pool="st")
                nq = len(qk_tiles)
                pid = {}

                def stA(n):
                    (m, M, gi, gname, rc, dst, r0) = qk_tiles[n]
                    pidx = cnt[0] % 3
                    cnt[0] += 1
                    i = (cnt[1] + n) % NE
                    mm_fm(m, M, pidx)
                    p.op("act", lambda e, i=i, pidx=pidx, M=M: e.copy(out=ev[i][0:M, :], in_=pm[pidx][0:M, :]),
                         reads=[tpm[pidx]], writes=[tev[i]])
                    if gi is not None:
                        p.op("act", lambda e, i=i, pidx=pidx, M=M: e.activation(out=sq2[i][0:M, :],
                                                                               in_=pm[pidx][0:M, :], func=AF.Square),
                             reads=[tpm[pidx]], writes=[tsq2[i]])

                def stB(n):
                    (m, M, gi, gname, rc, dst, r0) = qk_tiles[n]
                    i = (cnt[1] + n) % NE
                    if gi is None:
                        return
                    p.op("pe", lambda e, i=i, M=M, gi=gi: e.matmul(ps2[0:M, :], lhsT=cm[0:M, gi, 0:M],
                                                                  rhs=sq2[i][0:M, :], start=True, stop=True),
                         reads=[tsq2[i], self.t_cm], writes=[tps2])
                    p.op("act", lambda e, i=i, M=M: e.activation(out=rs2[i][0:M, :], in_=ps2[0:M, :],
                                                                 func=AF.Sqrt, bias=self.epsc[0:M, :]),
                         reads=[tps2, self.t_eps], writes=[trs2[i]])
                    p.op("dve", lambda e, i=i, M=M: e.reciprocal(out=rs2[i][0:M, :], in_=rs2[i][0:M, :]),
                         reads=[trs2[i]], writes=[trs2[i]])
                    g = self.vcol(l, gname)[0:M, :]
                    p.op("dve", lambda e, i=i, M=M, g=g: e.scalar_tensor_tensor(
                        out=xg[i][0:M, :], in0=ev[i][0:M, :], scalar=g, in1=rs2[i][0:M, :],
                        op0=ALU.mult, op1=ALU.mult),
                        reads=[tev[i], trs2[i], self.t_vec], writes=[txg[i]])

                def stC(n):
                    (m, M, gi, gname, rc, dst, r0) = qk_tiles[n]
                    i = (cnt[1] + n) % NE
                    xs, txs = (xg[i], txg[i]) if gi is not None else (ev[i], tev[i])
                    p.op("pe", lambda e, xs=xs, M=M, rc=rc: e.matmul(ps3[0:M, :], lhsT=cm[0:M, 6 + rc, 0:M],
                                                                    rhs=xs[0:M, :], start=True, stop=True),
                         reads=[txs, self.t_cm], writes=[tps3])
                    p.op("pool", lambda e, i=i, xs=xs, M=M, rc=rc, rpb=rpb: e.tensor_tensor(
                        out=t1[i][0:M, :], in0=xs[0:M, :], in1=rpb[0:M, 2 * rc, :], op=ALU.mult),
                        reads=[txs, trp[b]], writes=[tt1[i]])
                    p.op("dve", lambda e, i=i, M=M, rc=rc, rpb=rpb: e.tensor_tensor(
                        out=t2[i][0:M, :], in0=ps3[0:M, :], in1=rpb[0:M, 2 * rc + 1, :], op=ALU.mult),
                        reads=[tps3, trp[b]], writes=[tt2[i]])
                    p.op("pool", lambda e, i=i, M=M: e.tensor_tensor(
                        out=ob[i][0:M, :], in0=t1[i][0:M, :], in1=t2[i][0:M, :], op=ALU.add),
                        reads=[tt1[i], tt2[i]], writes=[tob[i]])
                    p.dma("sp", dst[r0:r0 + M, sl], ob[i][0:M, :], reads=[tob[i]], swrites=[self.tQK[c]], pool="st")

                for step in range(nq + 2):
                    if step < nq:
                        stA(step)
                    if 0 <= step - 1 < nq:
                        stB(step - 1)
                    if 0 <= step - 2 < nq:
                        stC(step - 2)
                cnt[1] += nq
                for j in range(4 if STOP > 4 else 0):
                    jb = j % 2
                    ts = slice(j * 128, (j + 1) * 128)
                    r0 = c * 512 + j * 128
                    for kc in range(8):
                        p.op("pe", lambda e, kc=kc, ts=ts, xnb=xnb: e.matmul(
                            ptk[:], lhsT=xnb[:, kc, ts], rhs=W[:, kc, 3072:3584], start=(kc == 0), stop=(kc == 7)),
                            reads=[tW[kc], txn[b]], writes=[tptk])
                    for kc in range(8):
                        p.op("pe", lambda e, kc=kc, ts=ts, xnb=xnb: e.matmul(
                            ptd[:, 0:256], lhsT=xnb[:, kc, ts], rhs=Wdi[:, kc, :], start=(kc == 0),
                            stop=(kc == 7)), reads=[tWdi, txn[b]], writes=[tptd])
                    p.op("act", lambda e, jb=jb: e.copy(out=tv[jb][:], in_=ptk[:]), reads=[tptk], writes=[ttv[jb]])
                    p.op("dve", lambda e, jb=jb: e.tensor_copy(out=tdv[jb][:], in_=ptd[:, 0:128]),
                         reads=[tptd], writes=[ttdv[jb]])
                    p.op("dve", lambda e, jb=jb: e.tensor_scalar(out=tiw[jb][:], in0=ptd[:, 128:256], scalar1=1.0 / 16.0,
                                                                 scalar2=None, op0=ALU.mult),
                         reads=[tptd], writes=[ttiw[jb]])
                    p.dma("sp", self.CV[r0:r0 + 128, :], tv[jb][:], reads=[ttv[jb]], swrites=[self.tQK[c]], pool="st")
                    p.dma("sp", self.DV[r0:r0 + 128, :], tdv[jb][:], reads=[ttdv[jb]], swrites=[self.tQK[c]],
                          pool="st")
                    p.dma("sp", self.IW[r0:r0 + 128, :], tiw[jb][:], reads=[ttiw[jb]], swrites=[self.tQK[c]],
                          pool="st")
            p.barrier()

    def build(self):
        nc = self.nc
        with nc.allow_low_precision("bf16 matmul operands, fp32 accumulation"):
            with self.es:
                self.load_consts()
                ph = self.phases
                for l in range(self.L):
                    if ph is None or "p1" in ph:
                        self.phase1(l)
                    if ph is None or "a" in ph or "b" in ph:
                        self.phaseAB(l)
                    if ph is None or "c" in ph:
                        self.phaseC(l)
                    if ph is None or "d" in ph:
                        self.phaseD(l)
                    if ph is None or "p3" in ph:
                        self.phase3(l)
                    if ph is None or "p4" in ph:
                        self.phase4(l)
                self.p.build()
        return nc


def host_inputs(inp, S, L):
    cm, rope = make_consts(S)
    vec = np.stack([pack_vec(inp, l) for l in range(L)])
    lru = lru_blockdiag(inp)
    common = dict(vec=vec, cmat=cm, rope=rope, lru=lru[:L])
    for k_dev, k_in in (("w_in", "w_in"), ("w_gate", "w_gate"), ("w_branch", "w_branch"), ("w_out", "w_out"),
                        ("w_up", "w_up"), ("w_down", "w_down"), ("w_ple_gate", "w_ple_gate"),
                        ("w_ple_in", "w_ple_in")):
        common[k_dev] = np.ascontiguousarray(inp[k_in][:L], dtype=np.float32)
    maps = []
    nb = inp["x"].shape[0]
    for b in range(nb):
        m = dict(common)
        m["xT"] = np.ascontiguousarray(inp["x"][b].T)
        m["pT"] = np.ascontiguousarray(np.transpose(inp["p"][:L, b], (0, 2, 1)))
        maps.append(m)
    return maps


def kernel(**inputs):
    inp = {k: np.asarray(v) for k, v in inputs.items()}
    nb, S, _ = inp["x"].shape
    L = inp["w_in"].shape[0]
    bld = B(S, L)
    nc = bld.build()
    maps = host_inputs(inp, S, L)
    res = run_bass_kernel_spmd(nc, maps, core_ids=list(range(nb)))
    out = np.stack([np.ascontiguousarray(res.results[b]["outT"].T) for b in range(nb)])
    return out.astype(np.float32)


def genA(self, l, es):
    p, nc, S = self.p, self.nc, self.S
    TC = 1024
    NT = S // TC
    if True:
        lst = self.sb(es, "lrust", [128, 8, 128], F32)
        tlst = p.tok("lrust")
        lw = self.sb(es, "lruw", [128, 8, 128], BF16)
        tlw = p.tok("lruw")
        p.dma("sp", lst[:], self.lru[l].rearrange("g c p n -> p (g c) n"), writes=[tlst])
        p.op("dve", lambda e: e.tensor_copy(out=lw[:], in_=lst[:]), reads=[tlst], writes=[tlw])
        cc = self.sb(es, "lruc", [128, 12], F32)
        tcc = p.tok("lruc")
        onec = self.sb(es, "onec", [128, 1], F32)
        tone = p.tok("onec")
        p.op("pool", lambda e: e.memset(onec[:], 1.0 + 2.0 ** -23), writes=[tone])
        lam = self.vcol(l, "lam", 0, 4)
        p.op("act", lambda e: e.activation(out=cc[:, 0:4], in_=lam, func=AF.Exp, scale=-1.0),
             reads=[self.t_vec], writes=[tcc])
        p.op("act", lambda e: e.activation(out=cc[:, 0:4], in_=cc[:, 0:4], func=AF.Ln, bias=1.0),
             reads=[tcc], writes=[tcc])
        p.op("dve", lambda e: e.tensor_scalar(out=cc[:, 4:8], in0=cc[:, 0:4], scalar1=-8.0, scalar2=None,
                                              op0=ALU.mult), reads=[tcc], writes=[tcc])
        p.op("dve", lambda e: e.tensor_scalar(out=cc[:, 8:12], in0=cc[:, 0:4], scalar1=-16.0, scalar2=None,
                                              op0=ALU.mult), reads=[tcc], writes=[tcc])

        def f32buf(name, n=TC):
            return self.sb(es, name, [128, n], F32), p.tok(name)

        xa, txa = f32buf("xa", TC + 3)
        xc, txc = f32buf("xc")
        xcb = self.sb(es, "xcb", [128, TC], BF16)
        txcb = p.tok("xcb")
        r, tr = f32buf("r")
        ig, tig = f32buf("ig")
        a, ta = f32buf("a")
        dr, tdr = f32buf("dr")
        gx, tgx = f32buf("gx")
        h, th = f32buf("h")
        ag, tag = f32buf("ag")
        tg, ttg = f32buf("tg")
        sg, tsg = f32buf("sg")
        ob = self.sb(es, "oba", [128, TC], BF16)
        tob = p.tok("oba")
        hst = self.sb(es, "hst", [128, 1], F32)
        thst = p.tok("hst")
        pr = self.ps(es, "pr", [128, TC])
        tpr = p.tok("pr")
        pi = self.ps(es, "pi", [128, TC])
        tpi = p.tok("pi")
        nsub = TC // 512
        for ct in range(4):
            rows = slice(ct * 128, (ct + 1) * 128)
            p.op("pool", lambda e: e.memset(hst[:], 0.0), writes=[thst])
            for t in range(NT):
                t0 = t * TC
                chs = list(range(t0 // 512, (t0 + TC) // 512))
                rd = [self.tU[c] for c in chs]
                if t == 0:
                    p.op("pool", lambda e: e.memset(xa[:, 0:3], 0.0), writes=[txa])
                    p.dma("sp", xa[:, 3:TC + 3], self.U16[rows, 0:TC], reads=rd, swrites=[txa])
                else:
                    p.dma("sp", xa[:, :], self.U16[rows, t0 - 3:t0 + TC], reads=rd + [self.tU[chs[0] - 1]],
                          writes=[txa])
                p.dma("sp", ag[:], self.U16[512 + ct * 128:512 + (ct + 1) * 128, t0:t0 + TC], reads=rd, writes=[tag])
                w = lambda k: self.vcol(l, "caw", k * 4 + ct)
                p.op("dve", lambda e, w0=w(0), bb=self.vcol(l, "cab", ct): e.tensor_scalar(
                    out=xc[:], in0=xa[:, 0:TC], scalar1=w0, scalar2=bb, op0=ALU.mult, op1=ALU.add),
                    reads=[txa, self.t_vec], writes=[txc])
                for k in range(1, 4):
                    p.op("dve", lambda e, k=k, wk=w(k): e.scalar_tensor_tensor(
                        out=xc[:], in0=xa[:, k:k + TC], scalar=wk, in1=xc[:], op0=ALU.mult, op1=ALU.add),
                        reads=[txa, txc, self.t_vec], writes=[txc])
                p.op("pool", lambda e: e.tensor_copy(out=xcb[:], in_=xc[:]), reads=[txc], writes=[txcb])
                for s in range(nsub):
                    ss = slice(s * 512, (s + 1) * 512)
                    p.op("pe", lambda e, ss=ss, ct=ct: e.matmul(pr[:, ss], lhsT=lw[:, ct, :], rhs=xcb[:, ss],
                                                               start=True, stop=True),
                         reads=[tlw, txcb], swrites=[tpr])
                    p.op("pe", lambda e, ss=ss, ct=ct: e.matmul(pi[:, ss], lhsT=lw[:, 4 + ct, :], rhs=xcb[:, ss],
                                                               start=True, stop=True),
                         reads=[tlw, txcb], swrites=[tpi])
                p.op("act", lambda e, bb=self.vcol(l, "lbr", ct): e.activation(out=r[:], in_=pr[:], func=AF.Sigmoid,
                                                                              bias=bb),
                     reads=[tpr, self.t_vec], writes=[tr])
                p.op("act", lambda e, bb=self.vcol(l, "lbi", ct): e.activation(out=ig[:], in_=pi[:], func=AF.Sigmoid,
                                                                              bias=bb),
                     reads=[tpi, self.t_vec], writes=[tig])
                p.op("act", lambda e, ct=ct: e.activation(out=a[:], in_=r[:], func=AF.Exp, scale=cc[:, 4 + ct:5 + ct]),
                     reads=[tr, tcc], writes=[ta])
                p.op("act", lambda e, ct=ct: e.activation(out=dr[:], in_=r[:], func=AF.Exp,
                                                          scale=cc[:, 8 + ct:9 + ct]),
                     reads=[tr, tcc], writes=[tdr])
                p.op("act", lambda e: e.activation(out=dr[:], in_=dr[:], func=AF.Sqrt, scale=-1.0, bias=onec[:]),
                     reads=[tdr, tone], writes=[tdr])
                p.op("pool", lambda e: e.tensor_tensor(out=gx[:], in0=ig[:], in1=xc[:], op=ALU.mult),
                     reads=[tig, txc], writes=[tgx])
                p.op("pool", lambda e: e.tensor_tensor(out=gx[:], in0=gx[:], in1=dr[:], op=ALU.mult),
                     reads=[tgx, tdr], writes=[tgx])
                p.op("dve", lambda e: e.tensor_tensor_scan(out=h[:], data0=a[:], data1=gx[:], initial=hst[:],
                                                           op0=ALU.mult, op1=ALU.add),
                     reads=[ta, tgx, thst], writes=[th])
                p.op("act", lambda e: e.copy(out=hst[:], in_=h[:, TC - 1:TC]), reads=[th], writes=[thst])
                p.op("pool", lambda e: e.tensor_tensor(out=tg[:], in0=ag[:], in1=ag[:], op=ALU.mult),
                     reads=[tag], writes=[ttg])
                p.op("pool", lambda e: e.tensor_scalar(out=tg[:], in0=tg[:], scalar1=0.044715, scalar2=1.0,
                                                       op0=ALU.mult, op1=ALU.add), reads=[ttg], writes=[ttg])
                p.op("pool", lambda e: e.tensor_tensor(out=tg[:], in0=tg[:], in1=ag[:], op=ALU.mult),
                     reads=[ttg, tag], writes=[ttg])
                p.op("act", lambda e: e.activation(out=sg[:], in_=tg[:], func=AF.Sigmoid,
                                                   scale=2.0 * math.sqrt(2.0 / math.pi)),
                     reads=[ttg], writes=[tsg])
                p.op("pool", lambda e: e.tensor_tensor(out=sg[:], in0=sg[:], in1=ag[:], op=ALU.mult),
                     reads=[tsg, tag], writes=[tsg])
                p.op("dve", lambda e: e.tensor_tensor(out=ob[:], in0=h[:], in1=sg[:], op=ALU.mult),
                     reads=[th, tsg], writes=[tob])
                for c in chs:
                    o = c * 512 - t0
                    p.dma("sp", self.OBR[0, rows, c * 512:(c + 1) * 512], ob[:, o:o + 512], reads=[tob],
                          swrites=[self.tOB[0][c]], pool="st")
                yield


B.genA = genA


def genB(self, l, es):
    p, nc, S = self.p, self.nc, self.S
    TC = 1024
    NT = S // TC
    KC = 31
    if True:
        vb = [self.sb(es, "vb", [128, TC], F32) for _ in range(2)]
        tvb = p.toks(2, "vb")
        gb = [self.sb(es, "gb", [128, TC], F32) for _ in range(2)]
        tgb = p.toks(2, "gb")
        y = [self.sb(es, "y", [128, TC + KC - 1], F32) for _ in range(4)]
        ty = p.toks(4, "y")
        acc = [self.sb(es, "acc", [128, TC], F32) for _ in range(4)]
        tacc = p.toks(4, "acc")
        sqb = [self.sb(es, "sqb", [128, TC], F32) for _ in range(2)]
        tsqb = p.toks(2, "sqb")
        rs = self.sb(es, "rsb", [128, TC], F32)
        trs = p.tok("rsb")
        zb = [self.sb(es, "zb", [128, TC], F32) for _ in range(2)]
        tzb = p.toks(2, "zb")
        ob = [self.sb(es, "obb", [128, TC], BF16) for _ in range(2)]
        tob = p.toks(2, "obb")
        pn = self.ps(es, "pn", [128, TC])
        tpn = p.tok("pn")
        nsub = TC // 512
        for ct in range(4):
            p.op("pool", lambda e, ct=ct: e.memset(y[ct][:, 0:KC - 1], 0.0), writes=[ty[ct]])
        for t in range(NT):
            t0 = t * TC
            chs = list(range(t0 // 512, (t0 + TC) // 512))
            rd = [self.tU[c] for c in chs]
            for ct in range(4):
                b = ct % 2
                p.dma("sp", vb[b][:], self.U16[1024 + ct * 128:1024 + (ct + 1) * 128, t0:t0 + TC], reads=rd,
                      writes=[tvb[b]])
                p.dma("sp", gb[b][:], self.U16[1536 + ct * 128:1536 + (ct + 1) * 128, t0:t0 + TC], reads=rd,
                      writes=[tgb[b]])
                p.op("act", lambda e, b=b: e.activation(out=gb[b][:], in_=gb[b][:], func=AF.Sigmoid),
                     reads=[tgb[b]], writes=[tgb[b]])
                if t > 0:
                    p.op("act", lambda e, ct=ct: e.copy(out=y[ct][:, 0:KC - 1], in_=y[ct][:, TC:TC + KC - 1]),
                         reads=[ty[ct]], writes=[ty[ct]])
                p.op("pool", lambda e, ct=ct, b=b: e.tensor_tensor(out=y[ct][:, KC - 1:KC - 1 + TC], in0=vb[b][:],
                                                                  in1=gb[b][:], op=ALU.mult),
                     reads=[tvb[b], tgb[b], ty[ct]], writes=[ty[ct]])
                w = lambda k: self.vcol(l, "cbw", k * 4 + ct)
                p.op("dve", lambda e, ct=ct, w0=w(0), bb=self.vcol(l, "cbb", ct): e.tensor_scalar(
                    out=acc[ct][:], in0=y[ct][:, 0:TC], scalar1=w0, scalar2=bb, op0=ALU.mult, op1=ALU.add),
                    reads=[ty[ct], self.t_vec], writes=[tacc[ct]])
                for k in range(1, KC):
                    p.op("dve", lambda e, ct=ct, k=k, wk=w(k): e.scalar_tensor_tensor(
                        out=acc[ct][:], in0=y[ct][:, k:k + TC], scalar=wk, in1=acc[ct][:], op0=ALU.mult,
                        op1=ALU.add), reads=[ty[ct], tacc[ct], self.t_vec], writes=[tacc[ct]])
                p.op("act", lambda e, ct=ct, b=b: e.activation(out=sqb[b][:], in_=acc[ct][:], func=AF.Square),
                     reads=[tacc[ct]], writes=[tsqb[b]])
                for s in range(nsub):
                    ss = slice(s * 512, (s + 1) * 512)
                    p.op("pe", lambda e, ss=ss, b=b, ct=ct: e.matmul(pn[:, ss], lhsT=self.cm[:, 5, :],
                                                                    rhs=sqb[b][:, ss], start=(ct == 0),
                                                                    stop=(ct == 3)),
                         reads=[tsqb[b], self.t_cm], swrites=[tpn])
                yield
            p.op("act", lambda e: e.activation(out=rs[:], in_=pn[:], func=AF.Sqrt, bias=self.epsc[:]),
                 reads=[tpn, self.t_eps], writes=[trs])
            p.op("dve", lambda e: e.reciprocal(out=rs[:], in_=rs[:]), reads=[trs], writes=[trs])
            for ct in range(4):
                b = ct % 2
                p.op("dve", lambda e, ct=ct, b=b, g=self.vcol(l, "cbn", ct): e.scalar_tensor_tensor(
                    out=zb[b][:], in0=acc[ct][:], scalar=g, in1=rs[:], op0=ALU.mult, op1=ALU.mult),
                    reads=[tacc[ct], trs, self.t_vec], writes=[tzb[b]])
                p.op("act", lambda e, b=b: e.activation(out=ob[b][:], in_=zb[b][:], func=AF.Silu),
                     reads=[tzb[b]], writes=[tob[b]])
                for c in chs:
                    o = c * 512 - t0
                    p.dma("sp", self.OBR[1, ct * 128:(ct + 1) * 128, c * 512:(c + 1) * 512], ob[b][:, o:o + 512],
                          reads=[tob[b]], swrites=[self.tOB[1][c]], pool="st")
            yield


B.genB = genB


def phaseAB(self, l):
    p = self.p
    NT = self.S // 1024
    with ExitStack() as es:
        gens = [(self.genA(l, es), 4 * NT), (self.genB(l, es), 5 * NT)]
        prog = [0] * len(gens)
        alive = [True] * len(gens)
        while any(alive):
            i = min((k for k in range(len(gens)) if alive[k]), key=lambda k: prog[k] / gens[k][1])
            try:
                next(gens[i][0])
                prog[i] += 1
            except StopIteration:
                alive[i] = False
    p.barrier()


B.phaseAB = phaseAB


def phaseC(self, l):
    p, nc, S = self.p, self.nc, self.S
    NQ = S // 512
    NKT = S // 128
    lam_init = 0.8 - 0.6 * math.exp(-0.3 * l)
    with ExitStack() as es:
        pr16 = self.sb(es, "pr16", [128, 16], F32)
        tpr16 = p.tok("pr16")
        lamt = self.sb(es, "lamt", [128, 4], F32)
        tlam = p.tok("lamt")
        sub2 = self.sb(es, "sub2", [128, 1], F32)
        NP = 2
        pss = [self.ps(es, "pss", [128, 1024]) for _ in range(NP)]
        tpss = p.toks(NP, "pss")
        pO = [self.ps(es, "pO", [128, 512]) for _ in range(2)]
        tpO = p.toks(2, "pO")
        pL = [self.ps(es, "pL", [128, 512]) for _ in range(2)]
        tpL = p.toks(2, "pL")
        pstat = pL[1][:, :]
        tpstat = tpL[1]
        p.op("pool", lambda e: e.memset(pr16[:], 0.0), writes=[tpr16])
        p.op("dve", lambda e: e.tensor_tensor(out=pr16[:, 0:1], in0=self.vcol(l, "lq1"), in1=self.vcol(l, "lk1"),
                                              op=ALU.mult), reads=[self.t_vec], swrites=[tpr16])
        p.op("dve", lambda e: e.tensor_tensor(out=pr16[:, 1:2], in0=self.vcol(l, "lq2"), in1=self.vcol(l, "lk2"),
                                              op=ALU.mult), reads=[self.t_vec], swrites=[tpr16])
        p.op("pe", lambda e: e.matmul(pstat[:, 0:16], lhsT=self.cm[:, 3, :], rhs=pr16[:], start=True, stop=True),
             reads=[tpr16, self.t_cm], writes=[tpstat])
        p.op("act", lambda e: e.activation(out=lamt[:, 0:2], in_=pstat[:, 0:2], func=AF.Exp, scale=128.0),
             reads=[tpstat], writes=[tlam])
        p.op("dve", lambda e: e.tensor_tensor(out=lamt[:, 2:3], in0=lamt[:, 1:2], in1=lamt[:, 0:1], op=ALU.subtract),
             reads=[tlam], writes=[tlam])
        p.op("dve", lambda e: e.tensor_scalar(out=lamt[:, 2:3], in0=lamt[:, 2:3], scalar1=-lam_init, scalar2=None,
                                              op0=ALU.add), reads=[tlam], writes=[tlam])
        p.op("dve", lambda e: e.tensor_scalar(out=lamt[:, 3:4], in0=self.vcol(l, "sub"), scalar1=1.0 - lam_init,
                                              scalar2=None, op0=ALU.mult), reads=[tlam, self.t_vec], writes=[tlam])
        kT = [self.sb(es, "kT", [128, S], BF16) for _ in range(2)]
        qT = [self.sb(es, "qT", [128, S], BF16) for _ in range(2)]
        vT = [self.sb(es, "vT", [128, NKT, 128], BF16) for _ in range(2)]
        thd = p.toks(2, "hd")
        pt = [self.sb(es, "pt", [128, 1024], BF16) for _ in range(3)]
        tpt = p.toks(3, "pt")
        rl = [self.sb(es, "rl", [128, 512], F32) for _ in range(2)]
        trl = p.toks(2, "rl")
        on = [self.sb(es, "on", [128, 512], F32) for _ in range(2)]
        ton = p.toks(2, "on")
        o = self.sb(es, "oc", [128, 512], F32)
        to = p.tok("oc")
        sq = self.sb(es, "sqc", [128, 512], F32)
        tsq = p.tok("sqc")
        rs = self.sb(es, "rsc", [128, 512], F32)
        trs = p.tok("rsc")
        ob = [self.sb(es, "obc", [128, 512], BF16) for _ in range(2)]
        tob = p.toks(2, "obc")
        allqk = list(self.tQK)
        mb = self.sb(es, "maskb", [128, 4, 512], BF16)
        tmb = p.tok("maskb")
        p.op("pool", lambda e: e.memset(mb[:], 0.0), writes=[tmb])
        for j in range(4):
            p.op("pool", lambda e, j=j: e.affine_select(
                out=mb[:, j, :], in_=mb[:, j, :], pattern=[[1, 512]], compare_op=ALU.is_ge,
                fill=self.freg(e, -30000.0), base=-128 * j, channel_multiplier=-1), reads=[tmb], writes=[tmb])

        def load_head(hd):
            b = hd % 2
            rows = slice(hd * 128, (hd + 1) * 128)
            p.dma("sp", kT[b][:], self.CK[rows, :], reads=allqk, writes=[thd[b]])
            p.dma("sp", qT[b][:], self.CQ[rows, :], reads=allqk, swrites=[thd[b]])
            p.dma("sp", vT[b][:], self.CV[:, rows].rearrange("(t p) d -> p t d", p=128), reads=allqk,
                  swrites=[thd[b]])

        load_head(0)
        for hd in range(4):
            b = hd % 2
            if hd + 1 < 4:
                load_head(hd + 1)
            units = [(qc, kt) for qc in range(NQ) for kt in range(4 * qc + 4)]
            base = self._cc
            self._cc += len(units)

            def emit_S(u, idx):
                qc, kt = u
                i = idx % NP
                qs = slice(qc * 512, (qc + 1) * 512)
                j = kt - 4 * qc
                for c in range(2):
                    ds = slice(c * 64, (c + 1) * 64)
                    p.op("pe", lambda e, i=i, b=b, ds=ds, kt=kt, qs=qs, c=c, j=j: e.matmul(
                        pss[i][:, c * 512:(c + 1) * 512], lhsT=kT[b][ds, kt * 128:(kt + 1) * 128], rhs=qT[b][ds, qs],
                        start=True, stop=(j < 0)), reads=[thd[b]], swrites=[tpss[i]])
                if j >= 0:
                    for c in range(2):
                        p.op("pe", lambda e, i=i, c=c, j=j: e.matmul(
                            pss[i][:, c * 512:(c + 1) * 512], lhsT=self.ident_bf[:], rhs=mb[:, j, :],
                            start=False, stop=True), reads=[tmb, self.t_identb], swrites=[tpss[i]])

            emit_S(units[0], base)
            for n, u in enumerate(units):
                qc, kt = u
                idx = base + n
                i = idx % NP
                ip = idx % 3
                nk = 4 * qc + 4
                qs = slice(qc * 512, (qc + 1) * 512)
                p.op("act", lambda e, i=i, ip=ip: e.activation(out=pt[ip][:], in_=pss[i][:], func=AF.Exp, scale=0.125),
                     reads=[tpss[i]], writes=[tpt[ip]])
                if n + 1 < len(units):
                    emit_S(units[n + 1], idx + 1)
                for c in range(2):
                    p.op("pe", lambda e, ip=ip, b=b, kt=kt, c=c, nk=nk: e.matmul(
                        pO[c][:], lhsT=vT[b][:, kt, :], rhs=pt[ip][:, c * 512:(c + 1) * 512], start=(kt == 0),
                        stop=(kt == nk - 1)), reads=[thd[b], tpt[ip]], writes=[tpO[c]])
                for c in range(2):
                    p.op("pe", lambda e, ip=ip, kt=kt, c=c, nk=nk: e.matmul(
                        pL[c][:], lhsT=self.ones_bf[:], rhs=pt[ip][:, c * 512:(c + 1) * 512], start=(kt == 0),
                        stop=(kt == nk - 1)), reads=[self.t_ones, tpt[ip]], writes=[tpL[c]])
                if kt == nk - 1:
                    for c in range(2):
                        p.op("dve", lambda e, c=c: e.reciprocal(out=rl[c][:], in_=pL[c][:]), reads=[tpL[c]],
                             writes=[trl[c]])
                        p.op("dve", lambda e, c=c: e.tensor_tensor(out=on[c][:], in0=pO[c][:], in1=rl[c][:],
                                                                   op=ALU.mult),
                             reads=[tpO[c], trl[c]], writes=[ton[c]])
                    c = 1
                    if c == 1:
                        ib = (hd * NQ + qc) % 2
                        p.op("dve", lambda e: e.scalar_tensor_tensor(out=o[:], in0=on[1][:], scalar=lamt[:, 2:3],
                                                                     in1=on[0][:], op0=ALU.mult, op1=ALU.add),
                             reads=[ton[0], ton[1], tlam], writes=[to])
                        p.op("pool", lambda e: e.tensor_tensor(out=sq[:], in0=o[:], in1=o[:], op=ALU.mult), reads=[to],
                             writes=[tsq])
                        p.op("pe", lambda e: e.matmul(pstat, lhsT=self.cm[:, 3, :], rhs=sq[:], start=True,
                                                      stop=True), reads=[tsq, self.t_cm], writes=[tpstat])
                        p.op("act", lambda e: e.activation(out=rs[:], in_=pstat, func=AF.Sqrt, bias=self.epsc[:]),
                             reads=[tpstat, self.t_eps], writes=[trs])
                        p.op("dve", lambda e: e.reciprocal(out=rs[:], in_=rs[:]), reads=[trs], writes=[trs])
                        p.op("dve", lambda e, ib=ib: e.scalar_tensor_tensor(out=ob[ib][:], in0=o[:],
                                                                           scalar=lamt[:, 3:4], in1=rs[:],
                                                                           op0=ALU.mult, op1=ALU.mult),
                             reads=[to, trs, tlam], writes=[tob[ib]])
                        p.dma("sp", self.OBR[2, hd * 128:(hd + 1) * 128, qs], ob[ib][:], reads=[tob[ib]],
                              swrites=[self.tOB[2][qc]], pool="st")
    p.barrier()


B.phaseC = phaseC


def phaseD(self, l):
    p, nc, S = self.p, self.nc, self.S
    NQT = S // 128
    NKT = S // 128
    NEG = -1.0e30
    with ExitStack() as es:
        ik4 = self.sb(es, "ik4", [128, S], BF16)
        iqt = [self.sb(es, "iqt", [128, 3, 128], BF16) for _ in range(2)]
        tiqt = p.toks(2, "iqt")
        dkT = self.sb(es, "dkT", [128, S], BF16)
        dvT = self.sb(es, "dvT", [128, NKT, 128], BF16)
        tres = p.tok("dres")
        allqk = list(self.tQK)
        for g in range(3):
            p.dma("sp", ik4[g * 32:(g + 1) * 32, :], self.IK[:, :], reads=allqk, swrites=[tres])
        p.dma("sp", dkT[:], self.DK[:, :], reads=allqk, swrites=[tres])
        p.dma("sp", dvT[:], self.DV.rearrange("(t p) d -> p t d", p=128), reads=allqk, swrites=[tres])
        SC = [self.sb(es, "SC", [128, S], F32) for _ in range(2)]
        tSC = p.toks(2, "SC")
        MK = [self.sb(es, "MK", [128, S], BF16) for _ in range(2)]
        tMK = p.toks(2, "MK")
        wq = [self.sb(es, "wq", [128, 128], F32) for _ in range(2)]
        twq = p.toks(2, "wq")
        dg = [self.sb(es, "dg", [128, 8, 128], BF16) for _ in range(2)]
        tdg = p.toks(2, "dg")
        T = [self.sb(es, "T", [128, 2, 512], BF16) for _ in range(2)]
        tT = p.toks(2, "T")
        st = [self.sb(es, "bst", [128, 8], F32) for _ in range(2)]
        tst = p.toks(2, "bst")
        W2 = [self.sb(es, "W2", [128, NBIS], F32) for _ in range(2)]
        tW2 = p.toks(2, "W2")
        ctab = self.sb(es, "ctab", [128, NBIS], F32)
        tctab = p.tok("ctab")
        for it in range(NBIS):
            p.op("pool", lambda e, it=it: e.memset(ctab[:, it:it + 1], 0.5 ** (it + 1)), swrites=[tctab])
        mTs = [self.sb(es, "mTs", [128, 128], BF16) for _ in range(2)]
        tmTs = p.toks(2, "mTs")
        dq = [self.sb(es, "dq", [128, 4, 128], BF16) for _ in range(2)]
        tdq = p.toks(2, "dq")
        E = [self.sb(es, "E", [128, 4, 128], BF16) for _ in range(2)]
        tE = p.toks(2, "E")
        P = [self.sb(es, "P", [128, 4, 128], BF16) for _ in range(2)]
        tP = p.toks(2, "P")
        rl = self.sb(es, "rld", [128, 512], F32)
        trl = p.tok("rld")
        od = [self.sb(es, "od", [128, 4, 128], BF16) for _ in range(2)]
        tod = p.toks(2, "od")
        pl = [self.ps(es, "pl", [128, 512]) for _ in range(2)]
        tpl = p.toks(2, "pl")
        NL1 = 3
        pl1 = [self.ps(es, "pl1", [128, 512]) for _ in range(NL1)]
        tpl1 = p.toks(NL1, "pl1")
        psc = self.ps(es, "psc", [128, 512])
        tpsc = p.tok("psc")
        mbT = self.sb(es, "mbT", [128, S], BF16)
        tmbT = p.tok("mbT")
        pO = self.ps(es, "pOd", [128, 512])
        tpO = p.tok("pOd")
        pL = self.ps(es, "pLd", [128, 512])
        tpL = p.tok("pLd")
        scale = 128.0 ** -0.5
        ctr = [0, 0]

        def part1(qt):
            qb = qt % 2
            qs = slice(qt * 128, (qt + 1) * 128)
            L = 128 * (qt + 1)
            nkc = (L + 511) // 512
            p.dma("sp", wq[qb][:], self.IW[qs, :], reads=[self.tQK[qt // 4]], writes=[twq[qb]])
            p.dma("sp", iqt[qb][0:96, 0, :], self.IQ[0:96, qs], reads=[self.tQK[qt // 4]], writes=[tiqt[qb]])
            p.dma("sp", iqt[qb][0:96, 1, :], self.IQ[96:192, qs], reads=[self.tQK[qt // 4]], swrites=[tiqt[qb]])
            p.dma("sp", iqt[qb][0:64, 2, :], self.IQ[192:256, qs], reads=[self.tQK[qt // 4]], swrites=[tiqt[qb]])
            for h in range(8):
                p.op("pool", lambda e, h=h, qb=qb: e.tensor_scalar(out=dg[qb][:, h, :], in0=self.ident_bf[:],
                                                                   scalar1=wq[qb][:, h:h + 1], scalar2=1.0,
                                                                   op0=ALU.mult, op1=ALU.mult),
                     reads=[twq[qb], self.t_identb], swrites=[tdg[qb]])
            subs = [(kc, h) for kc in range(nkc) for h in range(8)]
            x0 = ctr[0]
            ctr[0] += len(subs)

            def logits(n):
                kc, h = subs[n]
                x = (x0 + n) % NL1
                N = min(512, L - kc * 512)
                ks = slice(kc * 512, kc * 512 + N)
                g, r = h // 3, (h % 3) * 32
                p.op("pe", lambda e, x=x, g=g, r=r, ks=ks, N=N, qb=qb: e.matmul(
                    pl1[x][:, 0:N], lhsT=iqt[qb][r:r + 32, g, :], rhs=ik4[r:r + 32, ks],
                    start=True, stop=True), reads=[tres, tiqt[qb]], writes=[tpl1[x]])

            logits(0)
            if len(subs) > 1:
                logits(1)
            for n, (kc, h) in enumerate(subs):
                x = (x0 + n) % NL1
                xt = (x0 + n) % 2
                N = min(512, L - kc * 512)
                ks = slice(kc * 512, kc * 512 + N)
                p.op("act", lambda e, x=x, xt=xt, N=N: e.activation(out=T[xt][:, 0, 0:N], in_=pl1[x][:, 0:N],
                                                                   func=AF.Relu),
                     reads=[tpl1[x]], writes=[tT[xt]])
                if n + 2 < len(subs):
                    logits(n + 2)
                p.op("pe", lambda e, xt=xt, h=h, N=N, qb=qb: e.matmul(
                    psc[:, 0:N], lhsT=dg[qb][:, h, :], rhs=T[xt][:, 0, 0:N], start=(h == 0), stop=(h == 7)),
                    reads=[tdg[qb], tT[xt]], writes=[tpsc])
                if h == 7:
                    p.op("act", lambda e, ks=ks, N=N, qb=qb: e.copy(out=SC[qb][:, ks], in_=psc[:, 0:N]), reads=[tpsc],
                         swrites=[tSC[qb]])
                if h % 2 == 1:
                    yield

        def thresh(qt):
            qb = qt % 2
            L = 128 * (qt + 1)
            sc, mk, s_, ts_ = SC[qb], MK[qb], st[qb], tst[qb]
            w2 = W2[qb]
            if L > TOPK:
                p.op("dve", lambda e: e.tensor_reduce(out=s_[:, 1:2], in_=sc[:, 0:L], axis=AX.X, op=ALU.max),
                     reads=[tSC[qb]], writes=[ts_])
                p.op("dve", lambda e: e.tensor_reduce(out=s_[:, 0:1], in_=sc[:, 0:TOPK], axis=AX.X, op=ALU.min),
                     reads=[tSC[qb]], writes=[ts_])
                p.op("dve", lambda e: e.tensor_tensor(out=s_[:, 2:3], in0=s_[:, 1:2], in1=s_[:, 0:1],
                                                      op=ALU.subtract), reads=[ts_], writes=[ts_])
                p.op("dve", lambda e: e.tensor_scalar(out=w2[:], in0=ctab[:], scalar1=s_[:, 2:3], scalar2=None,
                                                      op0=ALU.mult), reads=[ts_, tctab], writes=[tW2[qb]])
                p.op("dve", lambda e: e.tensor_tensor(out=s_[:, 3:4], in0=s_[:, 0:1], in1=w2[:, 0:1], op=ALU.add),
                     reads=[ts_, tW2[qb]], writes=[ts_])
            else:
                p.op("dve", lambda e: e.memset(s_[:, 0:1], -1.0e29), writes=[ts_])
            p.op("pool", lambda e: e.affine_select(
                out=sc[:, qt * 128:(qt + 1) * 128], in_=sc[:, qt * 128:(qt + 1) * 128], pattern=[[-1, 128]],
                compare_op=ALU.is_ge, fill=self.freg(e, NEG), base=0, channel_multiplier=1),
                reads=[tSC[qb]], writes=[tSC[qb]])
            if L > TOPK:
                for it in range(NBIS):
                    p.op("dve", lambda e: e.tensor_scalar(out=mk[:, 0:L], in0=sc[:, 0:L], scalar1=s_[:, 3:4],
                                                          scalar2=None, op0=ALU.is_ge, op1=ALU.add,
                                                          accum_out=s_[:, 4:5]),
                         reads=[ts_, tSC[qb]], writes=[ts_], swrites=[tMK[qb]])
                    p.op("dve", lambda e: e.tensor_scalar(out=s_[:, 5:6], in0=s_[:, 4:5], scalar1=TOPK - 0.5,
                                                          scalar2=-0.5, op0=ALU.is_ge, op1=ALU.add),
                         reads=[ts_], writes=[ts_])
                    p.op("dve", lambda e, it=it: e.scalar_tensor_tensor(out=s_[:, 3:4], in0=s_[:, 5:6],
                                                                       scalar=w2[:, it:it + 1], in1=s_[:, 3:4],
                                                                       op0=ALU.mult, op1=ALU.add),
                         reads=[ts_, tW2[qb]], writes=[ts_])
                    yield
                p.op("dve", lambda e: e.scalar_tensor_tensor(out=s_[:, 0:1], in0=w2[:, NBIS - 1:NBIS], scalar=-0.5,
                                                             in1=s_[:, 3:4], op0=ALU.mult, op1=ALU.add),
                     reads=[ts_, tW2[qb]], writes=[ts_])
            p.op("dve", lambda e: e.tensor_scalar(out=mk[:, 0:L], in0=sc[:, 0:L], scalar1=s_[:, 0:1],
                                                  scalar2=None, op0=ALU.is_ge),
                 reads=[ts_, tSC[qb]], writes=[tMK[qb]])

        def part2(qt):
            qb = qt % 2
            qs = slice(qt * 128, (qt + 1) * 128)
            nkt = qt + 1
            p.dma("sp", dq[qb][:], self.DQ[:, qs].rearrange("(h d) q -> d h q", d=128),
                  reads=[self.tQK[qt // 4]], writes=[tdq[qb]])
            for gi, g0 in enumerate(range(0, nkt, 8)):
                n8 = min(8, nkt - g0)
                zz = gi % 2
                pmv = pl[zz][:, 0:512].bitcast(BF16)
                for t in range(n8):
                    kt = g0 + t
                    p.op("pe", lambda e, kt=kt, t=t, qb=qb, pmv=pmv: e.transpose(
                        out=pmv[:, t * 128:(t + 1) * 128], in_=MK[qb][:, kt * 128:(kt + 1) * 128],
                        identity=self.ident_bf[:]), reads=[tMK[qb], self.t_identb], swrites=[tpl[zz]])
                p.op("act", lambda e, g0=g0, n8=n8, pmv=pmv: e.activation(
                    out=mbT[:, g0 * 128:(g0 + n8) * 128], in_=pmv[:, 0:n8 * 128], func=AF.Identity,
                    scale=30000.0, bias=-30000.0), reads=[tpl[zz]], swrites=[tmbT])
                yield
            z0 = ctr[1]
            ctr[1] += nkt

            def front(kt):
                z = (z0 + kt) % 2
                kts = slice(kt * 128, (kt + 1) * 128)
                p.op("pe", lambda e, z=z, kts=kts, qb=qb: e.matmul(
                    pl[z][:, 0:512], lhsT=dkT[:, kts], rhs=dq[qb][:].rearrange("p h q -> p (h q)"), start=True,
                    stop=False), reads=[tres, tdq[qb]], writes=[tpl[z]])
                p.op("pe", lambda e, z=z, kts=kts: e.matmul(
                    pl[z][:, 0:512].rearrange("p (h q) -> p h q", h=4), lhsT=self.ident_bf[:],
                    rhs=mbT[:, kts].rearrange("p (o q) -> p o q", o=1).to_broadcast([128, 4, 128]),
                    start=False, stop=True), reads=[tmbT, self.t_identb], writes=[tpl[z]])

            front(0)
            for kt in range(nkt):
                z = (z0 + kt) % 2
                p.op("act", lambda e, z=z: e.activation(out=P[z][:].rearrange("p h q -> p (h q)"), in_=pl[z][:, 0:512],
                                                        func=AF.Exp, scale=scale),
                     reads=[tpl[z]], writes=[tP[z]])
                if kt + 1 < nkt:
                    front(kt + 1)
                p.op("pe", lambda e, z=z, kt=kt, qt=qt: e.matmul(
                    pO[:], lhsT=dvT[:, kt, :], rhs=P[z][:].rearrange("p h q -> p (h q)"), start=(kt == 0),
                    stop=(kt == qt)), reads=[tres, tP[z]], writes=[tpO])
                p.op("pe", lambda e, z=z, kt=kt, qt=qt: e.matmul(
                    pL[:], lhsT=self.ones_bf[:], rhs=P[z][:].rearrange("p h q -> p (h q)"), start=(kt == 0),
                    stop=(kt == qt)), reads=[self.t_ones, tP[z]], writes=[tpL])
                yield
            p.op("dve", lambda e: e.reciprocal(out=rl[:], in_=pL[:]), reads=[tpL], writes=[trl])
            p.op("dve", lambda e, qb=qb: e.tensor_tensor(out=od[qb][:].rearrange("p h q -> p (h q)"), in0=pO[:],
                                                         in1=rl[:], op=ALU.mult),
                 reads=[tpO, trl], writes=[tod[qb]])
            p.dma("sp", self.OBR[3, :, qs].rearrange("(h d) q -> d h q", d=128), od[qb][:], reads=[tod[qb]],
                  swrites=[self.tOB[3][qt // 4]], pool="st")

        def merge(gens):
            prog = [0] * len(gens)
            alive = [True] * len(gens)
            while any(alive):
                i = min((k for k in range(len(gens)) if alive[k]), key=lambda k: prog[k] / gens[k][1])
                try:
                    next(gens[i][0])
                    prog[i] += 1
                except StopIteration:
                    alive[i] = False

        def ensure_gen(g):
            return g

        for stage in range(NQT + 2):
            gens = []
            t1, t2, t3 = stage, stage - 1, stage - 2
            if 0 <= t1 < NQT:
                gens.append((part1(t1), ((128 * (t1 + 1) + 511) // 512) * 4 + 1))
            if 0 <= t2 < NQT:
                gens.append((thresh(t2), NBIS + 1))
            if 0 <= t3 < NQT:
                gens.append((part2(t3), t3 + 2 + (t3 + 8) // 8))
            merge(gens)
    p.barrier()


B.phaseD = phaseD


def phase3(self, l):
    p, nc, S = self.p, self.nc, self.S
    hin = self.xT if l == 0 else self.H
    with ExitStack() as es:
        Wg = self.sb(es, "Wg", [128, 32, D], BF16)
        tWg = p.toks(32, "Wg")
        Wb = self.sb(es, "Wb", [128, 16, D], BF16)
        tWb = p.toks(16, "Wb")
        Wo = self.sb(es, "Wo", [128, 8, D], BF16)
        tWo = p.toks(8, "Wo")
        with ExitStack() as es2:
            stg = [self.sb(es2, "p3stg", [128, D], F32) for _ in range(2)]
            tstg = p.toks(2, "p3stg")
            for j in range(4):
                self.load_weight(Wg[:, j * 8:(j + 1) * 8, :], tWg[j * 8:(j + 1) * 8], self.w_gate[l, j], 8, D, stg,
                                 tstg, gain=lambda kc: self.vcol(l, "gmix", kc))
                self.load_weight(Wb[:, j * 4:(j + 1) * 4, :], tWb[j * 4:(j + 1) * 4], self.w_branch[l, j], 4, D, stg,
                                 tstg)
            self.load_weight(Wo, tWo, self.w_out[l], 8, D, stg, tstg)
            p.barrier()
        xn = [self.sb(es, "xn3", [128, 8, 512], BF16) for _ in range(2)]
        txn = p.toks(2, "xn3")
        obr = [self.sb(es, "obr3", [128, 16, 512], BF16) for _ in range(2)]
        tobr = p.toks(2, "obr3")
        hb = self.sb(es, "hb3", [128, 8, 512], F32)
        thb = p.tok("hb3")
        mixed = self.sb(es, "mixed", [128, 8, 512], BF16)
        tmixed = p.toks(8, "mixed")
        sg = [self.sb(es, "sg3", [128, 512], F32) for _ in range(2)]
        tsg = p.toks(2, "sg3")
        pr = [self.sb(es, "pr3", [128, 512], F32) for _ in range(2)]
        tpr = p.toks(2, "pr3")
        mix = self.sb(es, "mix3", [128, 512], F32)
        tmix = p.tok("mix3")
        pg = [self.ps(es, "pg", [128, 512]) for _ in range(2)]
        tpg = p.toks(2, "pg")
        pp = [self.ps(es, "pp", [128, 512]) for _ in range(2)]
        tpp = p.toks(2, "pp")
        po = [self.ps(es, "po", [128, 512]) for _ in range(2)]
        tpo = p.toks(2, "po")

        def loads(c):
            b = c % 2
            sl = slice(c * 512, (c + 1) * 512)
            p.dma("sp", xn[b][:], self.XN[:, sl].rearrange("(k p) t -> p k t", p=128), reads=[self.tXN[c]],
                  writes=[txn[b]])
            for j in range(4):
                p.dma("sp", obr[b][:, j * 4:(j + 1) * 4, :],
                      self.OBR[j, :, sl].rearrange("(k p) t -> p k t", p=128), reads=[self.tOB[j][c]],
                      swrites=[tobr[b]])

        loads(0)
        cnt = 0
        for c in range(self.NCH):
            b = c % 2
            sl = slice(c * 512, (c + 1) * 512)
            if c + 1 < self.NCH:
                loads(c + 1)
            rd = [self.tH[c]] if l > 0 else []
            p.dma("sp", hb[:], hin[:, sl].rearrange("(k p) t -> p k t", p=128), reads=rd, writes=[thb])
            for m in range(8):
                ms = slice(m * 128, (m + 1) * 128)
                for j in range(4):
                    i = cnt % 2
                    cnt += 1
                    for kc in range(8):
                        p.op("pe", lambda e, i=i, j=j, kc=kc, ms=ms, b=b: e.matmul(
                            pg[i][:], lhsT=Wg[:, j * 8 + kc, ms], rhs=xn[b][:, kc, :], start=(kc == 0), stop=(kc == 7)),
                            reads=[tWg[j * 8 + kc], txn[b]], writes=[tpg[i]])
                    for kc in range(4):
                        p.op("pe", lambda e, i=i, j=j, kc=kc, ms=ms, b=b: e.matmul(
                            pp[i][:], lhsT=Wb[:, j * 4 + kc, ms], rhs=obr[b][:, j * 4 + kc, :], start=(kc == 0),
                            stop=(kc == 3)), reads=[tWb[j * 4 + kc], tobr[b]], writes=[tpp[i]])
                    p.op("act", lambda e, i=i: e.activation(out=sg[i][:], in_=pg[i][:], func=AF.Sigmoid),
                         reads=[tpg[i]], writes=[tsg[i]])
                    if j == 0:
                        p.op("dve", lambda e, i=i: e.tensor_tensor(out=mix[:], in0=pp[i][:], in1=sg[i][:], op=ALU.mult),
                             reads=[tpp[i], tsg[i]], writes=[tmix])
                    else:
                        p.op("dve", lambda e, i=i: e.tensor_tensor(out=pr[i][:], in0=pp[i][:], in1=sg[i][:],
                                                                   op=ALU.mult),
                             reads=[tpp[i], tsg[i]], writes=[tpr[i]])
                        if j < 3:
                            p.op("pool", lambda e, i=i: e.tensor_tensor(out=mix[:], in0=mix[:], in1=pr[i][:],
                                                                        op=ALU.add),
                                 reads=[tmix, tpr[i]], writes=[tmix])
                        else:
                            p.op("pool", lambda e, i=i, m=m: e.tensor_tensor(out=mixed[:, m, :], in0=mix[:],
                                                                             in1=pr[i][:], op=ALU.add),
                                 reads=[tmix, tpr[i]], writes=[tmixed[m]])
            for m2 in range(8):
                i = m2 % 2
                ms = slice(m2 * 128, (m2 + 1) * 128)
                for m in range(8):
                    p.op("pe", lambda e, i=i, m=m, ms=ms: e.matmul(po[i][:], lhsT=Wo[:, m, ms], rhs=mixed[:, m, :],
                                                                  start=(m == 0), stop=(m == 7)),
                         reads=[tWo[m], tmixed[m]], writes=[tpo[i]])
                p.op("dve", lambda e, i=i, m2=m2: e.tensor_tensor(out=hb[:, m2, :], in0=hb[:, m2, :], in1=po[i][:],
                                                                  op=ALU.add),
                     reads=[tpo[i]], swrites=[thb])
            p.dma("sp", self.H[:, sl].rearrange("(k p) t -> p k t", p=128), hb[:], reads=[thb],
                  writes=[self.tH[c]], pool="st")
    p.barrier()


B.phase3 = phase3


def phase4(self, l):
    p, nc, S = self.p, self.nc, self.S
    NF = 44
    with ExitStack() as es:
        Wu = self.sb(es, "Wu", [128, 8, 2 * DFF], BF16)
        tWu = p.toks(8, "Wu")
        Wd = self.sb(es, "Wd", [128, 22, D], BF16)
        tWd = p.toks(22, "Wd")
        with ExitStack() as es2:
            stg = [self.sb(es2, "p4stg", [128, 2 * DFF], F32) for _ in range(2)]
            tstg = p.toks(2, "p4stg")
            self.load_weight(Wu, tWu, self.w_up[l], 8, 2 * DFF, stg, tstg, gain=lambda kc: self.vcol(l, "gffn", kc))
            self.load_weight(Wd, tWd, self.w_down[l], 22, D, stg, tstg)
            p.barrier()
        hb = self.sb(es, "hb4", [128, 8, 512], F32)
        thb = p.tok("hb4")
        sq = [self.sb(es, "sq4", [128, 512], F32) for _ in range(2)]
        tsq = p.toks(2, "sq4")
        rstd = self.sb(es, "rstd4", [128, 512], F32)
        trstd = p.tok("rstd4")
        xf = self.sb(es, "xf", [128, 8, 512], BF16)
        txf = p.tok("xf")
        F = [self.sb(es, "F4", [128, 514], F32) for _ in range(2)]
        tF = p.toks(2, "F4")
        y = [self.sb(es, "y4", [128, 512], F32) for _ in range(4)]
        ty = p.toks(4, "y4")
        A = self.sb(es, "A4", [128, 22, 512], BF16)
        tA = p.toks(22, "A4")
        carry = [self.sb(es, "carry", [128, NF, 2], F32) for _ in range(2)]
        tcar = [p.toks(NF, "carry") for _ in range(2)]
        pst = self.ps(es, "pst4", [128, 512])
        tpst = p.tok("pst4")
        pu = [self.ps(es, "pu", [128, 512]) for _ in range(3)]
        tpu = p.toks(3, "pu")
        pd = [self.ps(es, "pd", [128, 512]) for _ in range(2)]
        tpd = p.toks(2, "pd")
        p.op("pool", lambda e: e.memset(carry[0][:], 0.0), writes=tcar[0])
        cnt = [0, 0]
        for c in range(self.NCH):
            sl = slice(c * 512, (c + 1) * 512)
            p.dma("sp", hb[:], self.H[:, sl].rearrange("(k p) t -> p k t", p=128), reads=[self.tH[c]], writes=[thb])
            for kc in range(8):
                i = kc % 2
                p.op("act", lambda e, i=i, kc=kc: e.activation(out=sq[i][:], in_=hb[:, kc, :], func=AF.Square),
                     reads=[thb], writes=[tsq[i]])
                p.op("pe", lambda e, i=i, kc=kc: e.matmul(pst[:], lhsT=self.cm[:, 1, :], rhs=sq[i][:], start=(kc == 0),
                                                         stop=(kc == 7)),
                     reads=[tsq[i], self.t_cm], writes=[tpst])
            p.op("act", lambda e: e.activation(out=rstd[:], in_=pst[:], func=AF.Sqrt, bias=self.epsc[:]),
                 reads=[tpst, self.t_eps], writes=[trstd])
            p.op("dve", lambda e: e.reciprocal(out=rstd[:], in_=rstd[:]), reads=[trstd], writes=[trstd])
            for kc in range(8):
                eng = "dve" if kc % 2 == 0 else "pool"
                p.op(eng, lambda e, kc=kc: e.tensor_tensor(out=xf[:, kc, :], in0=hb[:, kc, :], in1=rstd[:], op=ALU.mult),
                     reads=[thb, trstd], swrites=[txf])

            def up_tile(ft, yb):
                i = cnt[0] % 3
                cnt[0] += 1
                fb = cnt[1] % 2
                cnt[1] += 1
                cs = slice(ft * 128, (ft + 1) * 128)
                for kc in range(8):
                    p.op("pe", lambda e, i=i, kc=kc, cs=cs: e.matmul(pu[i][:], lhsT=Wu[:, kc, cs], rhs=xf[:, kc, :],
                                                                    start=(kc == 0), stop=(kc == 7)),
                         reads=[tWu[kc], txf], writes=[tpu[i]])
                w = lambda k: self.vcol(l, "cfw", k * NF + ft)
                cin, cout = carry[c % 2], carry[(c + 1) % 2]
                tcin, tcout = tcar[c % 2], tcar[(c + 1) % 2]
                p.op("act", lambda e, fb=fb, ft=ft, cin=cin: e.copy(out=F[fb][:, 0:2], in_=cin[:, ft, :]),
                     reads=[tcin[ft]], writes=[tF[fb]])
                p.op("act", lambda e, fb=fb, i=i: e.copy(out=F[fb][:, 2:514], in_=pu[i][:]), reads=[tpu[i]],
                     swrites=[tF[fb]])
                p.op("act", lambda e, i=i, ft=ft, cout=cout: e.copy(out=cout[:, ft, :], in_=pu[i][:, 510:512]),
                     reads=[tpu[i]], writes=[tcout[ft]])
                p.op("act", lambda e, i=i, yb=yb, w2=w(2), bb=self.vcol(l, "cfb", ft): e.activation(
                    out=y[yb][:], in_=pu[i][:], func=AF.Identity, scale=w2, bias=bb),
                    reads=[tpu[i], self.t_vec], writes=[ty[yb]])
                for k in (1, 0):
                    p.op("dve", lambda e, fb=fb, yb=yb, k=k, wk=w(k): e.scalar_tensor_tensor(
                        out=y[yb][:], in0=F[fb][:, k:k + 512], scalar=wk, in1=y[yb][:], op0=ALU.mult, op1=ALU.add),
                        reads=[tF[fb], ty[yb], self.t_vec], writes=[ty[yb]])

            for j in range(22):
                yg = (2 * j) % 4
                yv = (2 * j + 1) % 4
                up_tile(j, yg)
                up_tile(j + 22, yv)
                p.op("act", lambda e, yg=yg: e.activation(out=y[yg][:], in_=y[yg][:], func=AF.Silu),
                     reads=[ty[yg]], writes=[ty[yg]])
                p.op("pool", lambda e, yg=yg, yv=yv, j=j: e.tensor_tensor(out=A[:, j, :], in0=y[yg][:], in1=y[yv][:],
                                                                         op=ALU.mult),
                     reads=[ty[yg], ty[yv]], writes=[tA[j]])
            for m2 in range(8):
                i = m2 % 2
                ms = slice(m2 * 128, (m2 + 1) * 128)
                for j in range(22):
                    p.op("pe", lambda e, i=i, j=j, ms=ms: e.matmul(pd[i][:], lhsT=Wd[:, j, ms], rhs=A[:, j, :],
                                                                  start=(j == 0), stop=(j == 21)),
                         reads=[tWd[j], tA[j]], writes=[tpd[i]])
                p.op("dve", lambda e, i=i, m2=m2: e.tensor_tensor(out=hb[:, m2, :], in0=hb[:, m2, :], in1=pd[i][:],
                                                                  op=ALU.add),
                     reads=[tpd[i]], swrites=[thb])
            p.dma("sp", self.H[:, sl].rearrange("(k p) t -> p k t", p=128), hb[:], reads=[thb],
                  writes=[self.tH[c]], pool="st")
    p.barrier()
    self.phase5(l)


B.phase4 = phase4


def phase5(self, l):
    p, nc, S = self.p, self.nc, self.S
    last = (l == self.L - 1)
    dst = self.outT if last else self.H
    with ExitStack() as es:
        Wpg = self.sb(es, "Wpg", [128, 8, D], BF16)
        tWpg = p.toks(8, "Wpg")
        Wpi = self.sb(es, "Wpi", [128, 2, D], BF16)
        tWpi = p.toks(2, "Wpi")
        stg = [self.sb(es, "p5stg", [128, D], F32) for _ in range(2)]
        tstg = p.toks(2, "p5stg")
        self.load_weight(Wpg, tWpg, self.w_pg[l], 8, D, stg, tstg, gain=lambda kc: self.vcol(l, "gpg", kc))
        self.load_weight(Wpi, tWpi, self.w_pi[l], 2, D, stg, tstg)
        hb = [self.sb(es, "hb5", [128, 8, 512], F32) for _ in range(2)]
        thb = p.toks(2, "hb5")
        pb = [self.sb(es, "pb5", [128, 2, 512], F32) for _ in range(2)]
        tpb = p.toks(2, "pb5")
        pbb = self.sb(es, "pbb5", [128, 2, 512], BF16)
        tpbb = p.tok("pbb5")
        sq = [self.sb(es, "sq5", [128, 512], F32) for _ in range(2)]
        tsq = p.toks(2, "sq5")
        rstd = self.sb(es, "rstd5", [128, 512], F32)
        trstd = p.tok("rstd5")
        rse = self.sb(es, "rse5", [128, 512], F32)
        trse = p.tok("rse5")
        xg = self.sb(es, "xg5", [128, 8, 512], BF16)
        txg = p.tok("xg5")
        ee = self.sb(es, "ee5", [128, 8, 512], F32)
        tee = p.toks(8, "ee5")
        sg = [self.sb(es, "sg5", [128, 512], F32) for _ in range(2)]
        tsg = p.toks(2, "sg5")
        t1 = [self.sb(es, "t15", [128, 512], F32) for _ in range(2)]
        tt1 = p.toks(2, "t15")
        pst = self.ps(es, "pst5", [128, 512])
        tpst = p.tok("pst5")
        pse = self.ps(es, "pse5", [128, 512])
        tpse = p.tok("pse5")
        pe_ = [self.ps(es, "pe5", [128, 512]) for _ in range(2)]
        tpe = p.toks(2, "pe5")
        pg = [self.ps(es, "pg5", [128, 512]) for _ in range(2)]
        tpg = p.toks(2, "pg5")

        def loads(c):
            b = c % 2
            sl = slice(c * 512, (c + 1) * 512)
            p.dma("sp", hb[b][:], self.H[:, sl].rearrange("(k p) t -> p k t", p=128), reads=[self.tH[c]],
                  writes=[thb[b]])
            p.dma("sp", pb[b][:], self.pT[l, :, sl].rearrange("(k p) t -> p k t", p=128), writes=[tpb[b]])

        loads(0)
        for c in range(self.NCH):
            b = c % 2
            sl = slice(c * 512, (c + 1) * 512)
            if c + 1 < self.NCH:
                loads(c + 1)
            hbb = hb[b]
            for kc in range(8):
                i = kc % 2
                p.op("act", lambda e, i=i, kc=kc, hbb=hbb: e.activation(out=sq[i][:], in_=hbb[:, kc, :], func=AF.Square),
                     reads=[thb[b]], writes=[tsq[i]])
                p.op("pe", lambda e, i=i, kc=kc: e.matmul(pst[:], lhsT=self.cm[:, 1, :], rhs=sq[i][:], start=(kc == 0),
                                                         stop=(kc == 7)),
                     reads=[tsq[i], self.t_cm], writes=[tpst])
            p.op("act", lambda e: e.activation(out=rstd[:], in_=pst[:], func=AF.Sqrt, bias=self.epsc[:]),
                 reads=[tpst, self.t_eps], writes=[trstd])
            p.op("dve", lambda e: e.reciprocal(out=rstd[:], in_=rstd[:]), reads=[trstd], writes=[trstd])
            for kc in range(8):
                eng = "dve" if kc % 2 == 0 else "pool"
                p.op(eng, lambda e, kc=kc, hbb=hbb: e.tensor_tensor(out=xg[:, kc, :], in0=hbb[:, kc, :], in1=rstd[:],
                                                                   op=ALU.mult),
                     reads=[thb[b], trstd], swrites=[txg])
            p.op("pool", lambda e, b=b: e.tensor_copy(out=pbb[:], in_=pb[b][:]), reads=[tpb[b]], writes=[tpbb])
            for m in range(8):
                i = m % 2
                ms = slice(m * 128, (m + 1) * 128)
                for kc in range(2):
                    p.op("pe", lambda e, i=i, kc=kc, ms=ms: e.matmul(pe_[i][:], lhsT=Wpi[:, kc, ms], rhs=pbb[:, kc, :],
                                                                    start=(kc == 0), stop=(kc == 1)),
                         reads=[tWpi[kc], tpbb], writes=[tpe[i]])
                p.op("act", lambda e, i=i, m=m: e.copy(out=ee[:, m, :], in_=pe_[i][:]), reads=[tpe[i]], writes=[tee[m]])
                p.op("pool", lambda e, i=i, m=m: e.tensor_tensor(out=sq[i][:], in0=ee[:, m, :], in1=ee[:, m, :],
                                                                 op=ALU.mult),
                     reads=[tee[m]], writes=[tsq[i]])
                p.op("pe", lambda e, i=i, m=m: e.matmul(pse[:], lhsT=self.cm[:, 1, :], rhs=sq[i][:], start=(m == 0),
                                                       stop=(m == 7)),
                     reads=[tsq[i], self.t_cm], writes=[tpse])
            p.op("act", lambda e: e.activation(out=rse[:], in_=pse[:], func=AF.Sqrt, bias=self.epsc[:]),
                 reads=[tpse, self.t_eps], writes=[trse])
            p.op("dve", lambda e: e.reciprocal(out=rse[:], in_=rse[:]), reads=[trse], writes=[trse])
            for m in range(8):
                i = m % 2
                ms = slice(m * 128, (m + 1) * 128)
                for kc in range(8):
                    p.op("pe", lambda e, i=i, kc=kc, ms=ms: e.matmul(pg[i][:], lhsT=Wpg[:, kc, ms], rhs=xg[:, kc, :],
                                                                    start=(kc == 0), stop=(kc == 7)),
                         reads=[tWpg[kc], txg], writes=[tpg[i]])
                p.op("act", lambda e, i=i: e.activation(out=sg[i][:], in_=pg[i][:], func=AF.Sigmoid),
                     reads=[tpg[i]], writes=[tsg[i]])
                p.op("dve", lambda e, i=i, m=m, g=self.vcol(l, "gple", m): e.scalar_tensor_tensor(
                    out=t1[i][:], in0=ee[:, m, :], scalar=g, in1=rse[:], op0=ALU.mult, op1=ALU.mult),
                    reads=[tee[m], trse, self.t_vec], writes=[tt1[i]])
                p.op("pool", lambda e, i=i: e.tensor_tensor(out=t1[i][:], in0=t1[i][:], in1=sg[i][:], op=ALU.mult),
                     reads=[tt1[i], tsg[i]], writes=[tt1[i]])
                p.op("dve", lambda e, i=i, m=m, hbb=hbb: e.tensor_tensor(out=hbb[:, m, :], in0=hbb[:, m, :],
                                                                        in1=t1[i][:], op=ALU.add),
                     reads=[tt1[i], txg], swrites=[thb[b]])
            p.dma("sp", dst[:, sl].rearrange("(k p) t -> p k t", p=128), hbb[:], reads=[thb[b]],
                  writes=[self.tH[c]], pool="st")
    p.barrier()


B.phase5 = phase5
```
